# Optimizing a Trainium2 kernel written in Bass

```python
import math
import jax, jax.numpy as jnp
from jax import lax
import numpy as np

D_MODEL = 1024
BATCH = 4
SEQ = 8192
DEPTH = 1

CHUNK = 64
N_META = 16
Q_BLOCK = 128
EPS = 1e-6
D_RNN = 1280
RNN_BLOCKS = 10
RNN_BLOCK_DIM = D_RNN // RNN_BLOCKS
CONV_WIDTH = 4
LRU_C = 8.0
N_HEADS = 8
QK_NOPE = 128
QK_ROPE = 64
V_DIM = 128
Q_RANK = 384
KV_RANK = 256
ROPE_THETA = 10000.0
ATTN_SCALE = 1.0 / math.sqrt(QK_NOPE + QK_ROPE)
N_BRANCH = 2
D_FF = ((8 * D_MODEL // 3 + 255) // 256) * 256
IN_SPLITS = (D_RNN, D_RNN, Q_RANK, KV_RANK, QK_ROPE, N_BRANCH * D_MODEL)
D_IN = sum(IN_SPLITS)
D_BRANCH_IN = D_RNN + N_HEADS * V_DIM
PAD_CHUNK = 2 ** 30
NEG = -1e30

kernel_name = "hybrid_rglru_mla_gated_block"


def rmsnorm(x, g):
    xf = x.astype(jnp.float32)
    y = xf * lax.rsqrt(jnp.mean(xf * xf, axis=-1, keepdims=True) + EPS)
    return (y * g.astype(jnp.float32)).astype(x.dtype)


def apply_rope(x, cos, sin):
    x1, x2 = jnp.split(x.astype(jnp.float32), 2, axis=-1)
    return jnp.concatenate([x1 * cos - x2 * sin, x2 * cos + x1 * sin], axis=-1).astype(x.dtype)


def rglru_branch(u_x, u_gate, conv_w, conv_b, w_a, b_a, w_i, b_i, lam):
    B, L, _ = u_x.shape
    xp = jnp.pad(u_x, ((0, 0), (CONV_WIDTH - 1, 0), (0, 0)))
    xc = conv_b + xp[:, 0:L] * conv_w[0]
    for k in range(1, CONV_WIDTH):
        xc = xc + xp[:, k:k + L] * conv_w[k]
    xb = xc.reshape(B, L, RNN_BLOCKS, RNN_BLOCK_DIM)
    r = jax.nn.sigmoid(jnp.einsum('blhi,hij->blhj', xb, w_a).reshape(B, L, D_RNN) + b_a)
    i = jax.nn.sigmoid(jnp.einsum('blhi,hij->blhj', xb, w_i).reshape(B, L, D_RNN) + b_i)
    log_a = LRU_C * r.astype(jnp.float32) * jax.nn.log_sigmoid(lam.astype(jnp.float32))
    a = jnp.exp(log_a)
    b = jnp.sqrt(-jnp.expm1(2.0 * log_a)) * (i * xc).astype(jnp.float32)

    def combine(left, right):
        a_l, b_l = left
        a_r, b_r = right
        return a_l * a_r, a_r * b_l + b_r

    _, h = lax.associative_scan(combine, (a, b), axis=1)
    return h.astype(u_x.dtype) * jax.nn.gelu(u_gate)


def mla_branch(u_q, u_kv, u_kr, q_norm_g, w_uq, kv_norm_g, w_ukv, cos, sin, chunk_id):
    B, L, _ = u_q.shape
    nb = L // Q_BLOCK
    q = (rmsnorm(u_q, q_norm_g) @ w_uq).reshape(B, L, N_HEADS, QK_NOPE + QK_ROPE)
    q_nope, q_rope = q[..., :QK_NOPE], q[..., QK_NOPE:]
    q_rope = apply_rope(q_rope, cos[:, None, :], sin[:, None, :])
    kv = (rmsnorm(u_kv, kv_norm_g) @ w_ukv).reshape(B, L, N_HEADS, QK_NOPE + V_DIM)
    k_nope, v = kv[..., :QK_NOPE], kv[..., QK_NOPE:]
    k_rope = apply_rope(u_kr, cos, sin)

    def to_blocks(t):
        t = t.reshape((B, nb, Q_BLOCK) + t.shape[2:])
        return jnp.moveaxis(t, 1, 0)

    def attend(args):
        qn, qr, qc = args
        s = (jnp.einsum('bqhd,bkhd->bhqk', qn, k_nope)
             + jnp.einsum('bqhr,bkr->bhqk', qr, k_rope)).astype(jnp.float32) * ATTN_SCALE
        mask = chunk_id[None, :] <= qc[:, None]
        s = jnp.where(mask[None, None], s, NEG)
        p = jax.nn.softmax(s, axis=-1).astype(v.dtype)
        return jnp.einsum('bhqk,bkhd->bqhd', p, v)

    o = lax.map(attend, (to_blocks(q_nope), to_blocks(q_rope), chunk_id.reshape(nb, Q_BLOCK)))
    return jnp.moveaxis(o, 0, 1).reshape(B, L, N_HEADS * V_DIM)


def hybrid_layer(h, cos, sin, chunk_id, norm_mix_g, w_in, b_gate, conv_w, conv_b, w_rec_a, b_rec_a,
                 w_rec_i, b_rec_i, lru_lambda, q_norm_g, w_uq, kv_norm_g, w_ukv, w_branch, w_out,
                 norm_ffn_g, w_ffn_in, w_ffn_out):
    B, L, D = h.shape
    z = rmsnorm(h, norm_mix_g)
    u = z @ w_in
    u_x, u_g, u_q, u_kv, u_kr, u_m = jnp.split(u, np.cumsum(IN_SPLITS)[:-1].tolist(), axis=-1)
    y_rnn = rglru_branch(u_x, u_g, conv_w, conv_b, w_rec_a, b_rec_a, w_rec_i, b_rec_i, lru_lambda)
    y_att = mla_branch(u_q, u_kv, u_kr, q_norm_g, w_uq, kv_norm_g, w_ukv, cos, sin, chunk_id)
    p_rnn = y_rnn @ w_branch[:D_RNN]
    p_att = y_att @ w_branch[D_RNN:]
    gates = jax.nn.sigmoid(u_m + b_gate.reshape(-1)).reshape(B, L, N_BRANCH, D)
    mixed = gates[:, :, 0] * p_rnn + gates[:, :, 1] * p_att
    h = h + mixed @ w_out
    zf = rmsnorm(h, norm_ffn_g)
    gate, up = jnp.split(zf @ w_ffn_in, 2, axis=-1)
    return h + (jax.nn.silu(gate) * up) @ w_ffn_out


def setup_inputs(seed: int = 0) -> dict:
    key = jax.random.key(seed)
    ks = jax.random.split(key, 24)
    f32 = jnp.float32

    def nrm(k, shape, scale):
        return jax.random.normal(k, shape, f32) * scale

    a0 = jax.random.uniform(ks[11], (DEPTH, D_RNN), f32, 0.9, 0.999)
    return {
        "x": nrm(ks[0], (BATCH, SEQ, D_MODEL), 1.0),
        "meta_tokens": nrm(ks[1], (N_META, D_MODEL), 1.0),
        "norm_mix_g": 1.0 + nrm(ks[2], (DEPTH, D_MODEL), 0.02),
        "w_in": nrm(ks[3], (DEPTH, D_MODEL, D_IN), D_MODEL ** -0.5),
        "b_gate": nrm(ks[4], (DEPTH, N_BRANCH, D_MODEL), 0.02),
        "conv_w": nrm(ks[5], (DEPTH, CONV_WIDTH, D_RNN), CONV_WIDTH ** -0.5),
        "conv_b": nrm(ks[6], (DEPTH, D_RNN), 0.02),
        "w_rec_a": nrm(ks[7], (DEPTH, RNN_BLOCKS, RNN_BLOCK_DIM, RNN_BLOCK_DIM), RNN_BLOCK_DIM ** -0.5),
        "b_rec_a": nrm(ks[8], (DEPTH, D_RNN), 0.02),
        "w_rec_i": nrm(ks[9], (DEPTH, RNN_BLOCKS, RNN_BLOCK_DIM, RNN_BLOCK_DIM), RNN_BLOCK_DIM ** -0.5),
        "b_rec_i": nrm(ks[10], (DEPTH, D_RNN), 0.02),
        "lru_lambda": jnp.log(a0) - jnp.log1p(-a0),
        "q_norm_g": 1.0 + nrm(ks[12], (DEPTH, Q_RANK), 0.02),
        "w_uq": nrm(ks[13], (DEPTH, Q_RANK, N_HEADS * (QK_NOPE + QK_ROPE)), Q_RANK ** -0.5),
        "kv_norm_g": 1.0 + nrm(ks[14], (DEPTH, KV_RANK), 0.02),
        "w_ukv": nrm(ks[15], (DEPTH, KV_RANK, N_HEADS * (QK_NOPE + V_DIM)), KV_RANK ** -0.5),
        "w_branch": nrm(ks[16], (DEPTH, D_BRANCH_IN, D_MODEL), 1024 ** -0.5),
        "w_out": nrm(ks[17], (DEPTH, D_MODEL, D_MODEL), D_MODEL ** -0.5),
        "norm_ffn_g": 1.0 + nrm(ks[18], (DEPTH, D_MODEL), 0.02),
        "w_ffn_in": nrm(ks[19], (DEPTH, D_MODEL, 2 * D_FF), D_MODEL ** -0.5),
        "w_ffn_out": nrm(ks[20], (DEPTH, D_FF, D_MODEL), D_FF ** -0.5),
        "final_norm_g": 1.0 + nrm(ks[21], (D_MODEL,), 0.02),
    }


def reference(x, meta_tokens, norm_mix_g, w_in, b_gate, conv_w, conv_b, w_rec_a, b_rec_a, w_rec_i,
              b_rec_i, lru_lambda, q_norm_g, w_uq, kv_norm_g, w_ukv, w_branch, w_out, norm_ffn_g,
              w_ffn_in, w_ffn_out, final_norm_g):
    B, S, D = x.shape
    L = N_META + S
    Lp = ((L + Q_BLOCK - 1) // Q_BLOCK) * Q_BLOCK
    meta = jnp.broadcast_to(meta_tokens.astype(x.dtype)[None], (B, N_META, D))
    h = jnp.pad(jnp.concatenate([meta, x], axis=1), ((0, 0), (0, Lp - L), (0, 0)))
    idx = jnp.arange(Lp, dtype=jnp.int32)
    chunk_id = jnp.where(idx < N_META, 0, (idx - N_META) // CHUNK + 1)
    chunk_id = jnp.where(idx >= L, PAD_CHUNK, chunk_id)
    inv_freq = ROPE_THETA ** (-jnp.arange(0, QK_ROPE, 2, dtype=jnp.float32) / QK_ROPE)
    ang = idx.astype(jnp.float32)[:, None] * inv_freq[None, :]
    cos, sin = jnp.cos(ang), jnp.sin(ang)
    for l in range(DEPTH):
        h = hybrid_layer(h, cos, sin, chunk_id, norm_mix_g[l], w_in[l], b_gate[l], conv_w[l], conv_b[l],
                         w_rec_a[l], b_rec_a[l], w_rec_i[l], b_rec_i[l], lru_lambda[l], q_norm_g[l],
                         w_uq[l], kv_norm_g[l], w_ukv[l], w_branch[l], w_out[l], norm_ffn_g[l],
                         w_ffn_in[l], w_ffn_out[l])
    h = rmsnorm(h, final_norm_g)
    return h[:, N_META:L]
```

```python
import math
from contextlib import ExitStack

import numpy as np
import concourse.bass as bass
import concourse.mybir as mybir
from concourse.bass_utils import run_bass_kernel_spmd

F32 = mybir.dt.float32
BF16 = mybir.dt.bfloat16
AF = mybir.ActivationFunctionType
ALU = mybir.AluOpType

ENGS = ("tensor", "vector", "scalar", "gpsimd", "sync")

D = 1024
S = 8192
NB = 4
NMETA = 16
LT = NMETA + S
DRNN = 1280
NCH = 10
QR = 384
KVR = 256
ROPE = 64
NH = 8
DFF = 2816
NFF = 22
EPS = 1e-6
SCALE = 1.0 / math.sqrt(192.0)
OWN = 4096
BLK = 512
C_X, C_G, C_Q, C_KV, C_KR, C_M = 0, 1280, 2560, 2944, 3200, 3264

V_GMIX, V_GFFN, V_GFIN, V_BGATE = 0, 8, 16, 24
V_CONVW, V_CONVB, V_BA, V_BI, V_LAM = 40, 80, 90, 100, 110
V_GQ, V_GKV, V_SEL = 120, 123, 125
NV = 128


class Counter:
    def __init__(self, sem, name):
        self.sem = sem
        self.name = name
        self.count = 0


class Buf:
    __slots__ = ("name", "w", "r")

    def __init__(self, name=""):
        self.name = name
        self.w = None
        self.r = []


class Phase:
    def __init__(self, nc, ctrs):
        self.nc = nc
        self.ctrs = ctrs
        self.ops = {e: [] for e in ENGS}
        self.seen = {e: {} for e in ENGS}

    def _waits(self, eng, toks):
        seen = self.seen[eng]
        own = self.ctrs[eng] if eng == "tensor" else None
        best = {}
        for t in toks:
            if t is None:
                continue
            c, v = t
            if c is own:
                continue
            if best.get(c, 0) < v:
                best[c] = v
        out = []
        for c, v in best.items():
            if seen.get(c, 0) >= v:
                continue
            seen[c] = v
            out.append((c, v))
        return out

    def op(self, eng, fn, reads=(), writes=(), ctr=None, extra=()):
        toks = list(extra)
        for b in reads:
            toks.append(b.w)
        for b in writes:
            toks.append(b.w)
            toks.extend(b.r)
        waits = self._waits(eng, toks)
        c = ctr if ctr is not None else self.ctrs[eng]
        step = 16 if ctr is not None else 1
        c.count += step
        tok = (c, c.count)
        self.ops[eng].append((fn, waits, (c, step)))
        for b in reads:
            b.r.append(tok)
        for b in writes:
            b.w = tok
            b.r = []
        return tok

    def group(self, eng, fns, reads=(), writes=(), extra=()):
        n = len(fns)
        if n == 1:
            return self.op(eng, fns[0], reads, writes, extra=extra)
        toks = list(extra)
        for b in reads:
            toks.append(b.w)
        for b in writes:
            toks.append(b.w)
            toks.extend(b.r)
        self.ops[eng].append((fns[0], self._waits(eng, toks), None))
        for fn in fns[1:-1]:
            self.ops[eng].append((fn, [], None))
        c = self.ctrs[eng]
        c.count += 1
        tok = (c, c.count)
        self.ops[eng].append((fns[-1], [], (c, 1)))
        for b in reads:
            b.r.append(tok)
        for b in writes:
            b.w = tok
            b.r = []
        return tok

    def dma(self, eng, out, in_, ctr, reads=(), writes=(), extra=()):
        return self.op(eng, lambda e: e.dma_start(out=out, in_=in_), reads, writes, ctr=ctr, extra=extra)

    def emit(self, final_waits=()):
        nc = self.nc
        ops = self.ops
        fw = list(final_waits)
        with nc.Block() as block:
            def run(e, name):
                for fn, waits, inc in ops[name]:
                    for c, v in waits:
                        e.wait_ge(c.sem, v)
                    ins = fn(e)
                    if inc is not None:
                        ins.then_inc(inc[0].sem, inc[1])
                if name == "sync":
                    for c, v in fw:
                        e.wait_ge(c.sem, v)

            @block.tensor
            def _(e):
                run(e, "tensor")

            @block.vector
            def _(e):
                run(e, "vector")

            @block.scalar
            def _(e):
                run(e, "scalar")

            @block.gpsimd
            def _(e):
                run(e, "gpsimd")

            @block.sync
            def _(e):
                run(e, "sync")
        self.ops = {e: [] for e in ENGS}
        self.seen = {e: {} for e in ENGS}


class Builder:
    def __init__(self, debug=False, phases=(1, 2, 3, 4), nheads=None):
        self.debug = debug
        self.nheads = nheads
        self.phases = phases
        self.nc = bass.Bass("TRN2", target_bir_lowering=False)
        self.es = ExitStack()
        self.ndma = 0

    def din(self, name, shape, dt=F32):
        return self.nc.dram_tensor(name, list(shape), dt, kind="ExternalInput").ap()

    def dscratch(self, name, shape, dt):
        if self.debug:
            return self.nc.dram_tensor(name, list(shape), dt, kind="ExternalOutput").ap()
        return self.nc.dram_tensor(name, list(shape), dt).ap()

    def dctr(self):
        self.ndma += 1
        return Counter(self.es.enter_context(self.nc.semaphore("dq%d" % self.ndma)), "dq%d" % self.ndma)

    def sb(self, st, name, shape, dt):
        return st.enter_context(self.nc.sbuf_tensor(name, list(shape), dt))

    def mm(self, out, pairs, wbuf, rbufs, flags=None):
        n = len(pairs)
        fns = []
        for i, (l, r) in enumerate(pairs):
            st = (i == 0) if flags is None else flags[0]
            sp = (i == n - 1) if flags is None else flags[1]
            fns.append(self._mmfn(out, l, r, st, sp))
        return self.ph.group("tensor", fns, reads=rbufs, writes=[wbuf])

    @staticmethod
    def _mmfn(out, l, r, st, sp):
        return lambda e: e.matmul(out, l, r, start=st, stop=sp, skip_group_check=True)

    def act(self, out, in_, func, reads, writes, **kw):
        return self.ph.op("scalar", lambda e: e.activation(out=out, in_=in_, func=func, **kw), reads, writes)

    def tt(self, eng, out, in0, in1, op, reads, writes):
        return self.ph.op(eng, lambda e: e.tensor_tensor(out=out, in0=in0, in1=in1, op=op), reads, writes)

    def ts(self, eng, out, in0, s1, op0, reads, writes, s2=None, op1=None):
        if op1 is None:
            return self.ph.op(eng, lambda e: e.tensor_scalar(out=out, in0=in0, scalar1=s1, scalar2=None, op0=op0), reads, writes)
        return self.ph.op(eng, lambda e: e.tensor_scalar(out=out, in0=in0, scalar1=s1, scalar2=s2, op0=op0, op1=op1), reads, writes)

    def stt(self, out, in0, scalar, in1, op0, op1, reads, writes):
        return self.ph.op("vector", lambda e: e.scalar_tensor_tensor(out=out, in0=in0, scalar=scalar, in1=in1, op0=op0, op1=op1), reads, writes)

    def cp(self, eng, out, in_, reads, writes):
        return self.ph.op(eng, lambda e: e.tensor_copy(out=out, in_=in_), reads, writes)

    def recip(self, out, in_, reads, writes):
        return self.ph.op("vector", lambda e: e.reciprocal(out=out, in_=in_), reads, writes)

    def memset(self, eng, ap, val, writes):
        return self.ph.op(eng, lambda e: e.memset(ap, val), (), writes)

    def load(self, out, in_, wbuf, eng="sync", first=True):
        ctr = self.dctr_for(wbuf)
        if first:
            tok = self.ph.dma(eng, out, in_, ctr, writes=[wbuf])
        else:
            tok = self.ph.dma(eng, out, in_, ctr)
            wbuf.w = tok
        return tok

    def store(self, out, in_, srcbuf, eng="sync"):
        tok = self.ph.dma(eng, out, in_, self.dctr_for(srcbuf), reads=[srcbuf])
        self.pending.append(tok)
        return tok

    def end_phase(self):
        best = {}
        for c, v in self.pending:
            if best.get(c, 0) < v:
                best[c] = v
        self.pending = []
        self.ph.emit(final_waits=[(c, v) for c, v in best.items()])

    def dctr_for(self, buf):
        c = self._bufctr.get(id(buf))
        if c is None:
            c = self.dctr()
            self._bufctr[id(buf)] = c
        return c

    def bank(self, hold=False):
        nb = len(self._banks)
        for _ in range(nb):
            i = self._bank_i
            self._bank_i = (i + 1) % nb
            if id(self._bank_bufs[i]) not in self._held:
                if hold:
                    self._held.add(id(self._bank_bufs[i]))
                return self._banks[i], self._bank_bufs[i]
        raise RuntimeError("all PSUM banks held")

    def rel(self, pb):
        self._held.discard(id(pb))

    def rms8(self, xin, xbuf, N, gcol, sqb, sqbuf, sd, sdbuf, rstd, rsbuf, zT, zbuf, out_f32=None):
        vec = self.vec
        self.act(sqb[:, 0:8 * N], xin[:, 0:8 * N], AF.Square, [xbuf], [sqbuf])
        ps, pb = self.bank()
        self.mm(ps[:, 0:N], [(self.ones[:, :], sqb[:, c * N:(c + 1) * N]) for c in range(8)], pb, [sqbuf, self.cbuf])
        self.act(sd[:, 0:N], ps[:, 0:N], AF.Sqrt, [pb], [sdbuf], scale=1.0 / D, bias=self.epsc[:, 0:1])
        self.recip(rstd[:, 0:N], sd[:, 0:N], [sdbuf], [rsbuf])
        for c in range(8):
            self.stt(zT[:, c * N:(c + 1) * N], xin[:, c * N:(c + 1) * N], vec[:, gcol + c:gcol + c + 1], rstd[:, 0:N],
                     ALU.mult, ALU.mult, [xbuf, rsbuf, self.cbuf], [zbuf])

    def build(self):
        nc = self.nc
        es = self.es
        self._bufctr = {}
        self.pending = []
        xT = self.din("xT", [D, S])
        xoT = self.din("xoT", [D, OWN])
        metaT = self.din("metaT", [D, NMETA])
        w_in = self.din("w_in", [D, 5312])
        w_ra = self.din("w_rec_a", [NCH, 128, 128])
        w_ri = self.din("w_rec_i", [NCH, 128, 128])
        w_uq = self.din("w_uq", [QR, 1536])
        w_ukv = self.din("w_ukv", [KVR, 2048])
        w_br = self.din("w_branch", [2304, D])
        w_o = self.din("w_out", [D, D])
        w_f1 = self.din("w_ffn_in", [D, 2 * DFF])
        w_f2 = self.din("w_ffn_out", [DFF, D])
        vecs = self.din("vecs", [128, NV])
        masks = self.din("masks", [128, 256])
        cs_k = self.din("cs_k", [64, 2 * LT])
        cs_q = self.din("cs_q", [64, 2 * OWN])
        outT = nc.dram_tensor("outT", [D, OWN], F32, kind="ExternalOutput").ap()
        ckv_s = self.dscratch("ckv_s", [KVR, LT], BF16)
        kr_s = self.dscratch("kr_s", [ROPE, LT], BF16)
        yrnn_s = self.dscratch("yrnn_s", [DRNN, OWN], BF16)
        yatt_s = self.dscratch("yatt_s", [D, OWN], BF16)
        hmid_s = self.dscratch("hmid_s", [D, OWN], F32)
        self.sc_bufs = {k: Buf(k) for k in ["ckv", "kr", "yrnn", "yatt", "hmid", "out"]}

        self.ctrs = {e: Counter(es.enter_context(nc.semaphore("c_" + e)), e) for e in ENGS}
        self.ph = Phase(nc, self.ctrs)
        self._banks = [es.enter_context(nc.psum_tensor("pb%d" % i, [128, 512], F32)) for i in range(8)]
        self._bank_bufs = [Buf("pb%d" % i) for i in range(8)]
        self._bank_i = 0
        self._held = set()
        self.vec = self.sb(es, "vec", [128, NV], F32)
        self.vec2 = self.sb(es, "vec2", [128, 64], F32)
        self.ones = self.sb(es, "ones", [128, 128], BF16)
        self.epsc = self.sb(es, "epsc", [128, 1], F32)
        self.cbuf = Buf("consts")
        vec, vec2 = self.vec, self.vec2
        ph = self.ph
        self.load(vec[:, :], vecs, self.cbuf)
        self.memset("gpsimd", self.ones[:, :], 1.0, [self.cbuf])
        self.memset("gpsimd", self.epsc[:, :], EPS, [self.cbuf])
        self.ts("vector", vec2[:, 0:10], vec[:, V_BA:V_BA + 10], 0.5, ALU.mult, [self.cbuf], [self.cbuf])
        self.ts("vector", vec2[:, 10:20], vec[:, V_BI:V_BI + 10], 0.5, ALU.mult, [self.cbuf], [self.cbuf])
        self.act(vec2[:, 40:50], vec[:, V_LAM:V_LAM + 10], AF.Exp, [self.cbuf], [self.cbuf], scale=-1.0)
        self.act(vec2[:, 50:60], vec2[:, 40:50], AF.Ln, [self.cbuf], [self.cbuf], bias=1.0)
        self.ts("vector", vec2[:, 20:30], vec2[:, 50:60], -4.0, ALU.mult, [self.cbuf], [self.cbuf])
        self.ts("vector", vec2[:, 30:40], vec2[:, 50:60], -8.0, ALU.mult, [self.cbuf], [self.cbuf])

        if 1 in self.phases:
            self.phase1(xT, metaT, w_in, w_ra, w_ri, cs_k, ckv_s, kr_s, yrnn_s)
        if 2 in self.phases:
            self.phase2(xoT, w_in, w_uq, w_ukv, masks, cs_q, ckv_s, kr_s, yatt_s)
        if 3 in self.phases:
            self.phase3(xoT, w_in, w_br, w_o, yrnn_s, yatt_s, hmid_s)
        if 4 in self.phases:
            self.phase4(w_f1, w_f2, hmid_s, outT)
        else:
            with ExitStack() as st:
                z = self.sb(st, "zz", [128, 512], F32)
                zb = Buf("zz")
                self.memset("vector", z[:, :], 0.0, [zb])
                ov = outT.rearrange("(c p) t -> p c t", p=128)
                for c in range(8):
                    for t in range(OWN // 512):
                        self.store(ov[:, c, t * 512:(t + 1) * 512], z[:, :], zb)
                self.end_phase()
        es.close()
        return nc

    def wload(self, dst, src, buf):
        tok = self.ph.dma("gpsimd", dst, src, self.dctr_for(buf))
        buf.w = tok
        return tok

    def phase1(self, xT, metaT, w_in, w_ra, w_ri, cs_k, ckv_s, kr_s, yrnn_s):
        nc, ph, vec, vec2 = self.nc, self.ph, self.vec, self.vec2
        G = 5
        with ExitStack() as st:
            Wx = self.sb(st, "Wx", [128, 8 * DRNN], BF16)
            Wg = self.sb(st, "Wg", [128, 8 * DRNN], BF16)
            Wkv = self.sb(st, "Wkv", [128, 8 * KVR], BF16)
            Wkr = self.sb(st, "Wkr", [128, 8 * 128], BF16)
            Wa = self.sb(st, "Wa", [128, NCH * 128], BF16)
            Wi = self.sb(st, "Wi", [128, NCH * 128], BF16)
            wb = Buf("w1")
            w3 = w_in.rearrange("(k p) n -> p k n", p=128)
            for k in range(8):
                self.wload(Wx[:, k * DRNN:(k + 1) * DRNN], w3[:, k, C_X:C_X + DRNN], wb)
                self.wload(Wkv[:, k * KVR:(k + 1) * KVR], w3[:, k, C_KV:C_KV + KVR], wb)
                self.wload(Wkr[:, k * 128:k * 128 + 64], w3[:, k, C_KR:C_KR + 64], wb)
                self.wload(Wkr[:, k * 128 + 64:k * 128 + 96], w3[:, k, C_KR + 32:C_KR + 64], wb)
                self.wload(Wkr[:, k * 128 + 96:k * 128 + 128], w3[:, k, C_KR:C_KR + 32], wb)
            for j in range(NCH):
                self.wload(Wa[:, j * 128:(j + 1) * 128], w_ra[j], wb)
                self.wload(Wi[:, j * 128:(j + 1) * 128], w_ri[j], wb)
            for k in range(8):
                self.wload(Wg[:, k * DRNN:(k + 1) * DRNN], w3[:, k, C_G:C_G + DRNN], wb)

            N = BLK
            xin = self.sb(st, "xin", [128, 8 * N], F32); xb = Buf("xin")
            sqb = self.sb(st, "sqb", [128, 8 * N], BF16); sqbuf = Buf("sqb")
            sd = self.sb(st, "sd", [128, N], F32); sdb = Buf("sd")
            rstd = self.sb(st, "rstd", [128, N], F32); rsb = Buf("rstd")
            zT = self.sb(st, "zT", [128, 8 * N], BF16); zb = Buf("zT")
            ux = self.sb(st, "ux", [128, NCH * (N + 3)], F32); uxb = [Buf("ux%d" % j) for j in range(NCH)]
            xc = self.sb(st, "xc", [128, G * N], F32); xcb_ = [Buf("xc%d" % j) for j in range(G)]
            xcb = self.sb(st, "xcb", [128, G * N], BF16); xcbb = [Buf("xcb%d" % j) for j in range(G)]
            T1 = self.sb(st, "T1", [128, G * N], F32); T1b = [Buf("T1_%d" % j) for j in range(G)]
            T2 = self.sb(st, "T2", [128, G * N], F32); T2b = [Buf("T2_%d" % j) for j in range(G)]
            T3 = self.sb(st, "T3", [128, G * N], F32); T3b = [Buf("T3_%d" % j) for j in range(G)]
            T4 = self.sb(st, "T4", [128, G * N], F32); T4b = [Buf("T4_%d" % j) for j in range(G)]
            YT = self.sb(st, "YT", [128, G * (N // 2)], F32); YTb = [Buf("YT%d" % j) for j in range(G)]
            YU = self.sb(st, "YU", [128, G * (N // 2)], F32); YUb = [Buf("YU%d" % j) for j in range(G)]
            YO = self.sb(st, "YO", [128, G * (N // 2)], BF16); YOb = Buf("YO")
            hst = self.sb(st, "hst", [128, NCH], F32); hstb = [Buf("hst%d" % j) for j in range(NCH)]
            cs = self.sb(st, "cs", [64, 2 * N], F32); csb = Buf("cs")
            kvq = self.sb(st, "kvq", [128, 2 * N], BF16); kvqb = Buf("kvq")
            sd2 = self.sb(st, "sd2", [128, N], F32); sd2b = Buf("sd2")
            rs2 = self.sb(st, "rs2", [128, N], F32); rs2b = Buf("rs2")
            kvo = self.sb(st, "kvo", [128, 2 * N], BF16); kvob = Buf("kvo")
            kt1 = self.sb(st, "kt1", [64, N], F32); kt1b = Buf("kt1")
            kt2 = self.sb(st, "kt2", [64, N], F32); kt2b = Buf("kt2")
            kro = self.sb(st, "kro", [64, N], BF16); krob = Buf("kro")

            self.memset("gpsimd", ux[:, :], 0.0, uxb)
            self.memset("gpsimd", hst[:, :], 0.0, hstb)

            x3 = xT.rearrange("(c p) t -> p c t", p=128)
            m3 = metaT.rearrange("(c p) t -> p c t", p=128)
            ckv3 = ckv_s.rearrange("(c p) t -> p c t", p=128)
            yr3 = yrnn_s.rearrange("(c p) t -> p c t", p=128)
            nblk = S // N

            def load_block(bi):
                if bi == 0:
                    n = NMETA
                    for c in range(8):
                        self.load(xin[:, c * n:(c + 1) * n], m3[:, c, :], xb, first=(c == 0))
                    t0 = 0
                else:
                    n = N
                    for c in range(8):
                        self.load(xin[:, c * n:(c + 1) * n], x3[:, c, (bi - 1) * N:bi * N], xb, first=(c == 0))
                    t0 = NMETA + (bi - 1) * N
                return n, t0

            def load_cs(bi):
                n_ = NMETA if bi == 0 else N
                t_ = 0 if bi == 0 else NMETA + (bi - 1) * N
                self.load(cs[:, 0:n_], cs_k[:, t_:t_ + n_], csb)
                self.load(cs[:, N:N + n_], cs_k[:, LT + t_:LT + t_ + n_], csb, first=False)

            zT2 = self.sb(st, "zTb", [128, 8 * N], BF16)
            zTs = [zT, zT2]
            zbs = [zb, Buf("zTb")]

            def pre_steps(bi):
                n = NMETA if bi == 0 else N
                t0 = 0 if bi == 0 else NMETA + (bi - 1) * N
                zT_, zb_ = zTs[bi % 2], zbs[bi % 2]
                zc = [zT_[:, k * n:(k + 1) * n] for k in range(8)]
                stt_ = {}

                def s1():
                    self.act(sqb[:, 0:8 * n], xin[:, 0:8 * n], AF.Square, [xb], [sqbuf])
                    ps, pb = self.bank(hold=True)
                    self.mm(ps[:, 0:n], [(self.ones[:, :], sqb[:, c * n:(c + 1) * n]) for c in range(8)], pb, [sqbuf, self.cbuf])
                    stt_["ss"] = (ps, pb)

                def s2():
                    ps, pb = stt_["ss"]
                    self.act(sd[:, 0:n], ps[:, 0:n], AF.Sqrt, [pb], [sdb], scale=1.0 / D, bias=self.epsc[:, 0:1])
                    self.rel(pb)
                    self.recip(rstd[:, 0:n], sd[:, 0:n], [sdb], [rsb])

                def s3():
                    for c in range(8):
                        self.stt(zT_[:, c * n:(c + 1) * n], xin[:, c * n:(c + 1) * n], vec[:, V_GMIX + c:V_GMIX + c + 1], rstd[:, 0:n],
                                 ALU.mult, ALU.mult, [xb, rsb, self.cbuf], [zb_])
                    if bi < nblk:
                        load_block(bi + 1)

                def s4():
                    pk = []
                    for c2 in range(2):
                        ps, pb = self.bank(hold=True)
                        self.mm(ps[:, 0:n], [(Wkv[:, k * KVR + c2 * 128:k * KVR + (c2 + 1) * 128], zc[k]) for k in range(8)], pb, [zb_, wb])
                        pk.append((ps, pb))
                    stt_["pk"] = pk
                    for c2 in range(2):
                        self.act(kvq[:, c2 * n:(c2 + 1) * n], pk[c2][0][:, 0:n], AF.Square, [pk[c2][1]], [kvqb])
                    ps, pb = self.bank(hold=True)
                    self.mm(ps[:, 0:n], [(self.ones[:, :], kvq[:, c2 * n:(c2 + 1) * n]) for c2 in range(2)], pb, [kvqb, self.cbuf])
                    stt_["ss2"] = (ps, pb)

                def s5():
                    ps, pb = stt_["ss2"]
                    self.act(sd2[:, 0:n], ps[:, 0:n], AF.Sqrt, [pb], [sd2b], scale=1.0 / KVR, bias=self.epsc[:, 0:1])
                    self.rel(pb)
                    self.recip(rs2[:, 0:n], sd2[:, 0:n], [sd2b], [rs2b])

                def s6():
                    pk = stt_["pk"]
                    for c2 in range(2):
                        self.stt(kvo[:, c2 * n:(c2 + 1) * n], pk[c2][0][:, 0:n], vec[:, V_GKV + c2:V_GKV + c2 + 1], rs2[:, 0:n],
                                 ALU.mult, ALU.mult, [pk[c2][1], rs2b, self.cbuf], [kvob])
                        self.rel(pk[c2][1])
                        self.store(ckv3[:, c2, t0:t0 + n], kvo[:, c2 * n:(c2 + 1) * n], kvob)

                def s7():
                    pr = []
                    for hh in range(2):
                        ps, pb = self.bank()
                        self.mm(ps[0:64, 0:n], [(Wkr[:, k * 128 + hh * 64:k * 128 + (hh + 1) * 64], zc[k]) for k in range(8)], pb, [zb_, wb])
                        pr.append((ps, pb))
                    self.tt("vector", kt1[:, 0:n], pr[0][0][0:64, 0:n], cs[:, 0:n], ALU.mult, [pr[0][1], csb], [kt1b])
                    self.tt("vector", kt2[:, 0:n], pr[1][0][0:64, 0:n], cs[:, N:N + n], ALU.mult, [pr[1][1], csb], [kt2b])
                    self.tt("gpsimd", kro[:, 0:n], kt1[:, 0:n], kt2[:, 0:n], ALU.add, [kt1b, kt2b], [krob])
                    self.store(kr_s[:, t0:t0 + n], kro[:, 0:n], krob)
                    if bi < nblk:
                        load_cs(bi + 1)
                return [s1, s2, s3, s4, s5, s6, s7]

            def rnn_group(bi, g0, steps=()):
                steps = list(steps)

                def slot(k=1):
                    for _ in range(k):
                        if steps:
                            steps.pop(0)()
                n = NMETA if bi == 0 else N
                own = bi > 0
                W = N + 3
                zT_, zb_ = zTs[bi % 2], zbs[bi % 2]
                zc = [zT_[:, k * n:(k + 1) * n] for k in range(8)]
                chs = list(range(g0, g0 + G))
                pu, pg, pa, pi = {}, {}, {}, {}
                def issue_ux(j):
                    ps, pb = self.bank(hold=True)
                    self.mm(ps[:, 0:n], [(Wx[:, k * DRNN + j * 128:k * DRNN + (j + 1) * 128], zc[k]) for k in range(8)], pb, [zb_, wb])
                    pu[j] = (ps, pb)

                def tanh_stage(jj, j):
                    sl = slice(jj * N, jj * N + n)
                    self.act(T1[:, sl], pa[j][0][:, 0:n], AF.Tanh, [pa[j][1], self.cbuf], [T1b[jj]], scale=0.5, bias=vec2[:, j:j + 1])
                    self.act(T3[:, sl], pi[j][0][:, 0:n], AF.Tanh, [pi[j][1], self.cbuf], [T3b[jj]], scale=0.5, bias=vec2[:, 10 + j:11 + j])
                    self.rel(pa[j][1])
                    self.rel(pi[j][1])

                for j in chs:
                    issue_ux(j)
                for jj, j in enumerate(chs):
                    ps, pb = pu[j]
                    o = j * W
                    self.act(ux[:, o + 3:o + 3 + n], ps[:, 0:n], AF.Identity, [pb], [uxb[j]])
                    self.rel(pb)
                    xcj = xc[:, jj * N:jj * N + n]
                    self.act(xcj, ux[:, o:o + n], AF.Identity, [uxb[j], self.cbuf], [xcb_[jj]],
                             scale=vec[:, V_CONVW + j:V_CONVW + j + 1], bias=vec[:, V_CONVB + j:V_CONVB + j + 1])
                    for kk in range(1, 4):
                        self.stt(xcj, ux[:, o + kk:o + kk + n], vec[:, V_CONVW + kk * 10 + j:V_CONVW + kk * 10 + j + 1], xcj,
                                 ALU.mult, ALU.add, [uxb[j], self.cbuf], [xcb_[jj]])
                    self.cp("gpsimd", ux[:, o:o + 3], ux[:, o + n:o + n + 3], [], [uxb[j]])
                    self.act(xcb[:, jj * N:jj * N + n], xcj, AF.Identity, [xcb_[jj]], [xcbb[jj]])
                    if jj < 3:
                        slot()

                def issue_gates(jj, j):
                    pa_, pab = self.bank(hold=True)
                    self.mm(pa_[:, 0:n], [(Wa[:, j * 128:(j + 1) * 128], xcb[:, jj * N:jj * N + n])], pab, [xcbb[jj], wb])
                    pi_, pib = self.bank(hold=True)
                    self.mm(pi_[:, 0:n], [(Wi[:, j * 128:(j + 1) * 128], xcb[:, jj * N:jj * N + n])], pib, [xcbb[jj], wb])
                    pa[j] = (pa_, pab)
                    pi[j] = (pi_, pib)
                issue_gates(0, chs[0])
                for jj, j in enumerate(chs):
                    if jj + 1 < G:
                        issue_gates(jj + 1, chs[jj + 1])
                    tanh_stage(jj, j)
                if own:
                    for j in chs:
                        ps, pb = self.bank(hold=True)
                        self.mm(ps[:, 0:n], [(Wg[:, k * DRNN + j * 128:k * DRNN + (j + 1) * 128], zc[k]) for k in range(8)], pb, [zb_, wb])
                        pg[j] = (ps, pb)
                slot()
                for jj, j in enumerate(chs):
                    sl = slice(jj * N, jj * N + n)
                    self.stt(T3[:, sl], T3[:, sl], 1.0, xc[:, sl], ALU.add, ALU.mult, [xcb_[jj]], [T3b[jj]])
                for jj, j in enumerate(chs):
                    sl = slice(jj * N, jj * N + n)
                    self.act(T2[:, sl], T1[:, sl], AF.Exp, [T1b[jj], self.cbuf], [T2b[jj]], scale=vec2[:, 30 + j:31 + j], bias=vec2[:, 30 + j:31 + j])
                    self.act(T1[:, sl], T1[:, sl], AF.Exp, [self.cbuf], [T1b[jj]], scale=vec2[:, 20 + j:21 + j], bias=vec2[:, 20 + j:21 + j])
                slot()
                for jj, j in enumerate(chs):
                    sl = slice(jj * N, jj * N + n)
                    self.act(T2[:, sl], T2[:, sl], AF.Sqrt, [], [T2b[jj]], scale=-0.25, bias=0.25)
                slot()
                for jj, j in enumerate(chs):
                    sl = slice(jj * N, jj * N + n)
                    self.tt("gpsimd", T2[:, sl], T2[:, sl], T3[:, sl], ALU.mult, [T3b[jj]], [T2b[jj]])
                for jj, j in enumerate(chs):
                    sl = slice(jj * N, jj * N + n)
                    ph.op("vector", self._scanfn(T4[:, sl], T1[:, sl], T2[:, sl], hst[:, j:j + 1]),
                          [T1b[jj], T2b[jj], hstb[j]], [T4b[jj]])
                for jj, j in enumerate(chs):
                    self.cp("gpsimd", hst[:, j:j + 1], T4[:, jj * N + n - 1:jj * N + n], [T4b[jj]], [hstb[j]])
                slot()
                if not own:
                    slot(10)
                    return
                for jj, j in enumerate(chs):
                    sl = slice(jj * N, jj * N + n)
                    self.act(T3[:, sl], pg[j][0][:, 0:n], AF.Gelu_apprx_tanh, [pg[j][1]], [T3b[jj]])
                    self.rel(pg[j][1])
                h2 = N // 2
                for jj, j in enumerate(chs):
                    sl = slice(jj * N, jj * N + n)
                    gv = T3[:, sl].rearrange("p (a e t) -> p a e t", a=2, e=2)
                    hv = T4[:, sl].rearrange("p (a e t) -> p a e t", a=2, e=2)
                    ytv = YT[:, jj * h2:(jj + 1) * h2].rearrange("p (a t) -> p a t", a=2)
                    yuv = YU[:, jj * h2:(jj + 1) * h2].rearrange("p (a t) -> p a t", a=2)
                    yov = YO[:, jj * h2:(jj + 1) * h2].rearrange("p (a t) -> p a t", a=2)
                    self.stt(ytv, gv[:, :, 0, :], vec[:, V_SEL:V_SEL + 1], hv[:, :, 0, :], ALU.mult, ALU.mult,
                             [T3b[jj], T4b[jj], self.cbuf], [YTb[jj]])
                    self.stt(yuv, gv[:, :, 1, :], vec[:, V_SEL + 1:V_SEL + 2], hv[:, :, 1, :], ALU.mult, ALU.mult,
                             [T3b[jj], T4b[jj], self.cbuf], [YUb[jj]])
                    self.tt("gpsimd", yov, ytv, yuv, ALU.add, [YTb[jj], YUb[jj]], [YOb])
                ob0 = (bi - 1) * h2
                for jj, j in enumerate(chs):
                    self.store(yr3[:, j, ob0:ob0 + h2], YO[:, jj * h2:(jj + 1) * h2], YOb)
                slot(10)

            load_block(0)
            load_cs(0)
            for f in pre_steps(0):
                f()
            for bi in range(nblk + 1):
                rnn_group(bi, 0, pre_steps(bi + 1) if bi < nblk else ())
                rnn_group(bi, G)
            self.end_phase()

    @staticmethod
    def _scanfn(out, d0, d1, init):
        return lambda e: e.tensor_tensor_scan(out=out, data0=d0, data1=d1, initial=init, op0=ALU.mult, op1=ALU.add)

    def phase2(self, xoT, w_in, w_uq, w_ukv, masks, cs_q, ckv_s, kr_s, yatt_s):
        nc, ph, vec = self.nc, self.ph, self.vec
        N = BLK
        NT = 65
        with ExitStack() as st:
            ckvT = self.sb(st, "ckvT", [128, 2 * LT], BF16); ckvb = Buf("ckvT")
            krT = self.sb(st, "krT", [128, LT], BF16); krb = Buf("krT")
            qnT = self.sb(st, "qnT", [128, 3 * OWN], BF16); qnb = [Buf("qn%d" % i) for i in range(OWN // N)]
            csq = self.sb(st, "csq", [64, 2 * OWN], F32); csqb = Buf("csq")
            Wuq = self.sb(st, "Wuq", [128, 3 * 1536], BF16)
            Wus = self.sb(st, "Wus", [128, 3 * 512], BF16)
            Wukv = self.sb(st, "Wukv", [128, 2 * 2048], BF16)
            Mk = self.sb(st, "Mk", [128, 256], BF16)
            wb = Buf("w2")
            ckv3 = ckv_s.rearrange("(c p) t -> p c t", p=128)
            for c2 in range(2):
                self.load(ckvT[:, c2 * LT:(c2 + 1) * LT], ckv3[:, c2, :], ckvb, first=(c2 == 0))
            self.memset("gpsimd", krT[64:128, :], 0.0, [krb])
            self.load(krT[0:64, :], kr_s, krb)
            self.load(csq[:, :], cs_q, csqb)
            with ExitStack() as st2:
                Wq = self.sb(st2, "Wq", [128, 8 * QR], BF16)
                w3 = w_in.rearrange("(k p) n -> p k n", p=128)
                for k in range(8):
                    self.wload(Wq[:, k * QR:(k + 1) * QR], w3[:, k, C_Q:C_Q + QR], wb)
                xin = self.sb(st2, "xin2", [128, 8 * N], F32); xb = Buf("xin2")
                sqb = self.sb(st2, "sqb2", [128, 8 * N], BF16); sqbuf = Buf("sqb2")
                sd = self.sb(st2, "sd_2", [128, N], F32); sdb = Buf("sd_2")
                rstd = self.sb(st2, "rstd2", [128, N], F32); rsb = Buf("rstd2")
                zT = self.sb(st2, "zT2", [128, 8 * N], BF16); zb = Buf("zT2")
                qq = self.sb(st2, "qq", [128, 3 * N], BF16); qqb = Buf("qq")
                sd3 = self.sb(st2, "sd3", [128, N], F32); sd3b = Buf("sd3")
                rs3 = self.sb(st2, "rs3", [128, N], F32); rs3b = Buf("rs3")
                xo3 = xoT.rearrange("(c p) t -> p c t", p=128)
                for c in range(8):
                    self.load(xin[:, c * N:(c + 1) * N], xo3[:, c, 0:N], xb, first=(c == 0))
                for ob in range(OWN // N):
                    self.rms8(xin, xb, N, V_GMIX, sqb, sqbuf, sd, sdb, rstd, rsb, zT, zb)
                    if ob + 1 < OWN // N:
                        for c in range(8):
                            self.load(xin[:, c * N:(c + 1) * N], xo3[:, c, (ob + 1) * N:(ob + 2) * N], xb, first=(c == 0))
                    pq = []
                    for c3 in range(3):
                        ps, pb = self.bank()
                        self.mm(ps[:, 0:N], [(Wq[:, k * QR + c3 * 128:k * QR + (c3 + 1) * 128], zT[:, k * N:(k + 1) * N]) for k in range(8)], pb, [zb, wb])
                        pq.append((ps, pb))
                    for c3 in range(3):
                        self.act(qq[:, c3 * N:(c3 + 1) * N], pq[c3][0][:, 0:N], AF.Square, [pq[c3][1]], [qqb])
                    ps, pb = self.bank()
                    self.mm(ps[:, 0:N], [(self.ones[:, :], qq[:, c3 * N:(c3 + 1) * N]) for c3 in range(3)], pb, [qqb, self.cbuf])
                    self.act(sd3[:, :], ps[:, 0:N], AF.Sqrt, [pb], [sd3b], scale=1.0 / QR, bias=self.epsc[:, 0:1])
                    self.recip(rs3[:, :], sd3[:, :], [sd3b], [rs3b])
                    for c3 in range(3):
                        self.stt(qnT[:, c3 * OWN + ob * N:c3 * OWN + (ob + 1) * N], pq[c3][0][:, 0:N], vec[:, V_GQ + c3:V_GQ + c3 + 1], rs3[:, :],
                                 ALU.mult, ALU.mult, [pq[c3][1], rs3b, self.cbuf], [qnb[ob]])
                self.end_phase()
            uq3 = w_uq.rearrange("(k p) n -> p k n", p=128)
            for k in range(3):
                self.wload(Wuq[:, k * 1536:(k + 1) * 1536], uq3[:, k, :], wb)
                src = uq3[:, k, :].rearrange("p (h d) -> p h d", d=192)
                dst = Wus[:, k * 512:(k + 1) * 512].rearrange("p (h d) -> p h d", d=64)
                self.wload(dst[:, :, 0:32], src[:, :, 160:192], wb)
                self.wload(dst[:, :, 32:64], src[:, :, 128:160], wb)
            kv3 = w_ukv.rearrange("(k p) n -> p k n", p=128)
            for k in range(2):
                self.wload(Wukv[:, k * 2048:(k + 1) * 2048], kv3[:, k, :], wb)
            self.wload(Mk[:, :], masks, wb)
            KT = [self.sb(st, "KT%d" % i, [128, LT], BF16) for i in range(2)]; KTb = [Buf("KT%d" % i) for i in range(2)]
            Vh = [self.sb(st, "Vh%d" % i, [128, NT * 128], BF16) for i in range(2)]; Vb = [Buf("Vh%d" % i) for i in range(2)]
            QT = [self.sb(st, "QT%d" % i, [128, N], BF16) for i in range(2)]; QTb = [Buf("QT%d" % i) for i in range(2)]
            QRt = [self.sb(st, "QR%d" % i, [128, N], BF16) for i in range(2)]; QRb = [Buf("QR%d" % i) for i in range(2)]
            for i in range(2):
                self.memset("gpsimd", QRt[i][64:128, :], 0.0, [QRb[i]])
            qt1 = self.sb(st, "qt1", [64, N], F32); qt1b = Buf("qt1")
            qt2 = self.sb(st, "qt2", [64, N], F32); qt2b = Buf("qt2")
            NP = 4
            Pt = [self.sb(st, "Pt%d" % i, [128, N], BF16) for i in range(NP)]; Ptb = [Buf("Pt%d" % i) for i in range(NP)]
            rden = self.sb(st, "rden", [128, N], F32); rdb = Buf("rden")
            Ob = [self.sb(st, "Ob%d" % i, [128, N], BF16) for i in range(2)]; Obb = [Buf("Ob%d" % i) for i in range(2)]
            ya3 = yatt_s.rearrange("(c p) t -> p c t", p=128)
            allbanks, allbufs = self._banks, self._bank_bufs
            self._banks, self._bank_bufs, self._bank_i = allbanks[0:4], allbufs[0:4], 0
            OTs = [(allbanks[4], allbufs[4]), (allbanks[5], allbufs[5])]
            DNs = [(allbanks[6], allbufs[6]), (allbanks[7], allbufs[7])]
            nheads = NH if self.nheads is None else self.nheads
            NG = OWN // N

            def proj_tasks(h):
                KTh, KTbh, Vhh, Vbh = KT[h % 2], KTb[h % 2], Vh[h % 2], Vb[h % 2]
                tasks = []

                def ktask(c0, n):
                    def f():
                        ps, pb = self.bank()
                        self.mm(ps[:, 0:n], [(Wukv[:, k * 2048 + h * 256:k * 2048 + h * 256 + 128], ckvT[:, k * LT + c0:k * LT + c0 + n]) for k in range(2)],
                                pb, [ckvb, wb])
                        self.cp("vector", KTh[:, c0:c0 + n], ps[:, 0:n], [pb], [KTbh])
                    return f

                def vtask0():
                    ps, pb = self.bank()
                    self.mm(ps[0:NMETA, 0:128], [(ckvT[:, k * LT:k * LT + NMETA], Wukv[:, k * 2048 + h * 256 + 128:k * 2048 + h * 256 + 256]) for k in range(2)],
                            pb, [ckvb, wb])
                    self.cp("vector", Vhh[0:NMETA, 0:128], ps[0:NMETA, 0:128], [pb], [Vbh])

                def vtask(t4):
                    def f():
                        ps, pb = self.bank()
                        for q4 in range(4):
                            c0 = NMETA + (t4 + q4) * 128
                            self.mm(ps[:, q4 * 128:(q4 + 1) * 128],
                                    [(ckvT[:, k * LT + c0:k * LT + c0 + 128], Wukv[:, k * 2048 + h * 256 + 128:k * 2048 + h * 256 + 256]) for k in range(2)],
                                    pb, [ckvb, wb])
                        self.cp("vector", Vhh[:, (1 + t4) * 128:(5 + t4) * 128], ps[:, 0:512], [pb], [Vbh])
                    return f
                tasks.append(ktask(0, NMETA))
                tasks.append(vtask0)
                for i in range(S // N):
                    tasks.append(ktask(NMETA + i * N, N))
                    tasks.append(vtask(4 * i))
                return tasks

            def qproj(h, g, qi):
                q0 = g * N
                ps, pb = self.bank()
                self.mm(ps[:, 0:N], [(Wuq[:, k * 1536 + h * 192:k * 1536 + h * 192 + 128], qnT[:, k * OWN + q0:k * OWN + q0 + N]) for k in range(3)],
                        pb, [qnb[g], wb])
                self.cp("vector", QT[qi][:, :], ps[:, 0:N], [pb], [QTb[qi]])
                psa, pba = self.bank()
                self.mm(psa[0:64, 0:N], [(Wuq[:, k * 1536 + h * 192 + 128:k * 1536 + h * 192 + 192], qnT[:, k * OWN + q0:k * OWN + q0 + N]) for k in range(3)],
                        pba, [qnb[g], wb])
                self.tt("vector", qt1[:, :], psa[0:64, 0:N], csq[:, q0:q0 + N], ALU.mult, [pba, csqb], [qt1b])
                psb, pbb = self.bank()
                self.mm(psb[0:64, 0:N], [(Wus[:, k * 512 + h * 64:k * 512 + (h + 1) * 64], qnT[:, k * OWN + q0:k * OWN + q0 + N]) for k in range(3)],
                        pbb, [qnb[g], wb])
                self.tt("vector", qt2[:, :], psb[0:64, 0:N], csq[:, OWN + q0:OWN + q0 + N], ALU.mult, [pbb, csqb], [qt2b])
                self.tt("gpsimd", QRt[qi][0:64, :], qt1[:, :], qt2[:, :], ALU.add, [qt1b, qt2b], [QRb[qi]])

            groups = [(h, g) for h in range(nheads) for g in range(NG)]
            units = []
            for gi_, (h, g) in enumerate(groups):
                nkt = 1 + 8 * g + 8
                for kt in range(nkt):
                    units.append((gi_, h, g, kt, nkt))
            for t in proj_tasks(0):
                t()
            qproj(groups[0][0], groups[0][1], 0)
            pend = []
            pend_every = 1
            DPT = 2
            nU = len(units)
            live = {}
            for step in range(nU + DPT):
                if step < nU:
                    gi_, h, g, kt, nkt = units[step]
                    qi = gi_ % 2
                    if kt == 0 and g == 0 and h + 1 < nheads:
                        pend = proj_tasks(h + 1)
                        nun = sum(1 + 8 * g2 + 8 for g2 in range(NG))
                        pend_every = max(1, (nun - 40) // len(pend))
                    if kt == nkt // 2 and gi_ + 1 < len(groups):
                        qproj(groups[gi_ + 1][0], groups[gi_ + 1][1], (gi_ + 1) % 2)
                    if kt == 0:
                        k0, nk, qs, mask = 0, NMETA, 0, None
                    else:
                        s_ = kt - 1
                        k0, nk = NMETA + s_ * 128, 128
                        d = s_ - 8 * g
                        if d < 0:
                            qs, mask = 0, None
                        else:
                            qs, mask = (d // 2) * 128, d % 2
                    ps, pb = self.bank()
                    self.mm(ps[0:nk, qs:N], [(KT[h % 2][:, k0:k0 + nk], QT[qi][:, qs:N]), (krT[:, k0:k0 + nk], QRt[qi][:, qs:N])],
                            pb, [KTb[h % 2], krb, QTb[qi], QRb[qi]])
                    P, Pb = Pt[step % NP], Ptb[step % NP]
                    self.act(P[0:nk, qs:N], ps[0:nk, qs:N], AF.Exp, [pb], [Pb], scale=SCALE)
                    if mask is not None:
                        self.tt("gpsimd", P[:, qs:qs + 128], P[:, qs:qs + 128], Mk[:, mask * 128:(mask + 1) * 128], ALU.mult, [wb], [Pb])
                    live[step] = (nk, qs)
                    if pend and kt % pend_every == 0 and g >= 0:
                        pend.pop(0)()
                if step >= DPT:
                    u = step - DPT
                    gi_, h, g, kt, nkt = units[u]
                    qi = gi_ % 2
                    nk, qs = live.pop(u)
                    P, Pb = Pt[u % NP], Ptb[u % NP]
                    OT, OTb = OTs[qi]
                    DN, DNb = DNs[qi]
                    first = kt == 0
                    last = kt == nkt - 1
                    self.mm(OT[:, qs:N], [(Vh[h % 2][0:nk, kt * 128:(kt + 1) * 128], P[0:nk, qs:N])], OTb, [Vb[h % 2], Pb], flags=(first, last))
                    self.mm(DN[:, qs:N], [(self.ones[0:nk, :], P[0:nk, qs:N])], DNb, [Pb, self.cbuf], flags=(first, last))
                    if last:
                        q0 = g * N
                        self.recip(rden[:, :], DN[:, 0:N], [DNb], [rdb])
                        self.tt("vector", Ob[qi][:, :], OT[:, 0:N], rden[:, :], ALU.mult, [OTb, rdb], [Obb[qi]])
                        self.store(ya3[:, h, q0:q0 + N], Ob[qi][:, :], Obb[qi])
                        while g == NG - 1 and pend:
                            pend.pop(0)()
            self._banks, self._bank_bufs, self._bank_i = allbanks, allbufs, 0
            self.end_phase()

    def phase3(self, xoT, w_in, w_br, w_o, yrnn_s, yatt_s, hmid_s):
        nc, ph, vec = self.nc, self.ph, self.vec
        N = BLK
        with ExitStack() as st:
            Wm = self.sb(st, "Wm", [128, 8 * 2048], BF16)
            Wb = self.sb(st, "Wb", [128, 18 * D], BF16)
            Wo = self.sb(st, "Wo", [128, 8 * D], BF16)
            wb = Buf("w3")
            w3 = w_in.rearrange("(k p) n -> p k n", p=128)
            for k in range(8):
                self.wload(Wm[:, k * 2048:(k + 1) * 2048], w3[:, k, C_M:C_M + 2048], wb)
            b3 = w_br.rearrange("(k p) n -> p k n", p=128)
            for k in range(18):
                self.wload(Wb[:, k * D:(k + 1) * D], b3[:, k, :], wb)
            o3 = w_o.rearrange("(k p) n -> p k n", p=128)
            for k in range(8):
                self.wload(Wo[:, k * D:(k + 1) * D], o3[:, k, :], wb)
            xin = [self.sb(st, "xin3_%d" % i, [128, 8 * N], F32) for i in range(2)]; xb = [Buf("xin3_%d" % i) for i in range(2)]
            sqb = self.sb(st, "sqb3", [128, 8 * N], BF16); sqbuf = Buf("sqb3")
            sd = self.sb(st, "sd_3", [128, N], F32); sdb = Buf("sd_3")
            rstd = self.sb(st, "rstd3", [128, N], F32); rsb = Buf("rstd3")
            zT = self.sb(st, "zT3", [128, 8 * N], BF16); zb = Buf("zT3")
            yr = self.sb(st, "yr", [128, NCH * N], BF16); yrb = Buf("yr")
            ya = self.sb(st, "ya", [128, 8 * N], BF16); yab = Buf("ya")
            G0 = [self.sb(st, "G0_%d" % i, [128, N], F32) for i in range(2)]; G0b = [Buf("G0_%d" % i) for i in range(2)]
            G1 = [self.sb(st, "G1_%d" % i, [128, N], F32) for i in range(2)]; G1b = [Buf("G1_%d" % i) for i in range(2)]
            mx = self.sb(st, "mx", [128, 8 * N], BF16); mxb = Buf("mx")
            xo3 = xoT.rearrange("(c p) t -> p c t", p=128)
            yr3 = yrnn_s.rearrange("(c p) t -> p c t", p=128)
            ya3 = yatt_s.rearrange("(c p) t -> p c t", p=128)
            hm3 = hmid_s.rearrange("(c p) t -> p c t", p=128)
            nb = OWN // N

            def ld(ob):
                xi = xin[ob % 2]
                for c in range(8):
                    self.load(xi[:, c * N:(c + 1) * N], xo3[:, c, ob * N:(ob + 1) * N], xb[ob % 2], first=(c == 0))
            ld(0)
            for ob in range(nb):
                xi, xbb = xin[ob % 2], xb[ob % 2]
                self.rms8(xi, xbb, N, V_GMIX, sqb, sqbuf, sd, sdb, rstd, rsb, zT, zb)
                for c in range(NCH):
                    self.load(yr[:, c * N:(c + 1) * N], yr3[:, c, ob * N:(ob + 1) * N], yrb, first=(c == 0))
                for c in range(8):
                    self.load(ya[:, c * N:(c + 1) * N], ya3[:, c, ob * N:(ob + 1) * N], yab, first=(c == 0))
                if ob + 1 < nb:
                    ld(ob + 1)
                for fc in range(8):
                    i2 = fc % 2
                    p0, p0b = self.bank()
                    self.mm(p0[:, 0:N], [(Wm[:, k * 2048 + fc * 128:k * 2048 + (fc + 1) * 128], zT[:, k * N:(k + 1) * N]) for k in range(8)], p0b, [zb, wb])
                    p1, p1b = self.bank()
                    self.mm(p1[:, 0:N], [(Wm[:, k * 2048 + D + fc * 128:k * 2048 + D + (fc + 1) * 128], zT[:, k * N:(k + 1) * N]) for k in range(8)], p1b, [zb, wb])
                    pr, prb = self.bank()
                    self.mm(pr[:, 0:N], [(Wb[:, k * D + fc * 128:k * D + (fc + 1) * 128], yr[:, k * N:(k + 1) * N]) for k in range(NCH)], prb, [yrb, wb])
                    pa, pab = self.bank()
                    self.mm(pa[:, 0:N], [(Wb[:, (NCH + k) * D + fc * 128:(NCH + k) * D + (fc + 1) * 128], ya[:, k * N:(k + 1) * N]) for k in range(8)], pab, [yab, wb])
                    self.act(G0[i2][:, :], p0[:, 0:N], AF.Sigmoid, [p0b, self.cbuf], [G0b[i2]], bias=vec[:, V_BGATE + fc:V_BGATE + fc + 1])
                    self.act(G1[i2][:, :], p1[:, 0:N], AF.Sigmoid, [p1b, self.cbuf], [G1b[i2]], bias=vec[:, V_BGATE + 8 + fc:V_BGATE + 9 + fc])
                    self.tt("vector", G0[i2][:, :], pr[:, 0:N], G0[i2][:, :], ALU.mult, [prb], [G0b[i2]])
                    self.tt("vector", G1[i2][:, :], pa[:, 0:N], G1[i2][:, :], ALU.mult, [pab], [G1b[i2]])
                    self.tt("gpsimd", mx[:, fc * N:(fc + 1) * N], G0[i2][:, :], G1[i2][:, :], ALU.add, [G0b[i2], G1b[i2]], [mxb])
                for fc in range(8):
                    ps, pb = self.bank()
                    self.mm(ps[:, 0:N], [(Wo[:, k * D + fc * 128:k * D + (fc + 1) * 128], mx[:, k * N:(k + 1) * N]) for k in range(8)], pb, [mxb, wb])
                    self.tt("vector", xi[:, fc * N:(fc + 1) * N], ps[:, 0:N], xi[:, fc * N:(fc + 1) * N], ALU.add, [pb], [xbb])
                for c in range(8):
                    self.store(hm3[:, c, ob * N:(ob + 1) * N], xi[:, c * N:(c + 1) * N], xbb)
            self.end_phase()

    def phase4(self, w_f1, w_f2, hmid_s, outT):
        nc, ph, vec = self.nc, self.ph, self.vec
        N = BLK
        with ExitStack() as st:
            W1 = self.sb(st, "W1", [128, 8 * 2 * DFF], BF16)
            W2 = self.sb(st, "W2", [128, NFF * D], BF16)
            wb = Buf("w4")
            f13 = w_f1.rearrange("(k p) n -> p k n", p=128)
            for k in range(8):
                for (a, b) in ((0, 2048), (2048, 4096), (4096, 5632)):
                    self.wload(W1[:, k * 5632 + a:k * 5632 + b], f13[:, k, a:b], wb)
            f23 = w_f2.rearrange("(k p) n -> p k n", p=128)
            for k in range(NFF):
                self.wload(W2[:, k * D:(k + 1) * D], f23[:, k, :], wb)
            hm = [self.sb(st, "hm%d" % i, [128, 8 * N], F32) for i in range(2)]; hb = [Buf("hm%d" % i) for i in range(2)]
            sd = self.sb(st, "sd_4", [128, N], F32); sdb = Buf("sd_4")
            rstd = self.sb(st, "rstd4", [128, N], F32); rsb = Buf("rstd4")
            zT = self.sb(st, "zT4", [128, 8 * N], BF16); zb = Buf("zT4")
            sg = [self.sb(st, "sg%d" % i, [128, N], F32) for i in range(2)]; sgb = [Buf("sg%d" % i) for i in range(2)]
            actb = self.sb(st, "actb", [128, NFF * N], BF16); acb = Buf("actb")
            sqb, sqbuf = actb, acb
            hm3 = hmid_s.rearrange("(c p) t -> p c t", p=128)
            ov = outT.rearrange("(c p) t -> p c t", p=128)
            nb = OWN // N

            def ld(ob):
                for c in range(8):
                    self.load(hm[ob % 2][:, c * N:(c + 1) * N], hm3[:, c, ob * N:(ob + 1) * N], hb[ob % 2], first=(c == 0))
            ld(0)
            for ob in range(nb):
                h_, hbb = hm[ob % 2], hb[ob % 2]
                self.rms8(h_, hbb, N, V_GFFN, sqb, sqbuf, sd, sdb, rstd, rsb, zT, zb)
                if ob + 1 < nb:
                    ld(ob + 1)
                for j in range(NFF):
                    i2 = j % 2
                    pg, pgb = self.bank()
                    self.mm(pg[:, 0:N], [(W1[:, k * 5632 + j * 128:k * 5632 + (j + 1) * 128], zT[:, k * N:(k + 1) * N]) for k in range(8)], pgb, [zb, wb])
                    pu, pub = self.bank()
                    self.mm(pu[:, 0:N], [(W1[:, k * 5632 + DFF + j * 128:k * 5632 + DFF + (j + 1) * 128], zT[:, k * N:(k + 1) * N]) for k in range(8)], pub, [zb, wb])
                    self.act(sg[i2][:, :], pg[:, 0:N], AF.Silu, [pgb], [sgb[i2]])
                    self.tt("vector", actb[:, j * N:(j + 1) * N], pu[:, 0:N], sg[i2][:, :], ALU.mult, [pub, sgb[i2]], [acb])
                for fc in range(8):
                    ps, pb = self.bank()
                    self.mm(ps[:, 0:N], [(W2[:, k * D + fc * 128:k * D + (fc + 1) * 128], actb[:, k * N:(k + 1) * N]) for k in range(NFF)], pb, [acb, wb])
                    self.tt("vector", h_[:, fc * N:(fc + 1) * N], ps[:, 0:N], h_[:, fc * N:(fc + 1) * N], ALU.add, [pb], [hbb])
                self.act(sqb[:, 0:8 * N], h_[:, 0:8 * N], AF.Square, [hbb], [sqbuf])
                ps, pb = self.bank()
                self.mm(ps[:, 0:N], [(self.ones[:, :], sqb[:, c * N:(c + 1) * N]) for c in range(8)], pb, [sqbuf, self.cbuf])
                self.act(sd[:, 0:N], ps[:, 0:N], AF.Sqrt, [pb], [sdb], scale=1.0 / D, bias=self.epsc[:, 0:1])
                self.recip(rstd[:, 0:N], sd[:, 0:N], [sdb], [rsb])
                for c in range(8):
                    self.stt(h_[:, c * N:(c + 1) * N], h_[:, c * N:(c + 1) * N], vec[:, V_GFIN + c:V_GFIN + c + 1], rstd[:, 0:N],
                             ALU.mult, ALU.mult, [rsb, self.cbuf], [hbb])
                for c in range(8):
                    self.store(ov[:, c, ob * N:(ob + 1) * N], h_[:, c * N:(c + 1) * N], hbb)
            self.end_phase()


def _cos_sin_tables():
    inv = (10000.0 ** (-np.arange(0, ROPE, 2, dtype=np.float32) / np.float32(ROPE))).astype(np.float32)
    pos = np.arange(LT, dtype=np.float32)
    ang = (pos[:, None] * inv[None, :]).astype(np.float32)
    cos = np.cos(ang).astype(np.float32).T
    sin = np.sin(ang).astype(np.float32).T
    cos2 = np.concatenate([cos, cos], 0)
    sin2 = np.concatenate([-sin, sin], 0)
    return np.ascontiguousarray(cos2), np.ascontiguousarray(sin2)


def _chunkvec(v):
    v = np.asarray(v, np.float32).reshape(-1)
    return v.reshape(-1, 128).T


def make_in_maps(inputs):
    f = lambda k: np.asarray(inputs[k], np.float32)
    x = f("x")
    cos2, sin2 = _cos_sin_tables()
    cs_k = np.ascontiguousarray(np.concatenate([cos2, sin2], 1))
    dchunk = np.zeros((128, 128), np.float32)
    kk = np.arange(128)[:, None] // 64
    qq = np.arange(128)[None, :] // 64
    dmask = (kk <= qq).astype(np.float32)
    vec_common = np.zeros((128, NV), np.float32)
    vec_common[:, V_GMIX:V_GMIX + 8] = _chunkvec(f("norm_mix_g")[0])
    vec_common[:, V_GFFN:V_GFFN + 8] = _chunkvec(f("norm_ffn_g")[0])
    vec_common[:, V_GFIN:V_GFIN + 8] = _chunkvec(f("final_norm_g"))
    vec_common[:, V_BGATE:V_BGATE + 16] = _chunkvec(f("b_gate")[0].reshape(-1))
    cw = f("conv_w")[0]
    for k in range(4):
        vec_common[:, V_CONVW + 10 * k:V_CONVW + 10 * k + 10] = _chunkvec(cw[k])
    vec_common[:, V_CONVB:V_CONVB + 10] = _chunkvec(f("conv_b")[0])
    vec_common[:, V_BA:V_BA + 10] = _chunkvec(f("b_rec_a")[0])
    vec_common[:, V_BI:V_BI + 10] = _chunkvec(f("b_rec_i")[0])
    vec_common[:, V_LAM:V_LAM + 10] = _chunkvec(f("lru_lambda")[0])
    vec_common[:, V_GQ:V_GQ + 3] = _chunkvec(f("q_norm_g")[0])
    vec_common[:, V_GKV:V_GKV + 2] = _chunkvec(f("kv_norm_g")[0])
    shared = {
        "metaT": np.ascontiguousarray(f("meta_tokens").T),
        "w_in": np.ascontiguousarray(f("w_in")[0]),
        "w_rec_a": np.ascontiguousarray(f("w_rec_a")[0]),
        "w_rec_i": np.ascontiguousarray(f("w_rec_i")[0]),
        "w_uq": np.ascontiguousarray(f("w_uq")[0]),
        "w_ukv": np.ascontiguousarray(f("w_ukv")[0]),
        "w_branch": np.ascontiguousarray(f("w_branch")[0]),
        "w_out": np.ascontiguousarray(f("w_out")[0]),
        "w_ffn_in": np.ascontiguousarray(f("w_ffn_in")[0]),
        "w_ffn_out": np.ascontiguousarray(f("w_ffn_out")[0]),
        "cs_k": cs_k,
    }
    in_maps = []
    for core in range(8):
        b, c = core // 2, core % 2
        xb = x[b]
        own = xb.reshape(S // 128, 128, D)[c::2].reshape(OWN, D)
        vecs = vec_common.copy()
        vecs[:, V_SEL] = 1.0 if c == 0 else 0.0
        vecs[:, V_SEL + 1] = 0.0 if c == 0 else 1.0
        if c == 0:
            mk = np.concatenate([dmask, np.zeros_like(dmask)], 1)
        else:
            mk = np.concatenate([np.ones_like(dmask), dmask], 1)
        tiles = np.arange(c, S // 128, 2)
        posi = (NMETA + tiles[:, None] * 128 + np.arange(128)[None, :]).reshape(-1)
        cs_q = np.ascontiguousarray(np.concatenate([cos2[:, posi], sin2[:, posi]], 1))
        m = dict(shared)
        m["xT"] = np.ascontiguousarray(xb.T)
        m["xoT"] = np.ascontiguousarray(own.T)
        m["vecs"] = vecs
        m["masks"] = np.ascontiguousarray(mk)
        m["cs_q"] = cs_q
        in_maps.append(m)
    return in_maps


def assemble(results):
    out = np.empty((NB, S, D), np.float32)
    for core in range(8):
        b, c = core // 2, core % 2
        oT = np.asarray(results[core]["outT"], np.float32)
        o = oT.T.reshape(OWN // 128, 128, D)
        out[b].reshape(S // 128, 128, D)[c::2] = o
    return out


_NC_CACHE = {}


def kernel(**inputs):
    in_maps = make_in_maps(inputs)
    if "nc" not in _NC_CACHE:
        _NC_CACHE["nc"] = Builder().build()
    nc = _NC_CACHE["nc"]
    res = run_bass_kernel_spmd(nc, in_maps, core_ids=list(range(8)))
    return assemble(res.results)
```

```python
import math
from contextlib import ExitStack

import numpy as np
import concourse.bass as bass
import concourse.mybir as mybir
from concourse.bass_utils import run_bass_kernel_spmd

F32 = mybir.dt.float32
BF16 = mybir.dt.bfloat16
AF = mybir.ActivationFunctionType
ALU = mybir.AluOpType

ENGS = ("tensor", "vector", "scalar", "gpsimd", "sync")

D = 1024
S = 8192
NB = 4
NMETA = 16
LT = NMETA + S
DRNN = 1280
NCH = 10
QR = 384
KVR = 256
ROPE = 64
NH = 8
DFF = 2816
NFF = 22
EPS = 1e-6
SCALE = 1.0 / math.sqrt(192.0)
OWN = 4096
BLK = 512
C_X, C_G, C_Q, C_KV, C_KR, C_M = 0, 1280, 2560, 2944, 3200, 3264

V_GMIX, V_GFFN, V_GFIN, V_BGATE = 0, 8, 16, 24
V_CONVW, V_CONVB, V_BA, V_BI, V_LAM = 40, 80, 90, 100, 110
V_GQ, V_GKV, V_SEL = 120, 123, 125
NV = 128


class Counter:
    def __init__(self, sem, name):
        self.sem = sem
        self.name = name
        self.count = 0


class Buf:
    __slots__ = ("name", "w", "r")

    def __init__(self, name=""):
        self.name = name
        self.w = None
        self.r = []


class Phase:
    def __init__(self, nc, ctrs):
        self.nc = nc
        self.ctrs = ctrs
        self.ops = {e: [] for e in ENGS}
        self.seen = {e: {} for e in ENGS}

    def _waits(self, eng, toks):
        seen = self.seen[eng]
        own = self.ctrs[eng] if eng == "tensor" else None
        best = {}
        for t in toks:
            if t is None:
                continue
            c, v = t
            if c is own:
                continue
            if best.get(c, 0) < v:
                best[c] = v
        out = []
        for c, v in best.items():
            if seen.get(c, 0) >= v:
                continue
            seen[c] = v
            out.append((c, v))
        return out

    def op(self, eng, fn, reads=(), writes=(), ctr=None, extra=()):
        toks = list(extra)
        for b in reads:
            toks.append(b.w)
        for b in writes:
            toks.append(b.w)
            toks.extend(b.r)
        waits = self._waits(eng, toks)
        c = ctr if ctr is not None else self.ctrs[eng]
        step = 16 if ctr is not None else 1
        c.count += step
        tok = (c, c.count)
        self.ops[eng].append((fn, waits, (c, step)))
        for b in reads:
            b.r.append(tok)
        for b in writes:
            b.w = tok
            b.r = []
        return tok

    def group(self, eng, fns, reads=(), writes=(), extra=()):
        n = len(fns)
        if n == 1:
            return self.op(eng, fns[0], reads, writes, extra=extra)
        toks = list(extra)
        for b in reads:
            toks.append(b.w)
        for b in writes:
            toks.append(b.w)
            toks.extend(b.r)
        self.ops[eng].append((fns[0], self._waits(eng, toks), None))
        for fn in fns[1:-1]:
            self.ops[eng].append((fn, [], None))
        c = self.ctrs[eng]
        c.count += 1
        tok = (c, c.count)
        self.ops[eng].append((fns[-1], [], (c, 1)))
        for b in reads:
            b.r.append(tok)
        for b in writes:
            b.w = tok
            b.r = []
        return tok

    def dma(self, eng, out, in_, ctr, reads=(), writes=(), extra=()):
        return self.op(eng, lambda e: e.dma_start(out=out, in_=in_), reads, writes, ctr=ctr, extra=extra)

    def emit(self, final_waits=()):
        nc = self.nc
        ops = self.ops
        fw = list(final_waits)
        with nc.Block() as block:
            def run(e, name):
                for fn, waits, inc in ops[name]:
                    for c, v in waits:
                        e.wait_ge(c.sem, v)
                    ins = fn(e)
                    if inc is not None:
                        ins.then_inc(inc[0].sem, inc[1])
                if name == "sync":
                    for c, v in fw:
                        e.wait_ge(c.sem, v)

            @block.tensor
            def _(e):
                run(e, "tensor")

            @block.vector
            def _(e):
                run(e, "vector")

            @block.scalar
            def _(e):
                run(e, "scalar")

            @block.gpsimd
            def _(e):
                run(e, "gpsimd")

            @block.sync
            def _(e):
                run(e, "sync")
        self.ops = {e: [] for e in ENGS}
        self.seen = {e: {} for e in ENGS}


class Builder:
    def __init__(self, debug=False, phases=(1, 2, 3, 4), nheads=None):
        self.debug = debug
        self.nheads = nheads
        self.phases = phases
        self.nc = bass.Bass("TRN2", target_bir_lowering=False)
        self.es = ExitStack()
        self.ndma = 0

    def din(self, name, shape, dt=F32):
        return self.nc.dram_tensor(name, list(shape), dt, kind="ExternalInput").ap()

    def dscratch(self, name, shape, dt):
        if self.debug:
            return self.nc.dram_tensor(name, list(shape), dt, kind="ExternalOutput").ap()
        return self.nc.dram_tensor(name, list(shape), dt).ap()

    def dctr(self):
        self.ndma += 1
        return Counter(self.es.enter_context(self.nc.semaphore("dq%d" % self.ndma)), "dq%d" % self.ndma)

    def sb(self, st, name, shape, dt):
        return st.enter_context(self.nc.sbuf_tensor(name, list(shape), dt))

    def mm(self, out, pairs, wbuf, rbufs, flags=None):
        n = len(pairs)
        fns = []
        for i, (l, r) in enumerate(pairs):
            st = (i == 0) if flags is None else flags[0]
            sp = (i == n - 1) if flags is None else flags[1]
            fns.append(self._mmfn(out, l, r, st, sp))
        return self.ph.group("tensor", fns, reads=rbufs, writes=[wbuf])

    @staticmethod
    def _mmfn(out, l, r, st, sp):
        return lambda e: e.matmul(out, l, r, start=st, stop=sp, skip_group_check=True)

    def act(self, out, in_, func, reads, writes, **kw):
        return self.ph.op("scalar", lambda e: e.activation(out=out, in_=in_, func=func, **kw), reads, writes)

    def tt(self, eng, out, in0, in1, op, reads, writes):
        return self.ph.op(eng, lambda e: e.tensor_tensor(out=out, in0=in0, in1=in1, op=op), reads, writes)

    def ts(self, eng, out, in0, s1, op0, reads, writes, s2=None, op1=None):
        if op1 is None:
            return self.ph.op(eng, lambda e: e.tensor_scalar(out=out, in0=in0, scalar1=s1, scalar2=None, op0=op0), reads, writes)
        return self.ph.op(eng, lambda e: e.tensor_scalar(out=out, in0=in0, scalar1=s1, scalar2=s2, op0=op0, op1=op1), reads, writes)

    def stt(self, out, in0, scalar, in1, op0, op1, reads, writes):
        return self.ph.op("vector", lambda e: e.scalar_tensor_tensor(out=out, in0=in0, scalar=scalar, in1=in1, op0=op0, op1=op1), reads, writes)

    def cp(self, eng, out, in_, reads, writes):
        return self.ph.op(eng, lambda e: e.tensor_copy(out=out, in_=in_), reads, writes)

    def recip(self, out, in_, reads, writes):
        return self.ph.op("vector", lambda e: e.reciprocal(out=out, in_=in_), reads, writes)

    def memset(self, eng, ap, val, writes):
        return self.ph.op(eng, lambda e: e.memset(ap, val), (), writes)

    def load(self, out, in_, wbuf, eng="sync", first=True):
        ctr = self.dctr_for(wbuf)
        if first:
            tok = self.ph.dma(eng, out, in_, ctr, writes=[wbuf])
        else:
            tok = self.ph.dma(eng, out, in_, ctr)
            wbuf.w = tok
        return tok

    def store(self, out, in_, srcbuf, eng="sync"):
        tok = self.ph.dma(eng, out, in_, self.dctr_for(srcbuf), reads=[srcbuf])
        self.pending.append(tok)
        return tok

    def end_phase(self):
        best = {}
        for c, v in self.pending:
            if best.get(c, 0) < v:
                best[c] = v
        self.pending = []
        self.ph.emit(final_waits=[(c, v) for c, v in best.items()])

    def dctr_for(self, buf):
        c = self._bufctr.get(id(buf))
        if c is None:
            c = self.dctr()
            self._bufctr[id(buf)] = c
        return c

    def bank(self, hold=False):
        nb = len(self._banks)
        for _ in range(nb):
            i = self._bank_i
            self._bank_i = (i + 1) % nb
            if id(self._bank_bufs[i]) not in self._held:
                if hold:
                    self._held.add(id(self._bank_bufs[i]))
                return self._banks[i], self._bank_bufs[i]
        raise RuntimeError("all PSUM banks held")

    def rel(self, pb):
        self._held.discard(id(pb))

    def rms8(self, xin, xbuf, N, gcol, sqb, sqbuf, sd, sdbuf, rstd, rsbuf, zT, zbuf, out_f32=None):
        vec = self.vec
        self.act(sqb[:, 0:8 * N], xin[:, 0:8 * N], AF.Square, [xbuf], [sqbuf])
        ps, pb = self.bank()
        self.mm(ps[:, 0:N], [(self.ones[:, :], sqb[:, c * N:(c + 1) * N]) for c in range(8)], pb, [sqbuf, self.cbuf])
        self.act(sd[:, 0:N], ps[:, 0:N], AF.Sqrt, [pb], [sdbuf], scale=1.0 / D, bias=self.epsc[:, 0:1])
        self.recip(rstd[:, 0:N], sd[:, 0:N], [sdbuf], [rsbuf])
        for c in range(8):
            self.stt(zT[:, c * N:(c + 1) * N], xin[:, c * N:(c + 1) * N], vec[:, gcol + c:gcol + c + 1], rstd[:, 0:N],
                     ALU.mult, ALU.mult, [xbuf, rsbuf, self.cbuf], [zbuf])

    def build(self):
        nc = self.nc
        es = self.es
        self._bufctr = {}
        self.pending = []
        xT = self.din("xT", [D, S])
        xoT = self.din("xoT", [D, OWN])
        metaT = self.din("metaT", [D, NMETA])
        w_in = self.din("w_in", [D, 5312])
        w_ra = self.din("w_rec_a", [NCH, 128, 128])
        w_ri = self.din("w_rec_i", [NCH, 128, 128])
        w_uq = self.din("w_uq", [QR, 1536])
        w_ukv = self.din("w_ukv", [KVR, 2048])
        w_br = self.din("w_branch", [2304, D])
        w_o = self.din("w_out", [D, D])
        w_f1 = self.din("w_ffn_in", [D, 2 * DFF])
        w_f2 = self.din("w_ffn_out", [DFF, D])
        vecs = self.din("vecs", [128, NV])
        masks = self.din("masks", [128, 256])
        cs_k = self.din("cs_k", [64, 2 * LT])
        cs_q = self.din("cs_q", [64, 2 * OWN])
        outT = nc.dram_tensor("outT", [D, OWN], F32, kind="ExternalOutput").ap()
        ckv_s = self.dscratch("ckv_s", [KVR, LT], BF16)
        kr_s = self.dscratch("kr_s", [ROPE, LT], BF16)
        yrnn_s = self.dscratch("yrnn_s", [DRNN, OWN], BF16)
        yatt_s = self.dscratch("yatt_s", [D, OWN], BF16)
        hmid_s = self.dscratch("hmid_s", [D, OWN], F32)
        self.sc_bufs = {k: Buf(k) for k in ["ckv", "kr", "yrnn", "yatt", "hmid", "out"]}

        self.ctrs = {e: Counter(es.enter_context(nc.semaphore("c_" + e)), e) for e in ENGS}
        self.ph = Phase(nc, self.ctrs)
        self._banks = [es.enter_context(nc.psum_tensor("pb%d" % i, [128, 512], F32)) for i in range(8)]
        self._bank_bufs = [Buf("pb%d" % i) for i in range(8)]
        self._bank_i = 0
        self._held = set()
        self.vec = self.sb(es, "vec", [128, NV], F32)
        self.vec2 = self.sb(es, "vec2", [128, 64], F32)
        self.ones = self.sb(es, "ones", [128, 128], BF16)
        self.epsc = self.sb(es, "epsc", [128, 1], F32)
        self.cbuf = Buf("consts")
        vec, vec2 = self.vec, self.vec2
        ph = self.ph
        self.load(vec[:, :], vecs, self.cbuf)
        self.memset("gpsimd", self.ones[:, :], 1.0, [self.cbuf])
        self.memset("gpsimd", self.epsc[:, :], EPS, [self.cbuf])
        self.ts("vector", vec2[:, 0:10], vec[:, V_BA:V_BA + 10], 0.5, ALU.mult, [self.cbuf], [self.cbuf])
        self.ts("vector", vec2[:, 10:20], vec[:, V_BI:V_BI + 10], 0.5, ALU.mult, [self.cbuf], [self.cbuf])
        self.act(vec2[:, 40:50], vec[:, V_LAM:V_LAM + 10], AF.Exp, [self.cbuf], [self.cbuf], scale=-1.0)
        self.act(vec2[:, 50:60], vec2[:, 40:50], AF.Ln, [self.cbuf], [self.cbuf], bias=1.0)
        self.ts("vector", vec2[:, 20:30], vec2[:, 50:60], -4.0, ALU.mult, [self.cbuf], [self.cbuf])
        self.ts("vector", vec2[:, 30:40], vec2[:, 50:60], -8.0, ALU.mult, [self.cbuf], [self.cbuf])

        if 1 in self.phases:
            self.phase1(xT, metaT, w_in, w_ra, w_ri, cs_k, ckv_s, kr_s, yrnn_s)
        if 2 in self.phases:
            self.phase2(xoT, w_in, w_uq, w_ukv, masks, cs_q, ckv_s, kr_s, yatt_s)
        if 3 in self.phases:
            self.phase3(xoT, w_in, w_br, w_o, yrnn_s, yatt_s, hmid_s)
        if 4 in self.phases:
            self.phase4(w_f1, w_f2, hmid_s, outT)
        else:
            with ExitStack() as st:
                z = self.sb(st, "zz", [128, 512], F32)
                zb = Buf("zz")
                self.memset("vector", z[:, :], 0.0, [zb])
                ov = outT.rearrange("(c p) t -> p c t", p=128)
                for c in range(8):
                    for t in range(OWN // 512):
                        self.store(ov[:, c, t * 512:(t + 1) * 512], z[:, :], zb)
                self.end_phase()
        es.close()
        return nc

    def wload(self, dst, src, buf):
        tok = self.ph.dma("gpsimd", dst, src, self.dctr_for(buf))
        buf.w = tok
        return tok

    def phase1(self, xT, metaT, w_in, w_ra, w_ri, cs_k, ckv_s, kr_s, yrnn_s):
        nc, ph, vec, vec2 = self.nc, self.ph, self.vec, self.vec2
        G = 5
        with ExitStack() as st:
            Wx = self.sb(st, "Wx", [128, 8 * DRNN], BF16)
            Wg = self.sb(st, "Wg", [128, 8 * DRNN], BF16)
            Wkv = self.sb(st, "Wkv", [128, 8 * KVR], BF16)
            Wkr = self.sb(st, "Wkr", [128, 8 * 128], BF16)
            Wa = self.sb(st, "Wa", [128, NCH * 128], BF16)
            Wi = self.sb(st, "Wi", [128, NCH * 128], BF16)
            wb = Buf("w1")
            w3 = w_in.rearrange("(k p) n -> p k n", p=128)
            for k in range(8):
                self.wload(Wx[:, k * DRNN:(k + 1) * DRNN], w3[:, k, C_X:C_X + DRNN], wb)
                self.wload(Wkv[:, k * KVR:(k + 1) * KVR], w3[:, k, C_KV:C_KV + KVR], wb)
                self.wload(Wkr[:, k * 128:k * 128 + 64], w3[:, k, C_KR:C_KR + 64], wb)
                self.wload(Wkr[:, k * 128 + 64:k * 128 + 96], w3[:, k, C_KR + 32:C_KR + 64], wb)
                self.wload(Wkr[:, k * 128 + 96:k * 128 + 128], w3[:, k, C_KR:C_KR + 32], wb)
            for j in range(NCH):
                self.wload(Wa[:, j * 128:(j + 1) * 128], w_ra[j], wb)
                self.wload(Wi[:, j * 128:(j + 1) * 128], w_ri[j], wb)
            for k in range(8):
                self.wload(Wg[:, k * DRNN:(k + 1) * DRNN], w3[:, k, C_G:C_G + DRNN], wb)

            N = BLK
            xin = self.sb(st, "xin", [128, 8 * N], F32); xb = Buf("xin")
            sqb = self.sb(st, "sqb", [128, 8 * N], BF16); sqbuf = Buf("sqb")
            sd = self.sb(st, "sd", [128, N], F32); sdb = Buf("sd")
            rstd = self.sb(st, "rstd", [128, N], F32); rsb = Buf("rstd")
            zT = self.sb(st, "zT", [128, 8 * N], BF16); zb = Buf("zT")
            ux = self.sb(st, "ux", [128, NCH * (N + 3)], F32); uxb = [Buf("ux%d" % j) for j in range(NCH)]
            xc = self.sb(st, "xc", [128, G * N], F32); xcb_ = [Buf("xc%d" % j) for j in range(G)]
            xcb = self.sb(st, "xcb", [128, G * N], BF16); xcbb = [Buf("xcb%d" % j) for j in range(G)]
            T1 = self.sb(st, "T1", [128, G * N], F32); T1b = [Buf("T1_%d" % j) for j in range(G)]
            T2 = self.sb(st, "T2", [128, G * N], F32); T2b = [Buf("T2_%d" % j) for j in range(G)]
            T3 = self.sb(st, "T3", [128, G * N], F32); T3b = [Buf("T3_%d" % j) for j in range(G)]
            T4 = self.sb(st, "T4", [128, G * N], F32); T4b = [Buf("T4_%d" % j) for j in range(G)]
            YT = self.sb(st, "YT", [128, G * (N // 2)], F32); YTb = [Buf("YT%d" % j) for j in range(G)]
            YU = self.sb(st, "YU", [128, G * (N // 2)], F32); YUb = [Buf("YU%d" % j) for j in range(G)]
            YO = self.sb(st, "YO", [128, G * (N // 2)], BF16); YOb = Buf("YO")
            hst = self.sb(st, "hst", [128, NCH], F32); hstb = [Buf("hst%d" % j) for j in range(NCH)]
            cs = self.sb(st, "cs", [64, 2 * N], F32); csb = Buf("cs")
            kvq = self.sb(st, "kvq", [128, 2 * N], BF16); kvqb = Buf("kvq")
            sd2 = self.sb(st, "sd2", [128, N], F32); sd2b = Buf("sd2")
            rs2 = self.sb(st, "rs2", [128, N], F32); rs2b = Buf("rs2")
            kvo = self.sb(st, "kvo", [128, 2 * N], BF16); kvob = Buf("kvo")
            kt1 = self.sb(st, "kt1", [64, N], F32); kt1b = Buf("kt1")
            kt2 = self.sb(st, "kt2", [64, N], F32); kt2b = Buf("kt2")
            kro = self.sb(st, "kro", [64, N], BF16); krob = Buf("kro")

            self.memset("gpsimd", ux[:, :], 0.0, uxb)
            self.memset("gpsimd", hst[:, :], 0.0, hstb)

            x3 = xT.rearrange("(c p) t -> p c t", p=128)
            m3 = metaT.rearrange("(c p) t -> p c t", p=128)
            ckv3 = ckv_s.rearrange("(c p) t -> p c t", p=128)
            yr3 = yrnn_s.rearrange("(c p) t -> p c t", p=128)
            nblk = S // N

            def load_block(bi):
                if bi == 0:
                    n = NMETA
                    for c in range(8):
                        self.load(xin[:, c * n:(c + 1) * n], m3[:, c, :], xb, first=(c == 0))
                    t0 = 0
                else:
                    n = N
                    for c in range(8):
                        self.load(xin[:, c * n:(c + 1) * n], x3[:, c, (bi - 1) * N:bi * N], xb, first=(c == 0))
                    t0 = NMETA + (bi - 1) * N
                return n, t0

            def load_cs(bi):
                n_ = NMETA if bi == 0 else N
                t_ = 0 if bi == 0 else NMETA + (bi - 1) * N
                self.load(cs[:, 0:n_], cs_k[:, t_:t_ + n_], csb)
                self.load(cs[:, N:N + n_], cs_k[:, LT + t_:LT + t_ + n_], csb, first=False)

            zT2 = self.sb(st, "zTb", [128, 8 * N], BF16)
            zTs = [zT, zT2]
            zbs = [zb, Buf("zTb")]

            def pre_steps(bi):
                n = NMETA if bi == 0 else N
                t0 = 0 if bi == 0 else NMETA + (bi - 1) * N
                zT_, zb_ = zTs[bi % 2], zbs[bi % 2]
                zc = [zT_[:, k * n:(k + 1) * n] for k in range(8)]
                stt_ = {}

                def s1():
                    self.act(sqb[:, 0:8 * n], xin[:, 0:8 * n], AF.Square, [xb], [sqbuf])
                    ps, pb = self.bank(hold=True)
                    self.mm(ps[:, 0:n], [(self.ones[:, :], sqb[:, c * n:(c + 1) * n]) for c in range(8)], pb, [sqbuf, self.cbuf])
                    stt_["ss"] = (ps, pb)

                def s2():
                    ps, pb = stt_["ss"]
                    self.act(sd[:, 0:n], ps[:, 0:n], AF.Sqrt, [pb], [sdb], scale=1.0 / D, bias=self.epsc[:, 0:1])
                    self.rel(pb)
                    self.recip(rstd[:, 0:n], sd[:, 0:n], [sdb], [rsb])

                def s3():
                    for c in range(8):
                        self.stt(zT_[:, c * n:(c + 1) * n], xin[:, c * n:(c + 1) * n], vec[:, V_GMIX + c:V_GMIX + c + 1], rstd[:, 0:n],
                                 ALU.mult, ALU.mult, [xb, rsb, self.cbuf], [zb_])
                    if bi < nblk:
                        load_block(bi + 1)

                def s4():
                    pk = []
                    for c2 in range(2):
                        ps, pb = self.bank(hold=True)
                        self.mm(ps[:, 0:n], [(Wkv[:, k * KVR + c2 * 128:k * KVR + (c2 + 1) * 128], zc[k]) for k in range(8)], pb, [zb_, wb])
                        pk.append((ps, pb))
                    stt_["pk"] = pk
                    for c2 in range(2):
                        self.act(kvq[:, c2 * n:(c2 + 1) * n], pk[c2][0][:, 0:n], AF.Square, [pk[c2][1]], [kvqb])
                    ps, pb = self.bank(hold=True)
                    self.mm(ps[:, 0:n], [(self.ones[:, :], kvq[:, c2 * n:(c2 + 1) * n]) for c2 in range(2)], pb, [kvqb, self.cbuf])
                    stt_["ss2"] = (ps, pb)

                def s5():
                    ps, pb = stt_["ss2"]
                    self.act(sd2[:, 0:n], ps[:, 0:n], AF.Sqrt, [pb], [sd2b], scale=1.0 / KVR, bias=self.epsc[:, 0:1])
                    self.rel(pb)
                    self.recip(rs2[:, 0:n], sd2[:, 0:n], [sd2b], [rs2b])

                def s6():
                    pk = stt_["pk"]
                    for c2 in range(2):
                        self.stt(kvo[:, c2 * n:(c2 + 1) * n], pk[c2][0][:, 0:n], vec[:, V_GKV + c2:V_GKV + c2 + 1], rs2[:, 0:n],
                                 ALU.mult, ALU.mult, [pk[c2][1], rs2b, self.cbuf], [kvob])
                        self.rel(pk[c2][1])
                        self.store(ckv3[:, c2, t0:t0 + n], kvo[:, c2 * n:(c2 + 1) * n], kvob)

                def s7():
                    pr = []
                    for hh in range(2):
                        ps, pb = self.bank()
                        self.mm(ps[0:64, 0:n], [(Wkr[:, k * 128 + hh * 64:k * 128 + (hh + 1) * 64], zc[k]) for k in range(8)], pb, [zb_, wb])
                        pr.append((ps, pb))
                    self.tt("vector", kt1[:, 0:n], pr[0][0][0:64, 0:n], cs[:, 0:n], ALU.mult, [pr[0][1], csb], [kt1b])
                    self.tt("vector", kt2[:, 0:n], pr[1][0][0:64, 0:n], cs[:, N:N + n], ALU.mult, [pr[1][1], csb], [kt2b])
                    self.tt("gpsimd", kro[:, 0:n], kt1[:, 0:n], kt2[:, 0:n], ALU.add, [kt1b, kt2b], [krob])
                    self.store(kr_s[:, t0:t0 + n], kro[:, 0:n], krob)
                    if bi < nblk:
                        load_cs(bi + 1)
                return [s1, s2, s3, s4, s5, s6, s7]

            grp_state = {}

            def gctx(bi, g0):
                n = NMETA if bi == 0 else N
                zT_, zb_ = zTs[bi % 2], zbs[bi % 2]
                zc = [zT_[:, k * n:(k + 1) * n] for k in range(8)]
                st_ = grp_state.setdefault((bi, g0), {"pu": {}, "pg": {}, "pa": {}, "pi": {}})
                return n, zb_, zc, list(range(g0, g0 + G)), st_

            def front1(bi, g0):
                n, zb_, zc, chs, st_ = gctx(bi, g0)
                pu = st_["pu"]
                W = N + 3

                def issue_ux(j):
                    ps, pb = self.bank(hold=True)
                    self.mm(ps[:, 0:n], [(Wx[:, k * DRNN + j * 128:k * DRNN + (j + 1) * 128], zc[k]) for k in range(8)], pb, [zb_, wb])
                    pu[j] = (ps, pb)

                def cast(jj):
                    self.act(xcb[:, jj * N:jj * N + n], xc[:, jj * N:jj * N + n], AF.Identity, [xcb_[jj]], [xcbb[jj]])

                def chunk(jj, j):
                    def f():
                        if jj == 0:
                            issue_ux(chs[0])
                        if jj + 1 < G:
                            issue_ux(chs[jj + 1])
                        ps, pb = pu[j]
                        o = j * W
                        self.act(ux[:, o + 3:o + 3 + n], ps[:, 0:n], AF.Identity, [pb], [uxb[j]])
                        self.rel(pb)
                        xcj = xc[:, jj * N:jj * N + n]
                        self.act(xcj, ux[:, o:o + n], AF.Identity, [uxb[j], self.cbuf], [xcb_[jj]],
                                 scale=vec[:, V_CONVW + j:V_CONVW + j + 1], bias=vec[:, V_CONVB + j:V_CONVB + j + 1])
                        if jj >= 1:
                            cast(jj - 1)
                        for kk in range(1, 4):
                            self.stt(xcj, ux[:, o + kk:o + kk + n], vec[:, V_CONVW + kk * 10 + j:V_CONVW + kk * 10 + j + 1], xcj,
                                     ALU.mult, ALU.add, [uxb[j], self.cbuf], [xcb_[jj]])
                        self.cp("gpsimd", ux[:, o:o + 3], ux[:, o + n:o + n + 3], [], [uxb[j]])
                        if jj == G - 1:
                            cast(jj)
                    return f
                return [chunk(jj, j) for jj, j in enumerate(chs)]

            def front2(bi, g0, steps=()):
                n, zb_, zc, chs, st_ = gctx(bi, g0)
                pa, pi, pg = st_["pa"], st_["pi"], st_["pg"]
                own = bi > 0
                for f in steps:
                    f()

                def issue_gates(jj, j):
                    pa_, pab = self.bank(hold=True)
                    self.mm(pa_[:, 0:n], [(Wa[:, j * 128:(j + 1) * 128], xcb[:, jj * N:jj * N + n])], pab, [xcbb[jj], wb])
                    pi_, pib = self.bank(hold=True)
                    self.mm(pi_[:, 0:n], [(Wi[:, j * 128:(j + 1) * 128], xcb[:, jj * N:jj * N + n])], pib, [xcbb[jj], wb])
                    pa[j] = (pa_, pab)
                    pi[j] = (pi_, pib)
                issue_gates(0, chs[0])
                for jj, j in enumerate(chs):
                    if jj + 1 < G:
                        issue_gates(jj + 1, chs[jj + 1])
                    sl = slice(jj * N, jj * N + n)
                    self.act(T1[:, sl], pa[j][0][:, 0:n], AF.Tanh, [pa[j][1], self.cbuf], [T1b[jj]], scale=0.5, bias=vec2[:, j:j + 1])
                    self.act(T3[:, sl], pi[j][0][:, 0:n], AF.Tanh, [pi[j][1], self.cbuf], [T3b[jj]], scale=0.5, bias=vec2[:, 10 + j:11 + j])
                    self.rel(pa[j][1])
                    self.rel(pi[j][1])
                if own:
                    for j in chs:
                        ps, pb = self.bank(hold=True)
                        self.mm(ps[:, 0:n], [(Wg[:, k * DRNN + j * 128:k * DRNN + (j + 1) * 128], zc[k]) for k in range(8)], pb, [zb_, wb])
                        pg[j] = (ps, pb)
                for jj, j in enumerate(chs):
                    sl = slice(jj * N, jj * N + n)
                    self.stt(T3[:, sl], T3[:, sl], 1.0, xc[:, sl], ALU.add, ALU.mult, [xcb_[jj]], [T3b[jj]])

            def back(bi, g0):
                n, zb_, zc, chs, st_ = gctx(bi, g0)
                pg = st_["pg"]
                own = bi > 0
                L = []

                def mk(fn, *a):
                    return lambda: fn(*a)

                def c_(jj, j):
                    sl = slice(jj * N, jj * N + n)
                    self.act(T2[:, sl], T1[:, sl], AF.Exp, [T1b[jj], self.cbuf], [T2b[jj]], scale=vec2[:, 30 + j:31 + j], bias=vec2[:, 30 + j:31 + j])
                    self.act(T1[:, sl], T1[:, sl], AF.Exp, [self.cbuf], [T1b[jj]], scale=vec2[:, 20 + j:21 + j], bias=vec2[:, 20 + j:21 + j])

                def d_(jj, j):
                    sl = slice(jj * N, jj * N + n)
                    self.act(T2[:, sl], T2[:, sl], AF.Sqrt, [], [T2b[jj]], scale=-0.25, bias=0.25)

                def e1_(jj, j):
                    sl = slice(jj * N, jj * N + n)
                    self.tt("gpsimd", T2[:, sl], T2[:, sl], T3[:, sl], ALU.mult, [T3b[jj]], [T2b[jj]])

                def e2_(jj, j):
                    sl = slice(jj * N, jj * N + n)
                    ph.op("vector", self._scanfn(T4[:, sl], T1[:, sl], T2[:, sl], hst[:, j:j + 1]),
                          [T1b[jj], T2b[jj], hstb[j]], [T4b[jj]])

                def e3_(jj, j):
                    self.cp("gpsimd", hst[:, j:j + 1], T4[:, jj * N + n - 1:jj * N + n], [T4b[jj]], [hstb[j]])

                def f_(jj, j):
                    sl = slice(jj * N, jj * N + n)
                    self.act(T3[:, sl], pg[j][0][:, 0:n], AF.Gelu_apprx_tanh, [pg[j][1]], [T3b[jj]])
                    self.rel(pg[j][1])

                h2 = N // 2

                def h_(jj, j):
                    sl = slice(jj * N, jj * N + n)
                    gv = T3[:, sl].rearrange("p (a e t) -> p a e t", a=2, e=2)
                    hv = T4[:, sl].rearrange("p (a e t) -> p a e t", a=2, e=2)
                    ytv = YT[:, jj * h2:(jj + 1) * h2].rearrange("p (a t) -> p a t", a=2)
                    yuv = YU[:, jj * h2:(jj + 1) * h2].rearrange("p (a t) -> p a t", a=2)
                    yov = YO[:, jj * h2:(jj + 1) * h2].rearrange("p (a t) -> p a t", a=2)
                    self.stt(ytv, gv[:, :, 0, :], vec[:, V_SEL:V_SEL + 1], hv[:, :, 0, :], ALU.mult, ALU.mult,
                             [T3b[jj], T4b[jj], self.cbuf], [YTb[jj]])
                    self.stt(yuv, gv[:, :, 1, :], vec[:, V_SEL + 1:V_SEL + 2], hv[:, :, 1, :], ALU.mult, ALU.mult,
                             [T3b[jj], T4b[jj], self.cbuf], [YUb[jj]])
                    self.tt("gpsimd", yov, ytv, yuv, ALU.add, [YTb[jj], YUb[jj]], [YOb])

                def st_(jj, j):
                    ob0 = (bi - 1) * h2
                    self.store(yr3[:, j, ob0:ob0 + h2], YO[:, jj * h2:(jj + 1) * h2], YOb)
                stages = [c_, d_, e1_, e2_, e3_] + ([f_, h_, st_] if own else [])
                for fn in stages:
                    for jj, j in enumerate(chs):
                        L.append(mk(fn, jj, j))
                return L

            def merge(Lb, La):
                nb_, na_ = len(Lb), len(La)
                if na_ == 0:
                    for f in Lb:
                        f()
                    return
                every = max(1, nb_ // (na_ + 1))
                ia = 0
                for i, f in enumerate(Lb):
                    f()
                    if ia < na_ and (i + 1) % every == 0:
                        La[ia]()
                        ia += 1
                while ia < na_:
                    La[ia]()
                    ia += 1

            groups = [(bi, g0) for bi in range(nblk + 1) for g0 in (0, G)]
            load_block(0)
            load_cs(0)
            for f in pre_steps(0):
                f()
            for f in front1(*groups[0]):
                f()
            front2(*groups[0])
            pend_pre = []
            for gi_, (bi, g0) in enumerate(groups):
                Lb = back(bi, g0)
                La = []
                late = []
                if gi_ + 1 < len(groups):
                    La = front1(*groups[gi_ + 1])
                if g0 == 0 and bi < nblk:
                    ps_ = pre_steps(bi + 1)
                    La = [La[0], ps_[0], La[1], ps_[1], La[2], ps_[2]] + La[3:]
                    late = ps_[3:]
                merge(Lb, La)
                if gi_ + 1 < len(groups):
                    front2(*groups[gi_ + 1], steps=late)
                else:
                    for f in late:
                        f()
            self.end_phase()

    @staticmethod
    def _scanfn(out, d0, d1, init):
        return lambda e: e.tensor_tensor_scan(out=out, data0=d0, data1=d1, initial=init, op0=ALU.mult, op1=ALU.add)

    def phase2(self, xoT, w_in, w_uq, w_ukv, masks, cs_q, ckv_s, kr_s, yatt_s):
        nc, ph, vec = self.nc, self.ph, self.vec
        N = BLK
        NT = 65
        with ExitStack() as st:
            ckvT = self.sb(st, "ckvT", [128, 2 * LT], BF16); ckvb = Buf("ckvT")
            krT = self.sb(st, "krT", [128, LT], BF16); krb = Buf("krT")
            qnT = self.sb(st, "qnT", [128, 3 * OWN], BF16); qnb = [Buf("qn%d" % i) for i in range(OWN // N)]
            csq = self.sb(st, "csq", [64, 2 * OWN], F32); csqb = Buf("csq")
            Wuq = self.sb(st, "Wuq", [128, 3 * 1536], BF16)
            Wus = self.sb(st, "Wus", [128, 3 * 512], BF16)
            Wukv = self.sb(st, "Wukv", [128, 2 * 2048], BF16)
            Mk = self.sb(st, "Mk", [128, 256], BF16)
            wb = Buf("w2")
            ckv3 = ckv_s.rearrange("(c p) t -> p c t", p=128)
            for c2 in range(2):
                self.load(ckvT[:, c2 * LT:(c2 + 1) * LT], ckv3[:, c2, :], ckvb, first=(c2 == 0))
            self.memset("gpsimd", krT[64:128, :], 0.0, [krb])
            self.load(krT[0:64, :], kr_s, krb)
            self.load(csq[:, :], cs_q, csqb)
            with ExitStack() as st2:
                Wq = self.sb(st2, "Wq", [128, 8 * QR], BF16)
                w3 = w_in.rearrange("(k p) n -> p k n", p=128)
                for k in range(8):
                    self.wload(Wq[:, k * QR:(k + 1) * QR], w3[:, k, C_Q:C_Q + QR], wb)
                xin = self.sb(st2, "xin2", [128, 8 * N], F32); xb = Buf("xin2")
                sqb = self.sb(st2, "sqb2", [128, 8 * N], BF16); sqbuf = Buf("sqb2")
                sd = self.sb(st2, "sd_2", [128, N], F32); sdb = Buf("sd_2")
                rstd = self.sb(st2, "rstd2", [128, N], F32); rsb = Buf("rstd2")
                zT = self.sb(st2, "zT2", [128, 8 * N], BF16); zb = Buf("zT2")
                qq = self.sb(st2, "qq", [128, 3 * N], BF16); qqb = Buf("qq")
                sd3 = self.sb(st2, "sd3", [128, N], F32); sd3b = Buf("sd3")
                rs3 = self.sb(st2, "rs3", [128, N], F32); rs3b = Buf("rs3")
                xo3 = xoT.rearrange("(c p) t -> p c t", p=128)
                for c in range(8):
                    self.load(xin[:, c * N:(c + 1) * N], xo3[:, c, 0:N], xb, first=(c == 0))
                for ob in range(OWN // N):
                    self.rms8(xin, xb, N, V_GMIX, sqb, sqbuf, sd, sdb, rstd, rsb, zT, zb)
                    if ob + 1 < OWN // N:
                        for c in range(8):
                            self.load(xin[:, c * N:(c + 1) * N], xo3[:, c, (ob + 1) * N:(ob + 2) * N], xb, first=(c == 0))
                    pq = []
                    for c3 in range(3):
                        ps, pb = self.bank()
                        self.mm(ps[:, 0:N], [(Wq[:, k * QR + c3 * 128:k * QR + (c3 + 1) * 128], zT[:, k * N:(k + 1) * N]) for k in range(8)], pb, [zb, wb])
                        pq.append((ps, pb))
                    for c3 in range(3):
                        self.act(qq[:, c3 * N:(c3 + 1) * N], pq[c3][0][:, 0:N], AF.Square, [pq[c3][1]], [qqb])
                    ps, pb = self.bank()
                    self.mm(ps[:, 0:N], [(self.ones[:, :], qq[:, c3 * N:(c3 + 1) * N]) for c3 in range(3)], pb, [qqb, self.cbuf])
                    self.act(sd3[:, :], ps[:, 0:N], AF.Sqrt, [pb], [sd3b], scale=1.0 / QR, bias=self.epsc[:, 0:1])
                    self.recip(rs3[:, :], sd3[:, :], [sd3b], [rs3b])
                    for c3 in range(3):
                        self.stt(qnT[:, c3 * OWN + ob * N:c3 * OWN + (ob + 1) * N], pq[c3][0][:, 0:N], vec[:, V_GQ + c3:V_GQ + c3 + 1], rs3[:, :],
                                 ALU.mult, ALU.mult, [pq[c3][1], rs3b, self.cbuf], [qnb[ob]])
                self.end_phase()
            uq3 = w_uq.rearrange("(k p) n -> p k n", p=128)
            for k in range(3):
                self.wload(Wuq[:, k * 1536:(k + 1) * 1536], uq3[:, k, :], wb)
                src = uq3[:, k, :].rearrange("p (h d) -> p h d", d=192)
                dst = Wus[:, k * 512:(k + 1) * 512].rearrange("p (h d) -> p h d", d=64)
                self.wload(dst[:, :, 0:32], src[:, :, 160:192], wb)
                self.wload(dst[:, :, 32:64], src[:, :, 128:160], wb)
            kv3 = w_ukv.rearrange("(k p) n -> p k n", p=128)
            for k in range(2):
                self.wload(Wukv[:, k * 2048:(k + 1) * 2048], kv3[:, k, :], wb)
            self.wload(Mk[:, :], masks, wb)
            KT = [self.sb(st, "KT%d" % i, [128, LT], BF16) for i in range(2)]; KTb = [Buf("KT%d" % i) for i in range(2)]
            Vh = [self.sb(st, "Vh%d" % i, [128, NT * 128], BF16) for i in range(2)]; Vb = [Buf("Vh%d" % i) for i in range(2)]
            QT = [self.sb(st, "QT%d" % i, [128, N], BF16) for i in range(2)]; QTb = [Buf("QT%d" % i) for i in range(2)]
            QRt = [self.sb(st, "QR%d" % i, [128, N], BF16) for i in range(2)]; QRb = [Buf("QR%d" % i) for i in range(2)]
            for i in range(2):
                self.memset("gpsimd", QRt[i][64:128, :], 0.0, [QRb[i]])
            qt1 = self.sb(st, "qt1", [64, N], F32); qt1b = Buf("qt1")
            qt2 = self.sb(st, "qt2", [64, N], F32); qt2b = Buf("qt2")
            NP = 4
            Pt = [self.sb(st, "Pt%d" % i, [128, N], BF16) for i in range(NP)]; Ptb = [Buf("Pt%d" % i) for i in range(NP)]
            rden = self.sb(st, "rden", [128, N], F32); rdb = Buf("rden")
            Ob = [self.sb(st, "Ob%d" % i, [128, N], BF16) for i in range(2)]; Obb = [Buf("Ob%d" % i) for i in range(2)]
            ya3 = yatt_s.rearrange("(c p) t -> p c t", p=128)
            allbanks, allbufs = self._banks, self._bank_bufs
            self._banks, self._bank_bufs, self._bank_i = allbanks[0:4], allbufs[0:4], 0
            OTs = [(allbanks[4], allbufs[4]), (allbanks[5], allbufs[5])]
            DNs = [(allbanks[6], allbufs[6]), (allbanks[7], allbufs[7])]
            nheads = NH if self.nheads is None else self.nheads
            NG = OWN // N

            def proj_tasks(h):
                KTh, KTbh, Vhh, Vbh = KT[h % 2], KTb[h % 2], Vh[h % 2], Vb[h % 2]
                tasks = []

                def ktask(c0, n):
                    def f():
                        ps, pb = self.bank()
                        self.mm(ps[:, 0:n], [(Wukv[:, k * 2048 + h * 256:k * 2048 + h * 256 + 128], ckvT[:, k * LT + c0:k * LT + c0 + n]) for k in range(2)],
                                pb, [ckvb, wb])
                        self.cp("vector", KTh[:, c0:c0 + n], ps[:, 0:n], [pb], [KTbh])
                    return f

                def vtask0():
                    ps, pb = self.bank()
                    self.mm(ps[0:NMETA, 0:128], [(ckvT[:, k * LT:k * LT + NMETA], Wukv[:, k * 2048 + h * 256 + 128:k * 2048 + h * 256 + 256]) for k in range(2)],
                            pb, [ckvb, wb])
                    self.cp("vector", Vhh[0:NMETA, 0:128], ps[0:NMETA, 0:128], [pb], [Vbh])

                def vtask(t4):
                    def f():
                        ps, pb = self.bank()
                        for q4 in range(4):
                            c0 = NMETA + (t4 + q4) * 128
                            self.mm(ps[:, q4 * 128:(q4 + 1) * 128],
                                    [(ckvT[:, k * LT + c0:k * LT + c0 + 128], Wukv[:, k * 2048 + h * 256 + 128:k * 2048 + h * 256 + 256]) for k in range(2)],
                                    pb, [ckvb, wb])
                        self.cp("vector", Vhh[:, (1 + t4) * 128:(5 + t4) * 128], ps[:, 0:512], [pb], [Vbh])
                    return f
                tasks.append(ktask(0, NMETA))
                tasks.append(vtask0)
                for i in range(S // N):
                    tasks.append(ktask(NMETA + i * N, N))
                    tasks.append(vtask(4 * i))
                return tasks

            def qproj(h, g, qi):
                q0 = g * N
                ps, pb = self.bank()
                self.mm(ps[:, 0:N], [(Wuq[:, k * 1536 + h * 192:k * 1536 + h * 192 + 128], qnT[:, k * OWN + q0:k * OWN + q0 + N]) for k in range(3)],
                        pb, [qnb[g], wb])
                self.cp("vector", QT[qi][:, :], ps[:, 0:N], [pb], [QTb[qi]])
                psa, pba = self.bank()
                self.mm(psa[0:64, 0:N], [(Wuq[:, k * 1536 + h * 192 + 128:k * 1536 + h * 192 + 192], qnT[:, k * OWN + q0:k * OWN + q0 + N]) for k in range(3)],
                        pba, [qnb[g], wb])
                self.tt("vector", qt1[:, :], psa[0:64, 0:N], csq[:, q0:q0 + N], ALU.mult, [pba, csqb], [qt1b])
                psb, pbb = self.bank()
                self.mm(psb[0:64, 0:N], [(Wus[:, k * 512 + h * 64:k * 512 + (h + 1) * 64], qnT[:, k * OWN + q0:k * OWN + q0 + N]) for k in range(3)],
                        pbb, [qnb[g], wb])
                self.tt("vector", qt2[:, :], psb[0:64, 0:N], csq[:, OWN + q0:OWN + q0 + N], ALU.mult, [pbb, csqb], [qt2b])
                self.tt("gpsimd", QRt[qi][0:64, :], qt1[:, :], qt2[:, :], ALU.add, [qt1b, qt2b], [QRb[qi]])

            groups = [(h, g) for h in range(nheads) for g in range(NG)]
            units = []
            for gi_, (h, g) in enumerate(groups):
                nkt = 1 + 8 * g + 8
                for kt in range(nkt):
                    units.append((gi_, h, g, kt, nkt))
            for t in proj_tasks(0):
                t()
            qproj(groups[0][0], groups[0][1], 0)
            pend = []
            pend_every = 1
            DPT = 2
            nU = len(units)
            live = {}
            for step in range(nU + DPT):
                if step < nU:
                    gi_, h, g, kt, nkt = units[step]
                    qi = gi_ % 2
                    if kt == 0 and g == 0 and h + 1 < nheads:
                        pend = proj_tasks(h + 1)
                        nun = sum(1 + 8 * g2 + 8 for g2 in range(NG))
                        pend_every = max(1, (nun - 40) // len(pend))
                    if kt == nkt // 2 and gi_ + 1 < len(groups):
                        qproj(groups[gi_ + 1][0], groups[gi_ + 1][1], (gi_ + 1) % 2)
                    if kt == 0:
                        k0, nk, qs, mask = 0, NMETA, 0, None
                    else:
                        s_ = kt - 1
                        k0, nk = NMETA + s_ * 128, 128
                        d = s_ - 8 * g
                        if d < 0:
                            qs, mask = 0, None
                        else:
                            qs, mask = (d // 2) * 128, d % 2
                    ps, pb = self.bank()
                    self.mm(ps[0:nk, qs:N], [(KT[h % 2][:, k0:k0 + nk], QT[qi][:, qs:N]), (krT[:, k0:k0 + nk], QRt[qi][:, qs:N])],
                            pb, [KTb[h % 2], krb, QTb[qi], QRb[qi]])
                    P, Pb = Pt[step % NP], Ptb[step % NP]
                    self.act(P[0:nk, qs:N], ps[0:nk, qs:N], AF.Exp, [pb], [Pb], scale=SCALE)
                    if mask is not None:
                        self.tt("gpsimd", P[:, qs:qs + 128], P[:, qs:qs + 128], Mk[:, mask * 128:(mask + 1) * 128], ALU.mult, [wb], [Pb])
                    live[step] = (nk, qs)
                    if pend and kt % pend_every == 0 and g >= 0:
                        pend.pop(0)()
                if step >= DPT:
                    u = step - DPT
                    gi_, h, g, kt, nkt = units[u]
                    qi = gi_ % 2
                    nk, qs = live.pop(u)
                    P, Pb = Pt[u % NP], Ptb[u % NP]
                    OT, OTb = OTs[qi]
                    DN, DNb = DNs[qi]
                    first = kt == 0
                    last = kt == nkt - 1
                    self.mm(OT[:, qs:N], [(Vh[h % 2][0:nk, kt * 128:(kt + 1) * 128], P[0:nk, qs:N])], OTb, [Vb[h % 2], Pb], flags=(first, last))
                    self.mm(DN[:, qs:N], [(self.ones[0:nk, :], P[0:nk, qs:N])], DNb, [Pb, self.cbuf], flags=(first, last))
                    if last:
                        q0 = g * N
                        self.recip(rden[:, :], DN[:, 0:N], [DNb], [rdb])
                        self.tt("vector", Ob[qi][:, :], OT[:, 0:N], rden[:, :], ALU.mult, [OTb, rdb], [Obb[qi]])
                        self.store(ya3[:, h, q0:q0 + N], Ob[qi][:, :], Obb[qi])
                        while g == NG - 1 and pend:
                            pend.pop(0)()
            self._banks, self._bank_bufs, self._bank_i = allbanks, allbufs, 0
            self.end_phase()

    def phase3(self, xoT, w_in, w_br, w_o, yrnn_s, yatt_s, hmid_s):
        nc, ph, vec = self.nc, self.ph, self.vec
        N = BLK
        with ExitStack() as st:
            Wm = self.sb(st, "Wm", [128, 8 * 2048], BF16)
            Wb = self.sb(st, "Wb", [128, 18 * D], BF16)
            Wo = self.sb(st, "Wo", [128, 8 * D], BF16)
            wb = Buf("w3")
            w3 = w_in.rearrange("(k p) n -> p k n", p=128)
            for k in range(8):
                self.wload(Wm[:, k * 2048:(k + 1) * 2048], w3[:, k, C_M:C_M + 2048], wb)
            b3 = w_br.rearrange("(k p) n -> p k n", p=128)
            for k in range(18):
                self.wload(Wb[:, k * D:(k + 1) * D], b3[:, k, :], wb)
            o3 = w_o.rearrange("(k p) n -> p k n", p=128)
            for k in range(8):
                self.wload(Wo[:, k * D:(k + 1) * D], o3[:, k, :], wb)
            xin = [self.sb(st, "xin3_%d" % i, [128, 8 * N], F32) for i in range(2)]; xb = [Buf("xin3_%d" % i) for i in range(2)]
            sqb = self.sb(st, "sqb3", [128, 8 * N], BF16); sqbuf = Buf("sqb3")
            sd = self.sb(st, "sd_3", [128, N], F32); sdb = Buf("sd_3")
            rstd = self.sb(st, "rstd3", [128, N], F32); rsb = Buf("rstd3")
            zT = self.sb(st, "zT3", [128, 8 * N], BF16); zb = Buf("zT3")
            yr = self.sb(st, "yr", [128, NCH * N], BF16); yrb = Buf("yr")
            ya = self.sb(st, "ya", [128, 8 * N], BF16); yab = Buf("ya")
            G0 = [self.sb(st, "G0_%d" % i, [128, N], F32) for i in range(2)]; G0b = [Buf("G0_%d" % i) for i in range(2)]
            G1 = [self.sb(st, "G1_%d" % i, [128, N], F32) for i in range(2)]; G1b = [Buf("G1_%d" % i) for i in range(2)]
            mx = self.sb(st, "mx", [128, 8 * N], BF16); mxb = Buf("mx")
            xo3 = xoT.rearrange("(c p) t -> p c t", p=128)
            yr3 = yrnn_s.rearrange("(c p) t -> p c t", p=128)
            ya3 = yatt_s.rearrange("(c p) t -> p c t", p=128)
            hm3 = hmid_s.rearrange("(c p) t -> p c t", p=128)
            nb = OWN // N

            def ld(ob):
                xi = xin[ob % 2]
                for c in range(8):
                    self.load(xi[:, c * N:(c + 1) * N], xo3[:, c, ob * N:(ob + 1) * N], xb[ob % 2], first=(c == 0))
            ld(0)
            for ob in range(nb):
                xi, xbb = xin[ob % 2], xb[ob % 2]
                self.rms8(xi, xbb, N, V_GMIX, sqb, sqbuf, sd, sdb, rstd, rsb, zT, zb)
                for c in range(NCH):
                    self.load(yr[:, c * N:(c + 1) * N], yr3[:, c, ob * N:(ob + 1) * N], yrb, first=(c == 0))
                for c in range(8):
                    self.load(ya[:, c * N:(c + 1) * N], ya3[:, c, ob * N:(ob + 1) * N], yab, first=(c == 0))
                if ob + 1 < nb:
                    ld(ob + 1)
                for fc in range(8):
                    i2 = fc % 2
                    p0, p0b = self.bank()
                    self.mm(p0[:, 0:N], [(Wm[:, k * 2048 + fc * 128:k * 2048 + (fc + 1) * 128], zT[:, k * N:(k + 1) * N]) for k in range(8)], p0b, [zb, wb])
                    p1, p1b = self.bank()
                    self.mm(p1[:, 0:N], [(Wm[:, k * 2048 + D + fc * 128:k * 2048 + D + (fc + 1) * 128], zT[:, k * N:(k + 1) * N]) for k in range(8)], p1b, [zb, wb])
                    pr, prb = self.bank()
                    self.mm(pr[:, 0:N], [(Wb[:, k * D + fc * 128:k * D + (fc + 1) * 128], yr[:, k * N:(k + 1) * N]) for k in range(NCH)], prb, [yrb, wb])
                    pa, pab = self.bank()
                    self.mm(pa[:, 0:N], [(Wb[:, (NCH + k) * D + fc * 128:(NCH + k) * D + (fc + 1) * 128], ya[:, k * N:(k + 1) * N]) for k in range(8)], pab, [yab, wb])
                    self.act(G0[i2][:, :], p0[:, 0:N], AF.Sigmoid, [p0b, self.cbuf], [G0b[i2]], bias=vec[:, V_BGATE + fc:V_BGATE + fc + 1])
                    self.act(G1[i2][:, :], p1[:, 0:N], AF.Sigmoid, [p1b, self.cbuf], [G1b[i2]], bias=vec[:, V_BGATE + 8 + fc:V_BGATE + 9 + fc])
                    self.tt("vector", G0[i2][:, :], pr[:, 0:N], G0[i2][:, :], ALU.mult, [prb], [G0b[i2]])
                    self.tt("vector", G1[i2][:, :], pa[:, 0:N], G1[i2][:, :], ALU.mult, [pab], [G1b[i2]])
                    self.tt("gpsimd", mx[:, fc * N:(fc + 1) * N], G0[i2][:, :], G1[i2][:, :], ALU.add, [G0b[i2], G1b[i2]], [mxb])
                for fc in range(8):
                    ps, pb = self.bank()
                    self.mm(ps[:, 0:N], [(Wo[:, k * D + fc * 128:k * D + (fc + 1) * 128], mx[:, k * N:(k + 1) * N]) for k in range(8)], pb, [mxb, wb])
                    self.tt("vector", xi[:, fc * N:(fc + 1) * N], ps[:, 0:N], xi[:, fc * N:(fc + 1) * N], ALU.add, [pb], [xbb])
                for c in range(8):
                    self.store(hm3[:, c, ob * N:(ob + 1) * N], xi[:, c * N:(c + 1) * N], xbb)
            self.end_phase()

    def phase4(self, w_f1, w_f2, hmid_s, outT):
        nc, ph, vec = self.nc, self.ph, self.vec
        N = BLK
        with ExitStack() as st:
            W1 = self.sb(st, "W1", [128, 8 * 2 * DFF], BF16)
            W2 = self.sb(st, "W2", [128, NFF * D], BF16)
            wb = Buf("w4")
            f13 = w_f1.rearrange("(k p) n -> p k n", p=128)
            for k in range(8):
                for (a, b) in ((0, 2048), (2048, 4096), (4096, 5632)):
                    self.wload(W1[:, k * 5632 + a:k * 5632 + b], f13[:, k, a:b], wb)
            f23 = w_f2.rearrange("(k p) n -> p k n", p=128)
            for k in range(NFF):
                self.wload(W2[:, k * D:(k + 1) * D], f23[:, k, :], wb)
            hm = [self.sb(st, "hm%d" % i, [128, 8 * N], F32) for i in range(2)]; hb = [Buf("hm%d" % i) for i in range(2)]
            sd = self.sb(st, "sd_4", [128, N], F32); sdb = Buf("sd_4")
            rstd = self.sb(st, "rstd4", [128, N], F32); rsb = Buf("rstd4")
            zT = self.sb(st, "zT4", [128, 8 * N], BF16); zb = Buf("zT4")
            sg = [self.sb(st, "sg%d" % i, [128, N], F32) for i in range(2)]; sgb = [Buf("sg%d" % i) for i in range(2)]
            actb = self.sb(st, "actb", [128, NFF * N], BF16); acb = Buf("actb")
            sqb, sqbuf = actb, acb
            hm3 = hmid_s.rearrange("(c p) t -> p c t", p=128)
            ov = outT.rearrange("(c p) t -> p c t", p=128)
            nb = OWN // N

            def ld(ob):
                for c in range(8):
                    self.load(hm[ob % 2][:, c * N:(c + 1) * N], hm3[:, c, ob * N:(ob + 1) * N], hb[ob % 2], first=(c == 0))
            ld(0)
            for ob in range(nb):
                h_, hbb = hm[ob % 2], hb[ob % 2]
                self.rms8(h_, hbb, N, V_GFFN, sqb, sqbuf, sd, sdb, rstd, rsb, zT, zb)
                if ob + 1 < nb:
                    ld(ob + 1)
                for j in range(NFF):
                    i2 = j % 2
                    pg, pgb = self.bank()
                    self.mm(pg[:, 0:N], [(W1[:, k * 5632 + j * 128:k * 5632 + (j + 1) * 128], zT[:, k * N:(k + 1) * N]) for k in range(8)], pgb, [zb, wb])
                    pu, pub = self.bank()
                    self.mm(pu[:, 0:N], [(W1[:, k * 5632 + DFF + j * 128:k * 5632 + DFF + (j + 1) * 128], zT[:, k * N:(k + 1) * N]) for k in range(8)], pub, [zb, wb])
                    self.act(sg[i2][:, :], pg[:, 0:N], AF.Silu, [pgb], [sgb[i2]])
                    self.tt("vector", actb[:, j * N:(j + 1) * N], pu[:, 0:N], sg[i2][:, :], ALU.mult, [pub, sgb[i2]], [acb])
                for fc in range(8):
                    ps, pb = self.bank()
                    self.mm(ps[:, 0:N], [(W2[:, k * D + fc * 128:k * D + (fc + 1) * 128], actb[:, k * N:(k + 1) * N]) for k in range(NFF)], pb, [acb, wb])
                    self.tt("vector", h_[:, fc * N:(fc + 1) * N], ps[:, 0:N], h_[:, fc * N:(fc + 1) * N], ALU.add, [pb], [hbb])
                self.act(sqb[:, 0:8 * N], h_[:, 0:8 * N], AF.Square, [hbb], [sqbuf])
                ps, pb = self.bank()
                self.mm(ps[:, 0:N], [(self.ones[:, :], sqb[:, c * N:(c + 1) * N]) for c in range(8)], pb, [sqbuf, self.cbuf])
                self.act(sd[:, 0:N], ps[:, 0:N], AF.Sqrt, [pb], [sdb], scale=1.0 / D, bias=self.epsc[:, 0:1])
                self.recip(rstd[:, 0:N], sd[:, 0:N], [sdb], [rsb])
                for c in range(8):
                    self.stt(h_[:, c * N:(c + 1) * N], h_[:, c * N:(c + 1) * N], vec[:, V_GFIN + c:V_GFIN + c + 1], rstd[:, 0:N],
                             ALU.mult, ALU.mult, [rsb, self.cbuf], [hbb])
                for c in range(8):
                    self.store(ov[:, c, ob * N:(ob + 1) * N], h_[:, c * N:(c + 1) * N], hbb)
            self.end_phase()


def _cos_sin_tables():
    inv = (10000.0 ** (-np.arange(0, ROPE, 2, dtype=np.float32) / np.float32(ROPE))).astype(np.float32)
    pos = np.arange(LT, dtype=np.float32)
    ang = (pos[:, None] * inv[None, :]).astype(np.float32)
    cos = np.cos(ang).astype(np.float32).T
    sin = np.sin(ang).astype(np.float32).T
    cos2 = np.concatenate([cos, cos], 0)
    sin2 = np.concatenate([-sin, sin], 0)
    return np.ascontiguousarray(cos2), np.ascontiguousarray(sin2)


def _chunkvec(v):
    v = np.asarray(v, np.float32).reshape(-1)
    return v.reshape(-1, 128).T


def make_in_maps(inputs):
    f = lambda k: np.asarray(inputs[k], np.float32)
    x = f("x")
    cos2, sin2 = _cos_sin_tables()
    cs_k = np.ascontiguousarray(np.concatenate([cos2, sin2], 1))
    dchunk = np.zeros((128, 128), np.float32)
    kk = np.arange(128)[:, None] // 64
    qq = np.arange(128)[None, :] // 64
    dmask = (kk <= qq).astype(np.float32)
    vec_common = np.zeros((128, NV), np.float32)
    vec_common[:, V_GMIX:V_GMIX + 8] = _chunkvec(f("norm_mix_g")[0])
    vec_common[:, V_GFFN:V_GFFN + 8] = _chunkvec(f("norm_ffn_g")[0])
    vec_common[:, V_GFIN:V_GFIN + 8] = _chunkvec(f("final_norm_g"))
    vec_common[:, V_BGATE:V_BGATE + 16] = _chunkvec(f("b_gate")[0].reshape(-1))
    cw = f("conv_w")[0]
    for k in range(4):
        vec_common[:, V_CONVW + 10 * k:V_CONVW + 10 * k + 10] = _chunkvec(cw[k])
    vec_common[:, V_CONVB:V_CONVB + 10] = _chunkvec(f("conv_b")[0])
    vec_common[:, V_BA:V_BA + 10] = _chunkvec(f("b_rec_a")[0])
    vec_common[:, V_BI:V_BI + 10] = _chunkvec(f("b_rec_i")[0])
    vec_common[:, V_LAM:V_LAM + 10] = _chunkvec(f("lru_lambda")[0])
    vec_common[:, V_GQ:V_GQ + 3] = _chunkvec(f("q_norm_g")[0])
    vec_common[:, V_GKV:V_GKV + 2] = _chunkvec(f("kv_norm_g")[0])
    shared = {
        "metaT": np.ascontiguousarray(f("meta_tokens").T),
        "w_in": np.ascontiguousarray(f("w_in")[0]),
        "w_rec_a": np.ascontiguousarray(f("w_rec_a")[0]),
        "w_rec_i": np.ascontiguousarray(f("w_rec_i")[0]),
        "w_uq": np.ascontiguousarray(f("w_uq")[0]),
        "w_ukv": np.ascontiguousarray(f("w_ukv")[0]),
        "w_branch": np.ascontiguousarray(f("w_branch")[0]),
        "w_out": np.ascontiguousarray(f("w_out")[0]),
        "w_ffn_in": np.ascontiguousarray(f("w_ffn_in")[0]),
        "w_ffn_out": np.ascontiguousarray(f("w_ffn_out")[0]),
        "cs_k": cs_k,
    }
    in_maps = []
    for core in range(8):
        b, c = core // 2, core % 2
        xb = x[b]
        own = xb.reshape(S // 128, 128, D)[c::2].reshape(OWN, D)
        vecs = vec_common.copy()
        vecs[:, V_SEL] = 1.0 if c == 0 else 0.0
        vecs[:, V_SEL + 1] = 0.0 if c == 0 else 1.0
        if c == 0:
            mk = np.concatenate([dmask, np.zeros_like(dmask)], 1)
        else:
            mk = np.concatenate([np.ones_like(dmask), dmask], 1)
        tiles = np.arange(c, S // 128, 2)
        posi = (NMETA + tiles[:, None] * 128 + np.arange(128)[None, :]).reshape(-1)
        cs_q = np.ascontiguousarray(np.concatenate([cos2[:, posi], sin2[:, posi]], 1))
        m = dict(shared)
        m["xT"] = np.ascontiguousarray(xb.T)
        m["xoT"] = np.ascontiguousarray(own.T)
        m["vecs"] = vecs
        m["masks"] = np.ascontiguousarray(mk)
        m["cs_q"] = cs_q
        in_maps.append(m)
    return in_maps


def assemble(results):
    out = np.empty((NB, S, D), np.float32)
    for core in range(8):
        b, c = core // 2, core % 2
        oT = np.asarray(results[core]["outT"], np.float32)
        o = oT.T.reshape(OWN // 128, 128, D)
        out[b].reshape(S // 128, 128, D)[c::2] = o
    return out


_NC_CACHE = {}


def kernel(**inputs):
    in_maps = make_in_maps(inputs)
    if "nc" not in _NC_CACHE:
        _NC_CACHE["nc"] = Builder().build()
    nc = _NC_CACHE["nc"]
    res = run_bass_kernel_spmd(nc, in_maps, core_ids=list(range(8)))
    return assemble(res.results)
```

```python
import math
from contextlib import ExitStack

import numpy as np
import concourse.bass as bass
import concourse.mybir as mybir
from concourse.bass_utils import run_bass_kernel_spmd

F32 = mybir.dt.float32
BF16 = mybir.dt.bfloat16
AF = mybir.ActivationFunctionType
ALU = mybir.AluOpType

ENGS = ("tensor", "vector", "scalar", "gpsimd", "sync")

D = 1024
S = 8192
NB = 4
NMETA = 16
LT = NMETA + S
DRNN = 1280
NCH = 10
QR = 384
KVR = 256
ROPE = 64
NH = 8
DFF = 2816
NFF = 22
EPS = 1e-6
SCALE = 1.0 / math.sqrt(192.0)
OWN = 4096
BLK = 512
C_X, C_G, C_Q, C_KV, C_KR, C_M = 0, 1280, 2560, 2944, 3200, 3264

V_GMIX, V_GFFN, V_GFIN, V_BGATE = 0, 8, 16, 24
V_CONVW, V_CONVB, V_BA, V_BI, V_LAM = 40, 80, 90, 100, 110
V_GQ, V_GKV, V_SEL = 120, 123, 125
NV = 128


class Counter:
    def __init__(self, sem, name):
        self.sem = sem
        self.name = name
        self.count = 0


class Buf:
    __slots__ = ("name", "w", "r")

    def __init__(self, name=""):
        self.name = name
        self.w = None
        self.r = []


class Phase:
    def __init__(self, nc, ctrs):
        self.nc = nc
        self.ctrs = ctrs
        self.ops = {e: [] for e in ENGS}
        self.seen = {e: {} for e in ENGS}

    def _waits(self, eng, toks):
        seen = self.seen[eng]
        own = self.ctrs[eng] if eng == "tensor" else None
        best = {}
        for t in toks:
            if t is None:
                continue
            c, v = t
            if c is own:
                continue
            if best.get(c, 0) < v:
                best[c] = v
        out = []
        for c, v in best.items():
            if seen.get(c, 0) >= v:
                continue
            seen[c] = v
            out.append((c, v))
        return out

    def op(self, eng, fn, reads=(), writes=(), ctr=None, extra=()):
        toks = list(extra)
        for b in reads:
            toks.append(b.w)
        for b in writes:
            toks.append(b.w)
            toks.extend(b.r)
        waits = self._waits(eng, toks)
        c = ctr if ctr is not None else self.ctrs[eng]
        step = 16 if ctr is not None else 1
        c.count += step
        tok = (c, c.count)
        self.ops[eng].append((fn, waits, (c, step)))
        for b in reads:
            b.r.append(tok)
        for b in writes:
            b.w = tok
            b.r = []
        return tok

    def group(self, eng, fns, reads=(), writes=(), extra=()):
        n = len(fns)
        if n == 1:
            return self.op(eng, fns[0], reads, writes, extra=extra)
        toks = list(extra)
        for b in reads:
            toks.append(b.w)
        for b in writes:
            toks.append(b.w)
            toks.extend(b.r)
        self.ops[eng].append((fns[0], self._waits(eng, toks), None))
        for fn in fns[1:-1]:
            self.ops[eng].append((fn, [], None))
        c = self.ctrs[eng]
        c.count += 1
        tok = (c, c.count)
        self.ops[eng].append((fns[-1], [], (c, 1)))
        for b in reads:
            b.r.append(tok)
        for b in writes:
            b.w = tok
            b.r = []
        return tok

    def dma(self, eng, out, in_, ctr, reads=(), writes=(), extra=()):
        return self.op(eng, lambda e: e.dma_start(out=out, in_=in_), reads, writes, ctr=ctr, extra=extra)

    def emit(self, final_waits=()):
        nc = self.nc
        ops = self.ops
        fw = list(final_waits)
        with nc.Block() as block:
            def run(e, name):
                for fn, waits, inc in ops[name]:
                    for c, v in waits:
                        e.wait_ge(c.sem, v)
                    ins = fn(e)
                    if inc is not None:
                        ins.then_inc(inc[0].sem, inc[1])
                if name == "sync":
                    for c, v in fw:
                        e.wait_ge(c.sem, v)

            @block.tensor
            def _(e):
                run(e, "tensor")

            @block.vector
            def _(e):
                run(e, "vector")

            @block.scalar
            def _(e):
                run(e, "scalar")

            @block.gpsimd
            def _(e):
                run(e, "gpsimd")

            @block.sync
            def _(e):
                run(e, "sync")
        self.ops = {e: [] for e in ENGS}
        self.seen = {e: {} for e in ENGS}


class Builder:
    def __init__(self, debug=False, phases=(1, 2, 3, 4), nheads=None):
        self.debug = debug
        self.nheads = nheads
        self.phases = phases
        self.nc = bass.Bass("TRN2", target_bir_lowering=False)
        self.es = ExitStack()
        self.ndma = 0

    def din(self, name, shape, dt=F32):
        return self.nc.dram_tensor(name, list(shape), dt, kind="ExternalInput").ap()

    def dscratch(self, name, shape, dt):
        if self.debug:
            return self.nc.dram_tensor(name, list(shape), dt, kind="ExternalOutput").ap()
        return self.nc.dram_tensor(name, list(shape), dt).ap()

    def dctr(self):
        self.ndma += 1
        return Counter(self.es.enter_context(self.nc.semaphore("dq%d" % self.ndma)), "dq%d" % self.ndma)

    def sb(self, st, name, shape, dt):
        return st.enter_context(self.nc.sbuf_tensor(name, list(shape), dt))

    def mm(self, out, pairs, wbuf, rbufs, flags=None):
        n = len(pairs)
        fns = []
        for i, (l, r) in enumerate(pairs):
            st = (i == 0) if flags is None else flags[0]
            sp = (i == n - 1) if flags is None else flags[1]
            fns.append(self._mmfn(out, l, r, st, sp))
        return self.ph.group("tensor", fns, reads=rbufs, writes=[wbuf])

    @staticmethod
    def _mmfn(out, l, r, st, sp):
        return lambda e: e.matmul(out, l, r, start=st, stop=sp, skip_group_check=True)

    def act(self, out, in_, func, reads, writes, **kw):
        return self.ph.op("scalar", lambda e: e.activation(out=out, in_=in_, func=func, **kw), reads, writes)

    def tt(self, eng, out, in0, in1, op, reads, writes):
        return self.ph.op(eng, lambda e: e.tensor_tensor(out=out, in0=in0, in1=in1, op=op), reads, writes)

    def ts(self, eng, out, in0, s1, op0, reads, writes, s2=None, op1=None):
        if op1 is None:
            return self.ph.op(eng, lambda e: e.tensor_scalar(out=out, in0=in0, scalar1=s1, scalar2=None, op0=op0), reads, writes)
        return self.ph.op(eng, lambda e: e.tensor_scalar(out=out, in0=in0, scalar1=s1, scalar2=s2, op0=op0, op1=op1), reads, writes)

    def stt(self, out, in0, scalar, in1, op0, op1, reads, writes):
        return self.ph.op("vector", lambda e: e.scalar_tensor_tensor(out=out, in0=in0, scalar=scalar, in1=in1, op0=op0, op1=op1), reads, writes)

    def cp(self, eng, out, in_, reads, writes):
        return self.ph.op(eng, lambda e: e.tensor_copy(out=out, in_=in_), reads, writes)

    def recip(self, out, in_, reads, writes):
        return self.ph.op("vector", lambda e: e.reciprocal(out=out, in_=in_), reads, writes)

    def memset(self, eng, ap, val, writes):
        return self.ph.op(eng, lambda e: e.memset(ap, val), (), writes)

    def load(self, out, in_, wbuf, eng="sync", first=True):
        ctr = self.dctr_for(wbuf)
        if first:
            tok = self.ph.dma(eng, out, in_, ctr, writes=[wbuf])
        else:
            tok = self.ph.dma(eng, out, in_, ctr)
            wbuf.w = tok
        return tok

    def store(self, out, in_, srcbuf, eng="sync"):
        tok = self.ph.dma(eng, out, in_, self.dctr_for(srcbuf), reads=[srcbuf])
        self.pending.append(tok)
        return tok

    def end_phase(self):
        best = {}
        for c, v in self.pending:
            if best.get(c, 0) < v:
                best[c] = v
        self.pending = []
        self.ph.emit(final_waits=[(c, v) for c, v in best.items()])

    def dctr_for(self, buf):
        c = self._bufctr.get(id(buf))
        if c is None:
            c = self.dctr()
            self._bufctr[id(buf)] = c
        return c

    def bank(self, hold=False):
        nb = len(self._banks)
        for _ in range(nb):
            i = self._bank_i
            self._bank_i = (i + 1) % nb
            if id(self._bank_bufs[i]) not in self._held:
                if hold:
                    self._held.add(id(self._bank_bufs[i]))
                return self._banks[i], self._bank_bufs[i]
        raise RuntimeError("all PSUM banks held")

    def rel(self, pb):
        self._held.discard(id(pb))

    def rms8(self, xin, xbuf, N, gcol, sqb, sqbuf, sd, sdbuf, rstd, rsbuf, zT, zbuf, out_f32=None):
        vec = self.vec
        self.act(sqb[:, 0:8 * N], xin[:, 0:8 * N], AF.Square, [xbuf], [sqbuf])
        ps, pb = self.bank()
        self.mm(ps[:, 0:N], [(self.ones[:, :], sqb[:, c * N:(c + 1) * N]) for c in range(8)], pb, [sqbuf, self.cbuf])
        self.act(sd[:, 0:N], ps[:, 0:N], AF.Sqrt, [pb], [sdbuf], scale=1.0 / D, bias=self.epsc[:, 0:1])
        self.recip(rstd[:, 0:N], sd[:, 0:N], [sdbuf], [rsbuf])
        for c in range(8):
            self.stt(zT[:, c * N:(c + 1) * N], xin[:, c * N:(c + 1) * N], vec[:, gcol + c:gcol + c + 1], rstd[:, 0:N],
                     ALU.mult, ALU.mult, [xbuf, rsbuf, self.cbuf], [zbuf])

    def rms8p(self, xin, xbuf, N, gcol, sqs, sqsb, sd, sdbuf, rstd, rsbuf, zT, zbuf):
        vec = self.vec
        ps, pb = self.bank(hold=True)
        for c in range(8):
            self.act(sqs[c % 2][:, 0:N], xin[:, c * N:(c + 1) * N], AF.Square, [xbuf], [sqsb[c % 2]])
            self.mm(ps[:, 0:N], [(self.ones[:, :], sqs[c % 2][:, 0:N])], pb, [sqsb[c % 2], self.cbuf], flags=(c == 0, c == 7))
        self.act(sd[:, 0:N], ps[:, 0:N], AF.Sqrt, [pb], [sdbuf], scale=1.0 / D, bias=self.epsc[:, 0:1])
        self.rel(pb)
        self.recip(rstd[:, 0:N], sd[:, 0:N], [sdbuf], [rsbuf])
        for c in range(8):
            self.stt(zT[:, c * N:(c + 1) * N], xin[:, c * N:(c + 1) * N], vec[:, gcol + c:gcol + c + 1], rstd[:, 0:N],
                     ALU.mult, ALU.mult, [xbuf, rsbuf, self.cbuf], [zbuf])

    def build(self):
        nc = self.nc
        es = self.es
        self._bufctr = {}
        self.pending = []
        xT = self.din("xT", [D, S])
        xoT = self.din("xoT", [D, OWN])
        metaT = self.din("metaT", [D, NMETA])
        w_in = self.din("w_in", [D, 5312])
        w_ra = self.din("w_rec_a", [NCH, 128, 128])
        w_ri = self.din("w_rec_i", [NCH, 128, 128])
        w_uq = self.din("w_uq", [QR, 1536])
        w_ukv = self.din("w_ukv", [KVR, 2048])
        w_br = self.din("w_branch", [2304, D])
        w_o = self.din("w_out", [D, D])
        w_f1 = self.din("w_ffn_in", [D, 2 * DFF])
        w_f2 = self.din("w_ffn_out", [DFF, D])
        vecs = self.din("vecs", [128, NV])
        masks = self.din("masks", [128, 256])
        cs_k = self.din("cs_k", [64, 2 * LT])
        cs_q = self.din("cs_q", [64, 2 * OWN])
        outT = nc.dram_tensor("outT", [D, OWN], F32, kind="ExternalOutput").ap()
        ckv_s = self.dscratch("ckv_s", [KVR, LT], BF16)
        kr_s = self.dscratch("kr_s", [ROPE, LT], BF16)
        yrnn_s = self.dscratch("yrnn_s", [DRNN, OWN], BF16)
        yatt_s = self.dscratch("yatt_s", [D, OWN], BF16)
        hmid_s = self.dscratch("hmid_s", [D, OWN], F32)
        self.sc_bufs = {k: Buf(k) for k in ["ckv", "kr", "yrnn", "yatt", "hmid", "out"]}

        self.ctrs = {e: Counter(es.enter_context(nc.semaphore("c_" + e)), e) for e in ENGS}
        self.ph = Phase(nc, self.ctrs)
        self._banks = [es.enter_context(nc.psum_tensor("pb%d" % i, [128, 512], F32)) for i in range(8)]
        self._bank_bufs = [Buf("pb%d" % i) for i in range(8)]
        self._bank_i = 0
        self._held = set()
        self.vec = self.sb(es, "vec", [128, NV], F32)
        self.vec2 = self.sb(es, "vec2", [128, 64], F32)
        self.ones = self.sb(es, "ones", [128, 128], BF16)
        self.epsc = self.sb(es, "epsc", [128, 1], F32)
        self.cbuf = Buf("consts")
        vec, vec2 = self.vec, self.vec2
        ph = self.ph
        self.load(vec[:, :], vecs, self.cbuf)
        self.memset("gpsimd", self.ones[:, :], 1.0, [self.cbuf])
        self.memset("gpsimd", self.epsc[:, :], EPS, [self.cbuf])
        self.ts("vector", vec2[:, 0:10], vec[:, V_BA:V_BA + 10], 0.5, ALU.mult, [self.cbuf], [self.cbuf])
        self.ts("vector", vec2[:, 10:20], vec[:, V_BI:V_BI + 10], 0.5, ALU.mult, [self.cbuf], [self.cbuf])
        self.act(vec2[:, 40:50], vec[:, V_LAM:V_LAM + 10], AF.Exp, [self.cbuf], [self.cbuf], scale=-1.0)
        self.act(vec2[:, 50:60], vec2[:, 40:50], AF.Ln, [self.cbuf], [self.cbuf], bias=1.0)
        self.ts("vector", vec2[:, 20:30], vec2[:, 50:60], -4.0, ALU.mult, [self.cbuf], [self.cbuf])
        self.ts("vector", vec2[:, 30:40], vec2[:, 50:60], -8.0, ALU.mult, [self.cbuf], [self.cbuf])

        if 1 in self.phases:
            self.phase1(xT, metaT, w_in, w_ra, w_ri, cs_k, ckv_s, kr_s, yrnn_s)
        if 2 in self.phases:
            self.phase2(xoT, w_in, w_uq, w_ukv, masks, cs_q, ckv_s, kr_s, yatt_s)
        if 3 in self.phases:
            self.phase3(xoT, w_in, w_br, w_o, yrnn_s, yatt_s, hmid_s)
        if 4 in self.phases:
            self.phase4(w_f1, w_f2, hmid_s, outT)
        else:
            with ExitStack() as st:
                z = self.sb(st, "zz", [128, 512], F32)
                zb = Buf("zz")
                self.memset("vector", z[:, :], 0.0, [zb])
                ov = outT.rearrange("(c p) t -> p c t", p=128)
                for c in range(8):
                    for t in range(OWN // 512):
                        self.store(ov[:, c, t * 512:(t + 1) * 512], z[:, :], zb)
                self.end_phase()
        es.close()
        return nc

    def wload(self, dst, src, buf):
        tok = self.ph.dma("gpsimd", dst, src, self.dctr_for(buf))
        buf.w = tok
        return tok

    def phase1(self, xT, metaT, w_in, w_ra, w_ri, cs_k, ckv_s, kr_s, yrnn_s):
        nc, ph, vec, vec2 = self.nc, self.ph, self.vec, self.vec2
        G = 5
        with ExitStack() as st:
            Wx = self.sb(st, "Wx", [128, 8 * DRNN], BF16)
            Wg = self.sb(st, "Wg", [128, 8 * DRNN], BF16)
            Wkv = self.sb(st, "Wkv", [128, 8 * KVR], BF16)
            Wkr = self.sb(st, "Wkr", [128, 8 * 128], BF16)
            Wa = self.sb(st, "Wa", [128, NCH * 128], BF16)
            Wi = self.sb(st, "Wi", [128, NCH * 128], BF16)
            wb = Buf("w1")
            wkvb, wxb, wab, wgb = Buf("wkv"), Buf("wx"), Buf("wa"), Buf("wg")
            w3 = w_in.rearrange("(k p) n -> p k n", p=128)
            for k in range(8):
                self.wload(Wkv[:, k * KVR:(k + 1) * KVR], w3[:, k, C_KV:C_KV + KVR], wkvb)
                self.wload(Wkr[:, k * 128:k * 128 + 64], w3[:, k, C_KR:C_KR + 64], wkvb)
                self.wload(Wkr[:, k * 128 + 64:k * 128 + 96], w3[:, k, C_KR + 32:C_KR + 64], wkvb)
                self.wload(Wkr[:, k * 128 + 96:k * 128 + 128], w3[:, k, C_KR:C_KR + 32], wkvb)
            for k in range(8):
                self.wload(Wx[:, k * DRNN:(k + 1) * DRNN], w3[:, k, C_X:C_X + DRNN], wxb)
            for j in range(NCH):
                self.wload(Wa[:, j * 128:(j + 1) * 128], w_ra[j], wab)
                self.wload(Wi[:, j * 128:(j + 1) * 128], w_ri[j], wab)
            for k in range(8):
                self.wload(Wg[:, k * DRNN:(k + 1) * DRNN], w3[:, k, C_G:C_G + DRNN], wgb)

            N = BLK
            xin = self.sb(st, "xin", [128, 8 * N], F32); xb = Buf("xin")
            sqb = self.sb(st, "sqb", [128, 8 * N], BF16); sqbuf = Buf("sqb")
            sd = self.sb(st, "sd", [128, N], F32); sdb = Buf("sd")
            rstd = self.sb(st, "rstd", [128, N], F32); rsb = Buf("rstd")
            zT = self.sb(st, "zT", [128, 8 * N], BF16); zb = Buf("zT")
            ux = self.sb(st, "ux", [128, NCH * (N + 3)], F32); uxb = [Buf("ux%d" % j) for j in range(NCH)]
            xc = self.sb(st, "xc", [128, G * N], F32); xcb_ = [Buf("xc%d" % j) for j in range(G)]
            xcb = self.sb(st, "xcb", [128, G * N], BF16); xcbb = [Buf("xcb%d" % j) for j in range(G)]
            T1 = self.sb(st, "T1", [128, G * N], F32); T1b = [Buf("T1_%d" % j) for j in range(G)]
            T2 = self.sb(st, "T2", [128, G * N], F32); T2b = [Buf("T2_%d" % j) for j in range(G)]
            T3 = self.sb(st, "T3", [128, G * N], F32); T3b = [Buf("T3_%d" % j) for j in range(G)]
            T4 = self.sb(st, "T4", [128, G * N], F32); T4b = [Buf("T4_%d" % j) for j in range(G)]
            YT = self.sb(st, "YT", [128, G * (N // 2)], F32); YTb = [Buf("YT%d" % j) for j in range(G)]
            YU = self.sb(st, "YU", [128, G * (N // 2)], F32); YUb = [Buf("YU%d" % j) for j in range(G)]
            YO = self.sb(st, "YO", [128, G * (N // 2)], BF16); YOb = Buf("YO")
            hst = self.sb(st, "hst", [128, NCH], F32); hstb = [Buf("hst%d" % j) for j in range(NCH)]
            cs = self.sb(st, "cs", [64, 2 * N], F32); csb = Buf("cs")
            kvq = self.sb(st, "kvq", [128, 2 * N], BF16); kvqb = Buf("kvq")
            sd2 = self.sb(st, "sd2", [128, N], F32); sd2b = Buf("sd2")
            rs2 = self.sb(st, "rs2", [128, N], F32); rs2b = Buf("rs2")
            kvo = self.sb(st, "kvo", [128, 2 * N], BF16); kvob = Buf("kvo")
            kt1 = self.sb(st, "kt1", [64, N], F32); kt1b = Buf("kt1")
            kt2 = self.sb(st, "kt2", [64, N], F32); kt2b = Buf("kt2")
            kro = self.sb(st, "kro", [64, N], BF16); krob = Buf("kro")

            self.memset("gpsimd", ux[:, :], 0.0, uxb)
            self.memset("gpsimd", hst[:, :], 0.0, hstb)

            x3 = xT.rearrange("(c p) t -> p c t", p=128)
            m3 = metaT.rearrange("(c p) t -> p c t", p=128)
            ckv3 = ckv_s.rearrange("(c p) t -> p c t", p=128)
            yr3 = yrnn_s.rearrange("(c p) t -> p c t", p=128)
            nblk = S // N

            def load_block(bi):
                if bi == 0:
                    n = NMETA
                    for c in range(8):
                        self.load(xin[:, c * n:(c + 1) * n], m3[:, c, :], xb, first=(c == 0))
                    t0 = 0
                else:
                    n = N
                    for c in range(8):
                        self.load(xin[:, c * n:(c + 1) * n], x3[:, c, (bi - 1) * N:bi * N], xb, first=(c == 0))
                    t0 = NMETA + (bi - 1) * N
                return n, t0

            def load_cs(bi):
                n_ = NMETA if bi == 0 else N
                t_ = 0 if bi == 0 else NMETA + (bi - 1) * N
                self.load(cs[:, 0:n_], cs_k[:, t_:t_ + n_], csb)
                self.load(cs[:, N:N + n_], cs_k[:, LT + t_:LT + t_ + n_], csb, first=False)

            zT2 = self.sb(st, "zTb", [128, 8 * N], BF16)
            zTs = [zT, zT2]
            zbs = [zb, Buf("zTb")]

            def pre_steps(bi):
                n = NMETA if bi == 0 else N
                t0 = 0 if bi == 0 else NMETA + (bi - 1) * N
                zT_, zb_ = zTs[bi % 2], zbs[bi % 2]
                zc = [zT_[:, k * n:(k + 1) * n] for k in range(8)]
                stt_ = {}

                def s1():
                    self.act(sqb[:, 0:8 * n], xin[:, 0:8 * n], AF.Square, [xb], [sqbuf])
                    ps, pb = self.bank(hold=True)
                    self.mm(ps[:, 0:n], [(self.ones[:, :], sqb[:, c * n:(c + 1) * n]) for c in range(8)], pb, [sqbuf, self.cbuf])
                    stt_["ss"] = (ps, pb)

                def s2():
                    ps, pb = stt_["ss"]
                    self.act(sd[:, 0:n], ps[:, 0:n], AF.Sqrt, [pb], [sdb], scale=1.0 / D, bias=self.epsc[:, 0:1])
                    self.rel(pb)
                    self.recip(rstd[:, 0:n], sd[:, 0:n], [sdb], [rsb])

                def s3():
                    for c in range(8):
                        self.stt(zT_[:, c * n:(c + 1) * n], xin[:, c * n:(c + 1) * n], vec[:, V_GMIX + c:V_GMIX + c + 1], rstd[:, 0:n],
                                 ALU.mult, ALU.mult, [xb, rsb, self.cbuf], [zb_])
                    if bi < nblk:
                        load_block(bi + 1)

                def s4():
                    pk = []
                    for c2 in range(2):
                        ps, pb = self.bank(hold=True)
                        self.mm(ps[:, 0:n], [(Wkv[:, k * KVR + c2 * 128:k * KVR + (c2 + 1) * 128], zc[k]) for k in range(8)], pb, [zb_, wkvb])
                        pk.append((ps, pb))
                    stt_["pk"] = pk
                    for c2 in range(2):
                        self.act(kvq[:, c2 * n:(c2 + 1) * n], pk[c2][0][:, 0:n], AF.Square, [pk[c2][1]], [kvqb])
                    ps, pb = self.bank(hold=True)
                    self.mm(ps[:, 0:n], [(self.ones[:, :], kvq[:, c2 * n:(c2 + 1) * n]) for c2 in range(2)], pb, [kvqb, self.cbuf])
                    stt_["ss2"] = (ps, pb)

                def s5():
                    ps, pb = stt_["ss2"]
                    self.act(sd2[:, 0:n], ps[:, 0:n], AF.Sqrt, [pb], [sd2b], scale=1.0 / KVR, bias=self.epsc[:, 0:1])
                    self.rel(pb)
                    self.recip(rs2[:, 0:n], sd2[:, 0:n], [sd2b], [rs2b])

                def s6():
                    pk = stt_["pk"]
                    for c2 in range(2):
                        self.stt(kvo[:, c2 * n:(c2 + 1) * n], pk[c2][0][:, 0:n], vec[:, V_GKV + c2:V_GKV + c2 + 1], rs2[:, 0:n],
                                 ALU.mult, ALU.mult, [pk[c2][1], rs2b, self.cbuf], [kvob])
                        self.rel(pk[c2][1])
                        self.store(ckv3[:, c2, t0:t0 + n], kvo[:, c2 * n:(c2 + 1) * n], kvob)

                def s7():
                    pr = []
                    for hh in range(2):
                        ps, pb = self.bank()
                        self.mm(ps[0:64, 0:n], [(Wkr[:, k * 128 + hh * 64:k * 128 + (hh + 1) * 64], zc[k]) for k in range(8)], pb, [zb_, wkvb])
                        pr.append((ps, pb))
                    self.tt("vector", kt1[:, 0:n], pr[0][0][0:64, 0:n], cs[:, 0:n], ALU.mult, [pr[0][1], csb], [kt1b])
                    self.tt("vector", kt2[:, 0:n], pr[1][0][0:64, 0:n], cs[:, N:N + n], ALU.mult, [pr[1][1], csb], [kt2b])
                    self.tt("gpsimd", kro[:, 0:n], kt1[:, 0:n], kt2[:, 0:n], ALU.add, [kt1b, kt2b], [krob])
                    self.store(kr_s[:, t0:t0 + n], kro[:, 0:n], krob)
                    if bi < nblk:
                        load_cs(bi + 1)
                return [s1, s2, s3, s4, s5, s6, s7]

            grp_state = {}

            def gctx(bi, g0):
                n = NMETA if bi == 0 else N
                zT_, zb_ = zTs[bi % 2], zbs[bi % 2]
                zc = [zT_[:, k * n:(k + 1) * n] for k in range(8)]
                st_ = grp_state.setdefault((bi, g0), {"pu": {}, "pg": {}, "pa": {}, "pi": {}})
                return n, zb_, zc, list(range(g0, g0 + G)), st_

            def front1(bi, g0):
                n, zb_, zc, chs, st_ = gctx(bi, g0)
                pu = st_["pu"]
                W = N + 3

                def issue_ux(j):
                    ps, pb = self.bank(hold=True)
                    self.mm(ps[:, 0:n], [(Wx[:, k * DRNN + j * 128:k * DRNN + (j + 1) * 128], zc[k]) for k in range(8)], pb, [zb_, wxb])
                    pu[j] = (ps, pb)

                def cast(jj):
                    self.act(xcb[:, jj * N:jj * N + n], xc[:, jj * N:jj * N + n], AF.Identity, [xcb_[jj]], [xcbb[jj]])

                def chunk(jj, j):
                    def f():
                        if jj == 0:
                            issue_ux(chs[0])
                        if jj + 1 < G:
                            issue_ux(chs[jj + 1])
                        ps, pb = pu[j]
                        o = j * W
                        self.act(ux[:, o + 3:o + 3 + n], ps[:, 0:n], AF.Identity, [pb], [uxb[j]])
                        self.rel(pb)
                        xcj = xc[:, jj * N:jj * N + n]
                        self.act(xcj, ux[:, o:o + n], AF.Identity, [uxb[j], self.cbuf], [xcb_[jj]],
                                 scale=vec[:, V_CONVW + j:V_CONVW + j + 1], bias=vec[:, V_CONVB + j:V_CONVB + j + 1])
                        if jj >= 1:
                            cast(jj - 1)
                        for kk in range(1, 4):
                            self.stt(xcj, ux[:, o + kk:o + kk + n], vec[:, V_CONVW + kk * 10 + j:V_CONVW + kk * 10 + j + 1], xcj,
                                     ALU.mult, ALU.add, [uxb[j], self.cbuf], [xcb_[jj]])
                        self.cp("gpsimd", ux[:, o:o + 3], ux[:, o + n:o + n + 3], [], [uxb[j]])
                        if jj == G - 1:
                            cast(jj)
                    return f
                return [chunk(jj, j) for jj, j in enumerate(chs)]

            def front2(bi, g0, steps=()):
                n, zb_, zc, chs, st_ = gctx(bi, g0)
                pa, pi, pg = st_["pa"], st_["pi"], st_["pg"]
                own = bi > 0
                for f in steps:
                    f()

                def issue_gates(jj, j):
                    pa_, pab = self.bank(hold=True)
                    self.mm(pa_[:, 0:n], [(Wa[:, j * 128:(j + 1) * 128], xcb[:, jj * N:jj * N + n])], pab, [xcbb[jj], wab])
                    pi_, pib = self.bank(hold=True)
                    self.mm(pi_[:, 0:n], [(Wi[:, j * 128:(j + 1) * 128], xcb[:, jj * N:jj * N + n])], pib, [xcbb[jj], wab])
                    pa[j] = (pa_, pab)
                    pi[j] = (pi_, pib)
                issue_gates(0, chs[0])
                for jj, j in enumerate(chs):
                    if jj + 1 < G:
                        issue_gates(jj + 1, chs[jj + 1])
                    sl = slice(jj * N, jj * N + n)
                    self.act(T1[:, sl], pa[j][0][:, 0:n], AF.Tanh, [pa[j][1], self.cbuf], [T1b[jj]], scale=0.5, bias=vec2[:, j:j + 1])
                    self.act(T3[:, sl], pi[j][0][:, 0:n], AF.Tanh, [pi[j][1], self.cbuf], [T3b[jj]], scale=0.5, bias=vec2[:, 10 + j:11 + j])
                    self.rel(pa[j][1])
                    self.rel(pi[j][1])
                if own:
                    for j in chs:
                        ps, pb = self.bank(hold=True)
                        self.mm(ps[:, 0:n], [(Wg[:, k * DRNN + j * 128:k * DRNN + (j + 1) * 128], zc[k]) for k in range(8)], pb, [zb_, wgb])
                        pg[j] = (ps, pb)
                for jj, j in enumerate(chs):
                    sl = slice(jj * N, jj * N + n)
                    self.stt(T3[:, sl], T3[:, sl], 1.0, xc[:, sl], ALU.add, ALU.mult, [xcb_[jj]], [T3b[jj]])

            def back(bi, g0):
                n, zb_, zc, chs, st_ = gctx(bi, g0)
                pg = st_["pg"]
                own = bi > 0
                L = []

                def mk(fn, *a):
                    return lambda: fn(*a)

                def c_(jj, j):
                    sl = slice(jj * N, jj * N + n)
                    self.act(T2[:, sl], T1[:, sl], AF.Exp, [T1b[jj], self.cbuf], [T2b[jj]], scale=vec2[:, 30 + j:31 + j], bias=vec2[:, 30 + j:31 + j])
                    self.act(T1[:, sl], T1[:, sl], AF.Exp, [self.cbuf], [T1b[jj]], scale=vec2[:, 20 + j:21 + j], bias=vec2[:, 20 + j:21 + j])

                def d_(jj, j):
                    sl = slice(jj * N, jj * N + n)
                    self.act(T2[:, sl], T2[:, sl], AF.Sqrt, [], [T2b[jj]], scale=-0.25, bias=0.25)

                def e1_(jj, j):
                    sl = slice(jj * N, jj * N + n)
                    self.tt("gpsimd", T2[:, sl], T2[:, sl], T3[:, sl], ALU.mult, [T3b[jj]], [T2b[jj]])

                def e2_(jj, j):
                    sl = slice(jj * N, jj * N + n)
                    ph.op("vector", self._scanfn(T4[:, sl], T1[:, sl], T2[:, sl], hst[:, j:j + 1]),
                          [T1b[jj], T2b[jj], hstb[j]], [T4b[jj]])

                def e3_(jj, j):
                    self.cp("gpsimd", hst[:, j:j + 1], T4[:, jj * N + n - 1:jj * N + n], [T4b[jj]], [hstb[j]])

                def f_(jj, j):
                    sl = slice(jj * N, jj * N + n)
                    self.act(T3[:, sl], pg[j][0][:, 0:n], AF.Gelu_apprx_tanh, [pg[j][1]], [T3b[jj]])
                    self.rel(pg[j][1])

                h2 = N // 2

                def h_(jj, j):
                    sl = slice(jj * N, jj * N + n)
                    gv = T3[:, sl].rearrange("p (a e t) -> p a e t", a=2, e=2)
                    hv = T4[:, sl].rearrange("p (a e t) -> p a e t", a=2, e=2)
                    ytv = YT[:, jj * h2:(jj + 1) * h2].rearrange("p (a t) -> p a t", a=2)
                    yuv = YU[:, jj * h2:(jj + 1) * h2].rearrange("p (a t) -> p a t", a=2)
                    yov = YO[:, jj * h2:(jj + 1) * h2].rearrange("p (a t) -> p a t", a=2)
                    self.stt(ytv, gv[:, :, 0, :], vec[:, V_SEL:V_SEL + 1], hv[:, :, 0, :], ALU.mult, ALU.mult,
                             [T3b[jj], T4b[jj], self.cbuf], [YTb[jj]])
                    self.stt(yuv, gv[:, :, 1, :], vec[:, V_SEL + 1:V_SEL + 2], hv[:, :, 1, :], ALU.mult, ALU.mult,
                             [T3b[jj], T4b[jj], self.cbuf], [YUb[jj]])
                    self.tt("gpsimd", yov, ytv, yuv, ALU.add, [YTb[jj], YUb[jj]], [YOb])

                def st_(jj, j):
                    ob0 = (bi - 1) * h2
                    self.store(yr3[:, j, ob0:ob0 + h2], YO[:, jj * h2:(jj + 1) * h2], YOb)
                stages = [c_, d_, e1_, e2_, e3_] + ([f_, h_, st_] if own else [])
                for fn in stages:
                    for jj, j in enumerate(chs):
                        L.append(mk(fn, jj, j))
                return L

            def merge(Lb, La):
                nb_, na_ = len(Lb), len(La)
                if na_ == 0:
                    for f in Lb:
                        f()
                    return
                every = max(1, nb_ // (na_ + 1))
                ia = 0
                for i, f in enumerate(Lb):
                    f()
                    if ia < na_ and (i + 1) % every == 0:
                        La[ia]()
                        ia += 1
                while ia < na_:
                    La[ia]()
                    ia += 1

            groups = [(bi, g0) for bi in range(nblk + 1) for g0 in (0, G)]
            load_block(0)
            load_cs(0)
            for f in pre_steps(0):
                f()
            for f in front1(*groups[0]):
                f()
            front2(*groups[0])
            pend_pre = []
            for gi_, (bi, g0) in enumerate(groups):
                Lb = back(bi, g0)
                La = []
                late = []
                if gi_ + 1 < len(groups):
                    La = front1(*groups[gi_ + 1])
                if g0 == 0 and bi < nblk:
                    ps_ = pre_steps(bi + 1)
                    La = [La[0], ps_[0], La[1], ps_[1], La[2], ps_[2]] + La[3:]
                    late = ps_[3:]
                merge(Lb, La)
                if gi_ + 1 < len(groups):
                    front2(*groups[gi_ + 1], steps=late)
                else:
                    for f in late:
                        f()
            self.end_phase()

    @staticmethod
    def _scanfn(out, d0, d1, init):
        return lambda e: e.tensor_tensor_scan(out=out, data0=d0, data1=d1, initial=init, op0=ALU.mult, op1=ALU.add)

    def phase2(self, xoT, w_in, w_uq, w_ukv, masks, cs_q, ckv_s, kr_s, yatt_s):
        nc, ph, vec = self.nc, self.ph, self.vec
        N = BLK
        NT = 65
        with ExitStack() as st:
            ckvT = self.sb(st, "ckvT", [128, 2 * LT], BF16); ckvb = Buf("ckvT")
            krT = self.sb(st, "krT", [128, LT], BF16); krb = Buf("krT")
            qnT = self.sb(st, "qnT", [128, 3 * OWN], BF16); qnb = [Buf("qn%d" % i) for i in range(OWN // N)]
            csq = self.sb(st, "csq", [64, 2 * OWN], F32); csqb = Buf("csq")
            Wuq = self.sb(st, "Wuq", [128, 3 * 1536], BF16)
            Wus = self.sb(st, "Wus", [128, 3 * 512], BF16)
            Wukv = self.sb(st, "Wukv", [128, 2 * 2048], BF16)
            Mk = self.sb(st, "Mk", [128, 256], BF16)
            wb = Buf("w2")
            ckv3 = ckv_s.rearrange("(c p) t -> p c t", p=128)
            for c2 in range(2):
                self.load(ckvT[:, c2 * LT:(c2 + 1) * LT], ckv3[:, c2, :], ckvb, first=(c2 == 0))
            self.memset("gpsimd", krT[64:128, :], 0.0, [krb])
            self.load(krT[0:64, :], kr_s, krb)
            self.load(csq[:, :], cs_q, csqb)
            with ExitStack() as st2:
                Wq = self.sb(st2, "Wq", [128, 8 * QR], BF16)
                w3 = w_in.rearrange("(k p) n -> p k n", p=128)
                for k in range(8):
                    self.wload(Wq[:, k * QR:(k + 1) * QR], w3[:, k, C_Q:C_Q + QR], wb)
                xin = self.sb(st2, "xin2", [128, 8 * N], F32); xb = Buf("xin2")
                sqb = self.sb(st2, "sqb2", [128, 8 * N], BF16); sqbuf = Buf("sqb2")
                sd = self.sb(st2, "sd_2", [128, N], F32); sdb = Buf("sd_2")
                rstd = self.sb(st2, "rstd2", [128, N], F32); rsb = Buf("rstd2")
                zT = self.sb(st2, "zT2", [128, 8 * N], BF16); zb = Buf("zT2")
                qq = self.sb(st2, "qq", [128, 3 * N], BF16); qqb = Buf("qq")
                sd3 = self.sb(st2, "sd3", [128, N], F32); sd3b = Buf("sd3")
                rs3 = self.sb(st2, "rs3", [128, N], F32); rs3b = Buf("rs3")
                xo3 = xoT.rearrange("(c p) t -> p c t", p=128)
                for c in range(8):
                    self.load(xin[:, c * N:(c + 1) * N], xo3[:, c, 0:N], xb, first=(c == 0))
                for ob in range(OWN // N):
                    self.rms8(xin, xb, N, V_GMIX, sqb, sqbuf, sd, sdb, rstd, rsb, zT, zb)
                    if ob + 1 < OWN // N:
                        for c in range(8):
                            self.load(xin[:, c * N:(c + 1) * N], xo3[:, c, (ob + 1) * N:(ob + 2) * N], xb, first=(c == 0))
                    pq = []
                    for c3 in range(3):
                        ps, pb = self.bank()
                        self.mm(ps[:, 0:N], [(Wq[:, k * QR + c3 * 128:k * QR + (c3 + 1) * 128], zT[:, k * N:(k + 1) * N]) for k in range(8)], pb, [zb, wb])
                        pq.append((ps, pb))
                    for c3 in range(3):
                        self.act(qq[:, c3 * N:(c3 + 1) * N], pq[c3][0][:, 0:N], AF.Square, [pq[c3][1]], [qqb])
                    ps, pb = self.bank()
                    self.mm(ps[:, 0:N], [(self.ones[:, :], qq[:, c3 * N:(c3 + 1) * N]) for c3 in range(3)], pb, [qqb, self.cbuf])
                    self.act(sd3[:, :], ps[:, 0:N], AF.Sqrt, [pb], [sd3b], scale=1.0 / QR, bias=self.epsc[:, 0:1])
                    self.recip(rs3[:, :], sd3[:, :], [sd3b], [rs3b])
                    for c3 in range(3):
                        self.stt(qnT[:, c3 * OWN + ob * N:c3 * OWN + (ob + 1) * N], pq[c3][0][:, 0:N], vec[:, V_GQ + c3:V_GQ + c3 + 1], rs3[:, :],
                                 ALU.mult, ALU.mult, [pq[c3][1], rs3b, self.cbuf], [qnb[ob]])
                self.end_phase()
            uq3 = w_uq.rearrange("(k p) n -> p k n", p=128)
            for k in range(3):
                self.wload(Wuq[:, k * 1536:(k + 1) * 1536], uq3[:, k, :], wb)
                src = uq3[:, k, :].rearrange("p (h d) -> p h d", d=192)
                dst = Wus[:, k * 512:(k + 1) * 512].rearrange("p (h d) -> p h d", d=64)
                self.wload(dst[:, :, 0:32], src[:, :, 160:192], wb)
                self.wload(dst[:, :, 32:64], src[:, :, 128:160], wb)
            kv3 = w_ukv.rearrange("(k p) n -> p k n", p=128)
            for k in range(2):
                self.wload(Wukv[:, k * 2048:(k + 1) * 2048], kv3[:, k, :], wb)
            self.wload(Mk[:, :], masks, wb)
            KT = [self.sb(st, "KT%d" % i, [128, LT], BF16) for i in range(2)]; KTb = [Buf("KT%d" % i) for i in range(2)]
            Vh = [self.sb(st, "Vh%d" % i, [128, NT * 128], BF16) for i in range(2)]; Vb = [Buf("Vh%d" % i) for i in range(2)]
            QT = [self.sb(st, "QT%d" % i, [128, N], BF16) for i in range(2)]; QTb = [Buf("QT%d" % i) for i in range(2)]
            QRt = [self.sb(st, "QR%d" % i, [128, N], BF16) for i in range(2)]; QRb = [Buf("QR%d" % i) for i in range(2)]
            for i in range(2):
                self.memset("gpsimd", QRt[i][64:128, :], 0.0, [QRb[i]])
            qt1 = self.sb(st, "qt1", [64, N], F32); qt1b = Buf("qt1")
            qt2 = self.sb(st, "qt2", [64, N], F32); qt2b = Buf("qt2")
            NP = 4
            Pt = [self.sb(st, "Pt%d" % i, [128, N], BF16) for i in range(NP)]; Ptb = [Buf("Pt%d" % i) for i in range(NP)]
            rden = self.sb(st, "rden", [128, N], F32); rdb = Buf("rden")
            Ob = [self.sb(st, "Ob%d" % i, [128, N], BF16) for i in range(2)]; Obb = [Buf("Ob%d" % i) for i in range(2)]
            ya3 = yatt_s.rearrange("(c p) t -> p c t", p=128)
            allbanks, allbufs = self._banks, self._bank_bufs
            self._banks, self._bank_bufs, self._bank_i = allbanks[0:4], allbufs[0:4], 0
            OTs = [(allbanks[4], allbufs[4]), (allbanks[5], allbufs[5])]
            DNs = [(allbanks[6], allbufs[6]), (allbanks[7], allbufs[7])]
            nheads = NH if self.nheads is None else self.nheads
            NG = OWN // N

            def proj_tasks(h):
                KTh, KTbh, Vhh, Vbh = KT[h % 2], KTb[h % 2], Vh[h % 2], Vb[h % 2]
                tasks = []

                def ktask(c0, n):
                    def f():
                        ps, pb = self.bank()
                        self.mm(ps[:, 0:n], [(Wukv[:, k * 2048 + h * 256:k * 2048 + h * 256 + 128], ckvT[:, k * LT + c0:k * LT + c0 + n]) for k in range(2)],
                                pb, [ckvb, wb])
                        self.cp("vector", KTh[:, c0:c0 + n], ps[:, 0:n], [pb], [KTbh])
                    return f

                def vtask0():
                    ps, pb = self.bank()
                    self.mm(ps[0:NMETA, 0:128], [(ckvT[:, k * LT:k * LT + NMETA], Wukv[:, k * 2048 + h * 256 + 128:k * 2048 + h * 256 + 256]) for k in range(2)],
                            pb, [ckvb, wb])
                    self.cp("vector", Vhh[0:NMETA, 0:128], ps[0:NMETA, 0:128], [pb], [Vbh])

                def vtask(t4):
                    def f():
                        ps, pb = self.bank()
                        for q4 in range(4):
                            c0 = NMETA + (t4 + q4) * 128
                            self.mm(ps[:, q4 * 128:(q4 + 1) * 128],
                                    [(ckvT[:, k * LT + c0:k * LT + c0 + 128], Wukv[:, k * 2048 + h * 256 + 128:k * 2048 + h * 256 + 256]) for k in range(2)],
                                    pb, [ckvb, wb])
                        self.cp("vector", Vhh[:, (1 + t4) * 128:(5 + t4) * 128], ps[:, 0:512], [pb], [Vbh])
                    return f
                tasks.append(ktask(0, NMETA))
                tasks.append(vtask0)
                for i in range(S // N):
                    tasks.append(ktask(NMETA + i * N, N))
                    tasks.append(vtask(4 * i))
                return tasks

            def qproj(h, g, qi):
                q0 = g * N
                ps, pb = self.bank()
                self.mm(ps[:, 0:N], [(Wuq[:, k * 1536 + h * 192:k * 1536 + h * 192 + 128], qnT[:, k * OWN + q0:k * OWN + q0 + N]) for k in range(3)],
                        pb, [qnb[g], wb])
                self.cp("vector", QT[qi][:, :], ps[:, 0:N], [pb], [QTb[qi]])
                psa, pba = self.bank()
                self.mm(psa[0:64, 0:N], [(Wuq[:, k * 1536 + h * 192 + 128:k * 1536 + h * 192 + 192], qnT[:, k * OWN + q0:k * OWN + q0 + N]) for k in range(3)],
                        pba, [qnb[g], wb])
                self.tt("vector", qt1[:, :], psa[0:64, 0:N], csq[:, q0:q0 + N], ALU.mult, [pba, csqb], [qt1b])
                psb, pbb = self.bank()
                self.mm(psb[0:64, 0:N], [(Wus[:, k * 512 + h * 64:k * 512 + (h + 1) * 64], qnT[:, k * OWN + q0:k * OWN + q0 + N]) for k in range(3)],
                        pbb, [qnb[g], wb])
                self.tt("vector", qt2[:, :], psb[0:64, 0:N], csq[:, OWN + q0:OWN + q0 + N], ALU.mult, [pbb, csqb], [qt2b])
                self.tt("gpsimd", QRt[qi][0:64, :], qt1[:, :], qt2[:, :], ALU.add, [qt1b, qt2b], [QRb[qi]])

            groups = [(h, g) for h in range(nheads) for g in range(NG)]
            units = []
            for gi_, (h, g) in enumerate(groups):
                nkt = 1 + 8 * g + 8
                for kt in range(nkt):
                    units.append((gi_, h, g, kt, nkt))
            for t in proj_tasks(0):
                t()
            qproj(groups[0][0], groups[0][1], 0)
            pend = []
            pend_every = 1
            DPT = 2
            nU = len(units)
            live = {}
            for step in range(nU + DPT):
                if step < nU:
                    gi_, h, g, kt, nkt = units[step]
                    qi = gi_ % 2
                    if kt == 0 and g == 0 and h + 1 < nheads:
                        pend = proj_tasks(h + 1)
                        nun = sum(1 + 8 * g2 + 8 for g2 in range(NG))
                        pend_every = max(1, (nun - 40) // len(pend))
                    if kt == nkt // 2 and gi_ + 1 < len(groups):
                        qproj(groups[gi_ + 1][0], groups[gi_ + 1][1], (gi_ + 1) % 2)
                    if kt == 0:
                        k0, nk, qs, mask = 0, NMETA, 0, None
                    else:
                        s_ = kt - 1
                        k0, nk = NMETA + s_ * 128, 128
                        d = s_ - 8 * g
                        if d < 0:
                            qs, mask = 0, None
                        else:
                            qs, mask = (d // 2) * 128, d % 2
                    ps, pb = self.bank()
                    self.mm(ps[0:nk, qs:N], [(KT[h % 2][:, k0:k0 + nk], QT[qi][:, qs:N]), (krT[:, k0:k0 + nk], QRt[qi][:, qs:N])],
                            pb, [KTb[h % 2], krb, QTb[qi], QRb[qi]])
                    P, Pb = Pt[step % NP], Ptb[step % NP]
                    self.act(P[0:nk, qs:N], ps[0:nk, qs:N], AF.Exp, [pb], [Pb], scale=SCALE)
                    if mask is not None:
                        self.tt("gpsimd", P[:, qs:qs + 128], P[:, qs:qs + 128], Mk[:, mask * 128:(mask + 1) * 128], ALU.mult, [wb], [Pb])
                    live[step] = (nk, qs)
                    if pend and kt % pend_every == 0 and g >= 0:
                        pend.pop(0)()
                if step >= DPT:
                    u = step - DPT
                    gi_, h, g, kt, nkt = units[u]
                    qi = gi_ % 2
                    nk, qs = live.pop(u)
                    P, Pb = Pt[u % NP], Ptb[u % NP]
                    OT, OTb = OTs[qi]
                    DN, DNb = DNs[qi]
                    first = kt == 0
                    last = kt == nkt - 1
                    self.mm(OT[:, qs:N], [(Vh[h % 2][0:nk, kt * 128:(kt + 1) * 128], P[0:nk, qs:N])], OTb, [Vb[h % 2], Pb], flags=(first, last))
                    self.mm(DN[:, qs:N], [(self.ones[0:nk, :], P[0:nk, qs:N])], DNb, [Pb, self.cbuf], flags=(first, last))
                    if last:
                        q0 = g * N
                        self.recip(rden[:, :], DN[:, 0:N], [DNb], [rdb])
                        self.tt("vector", Ob[qi][:, :], OT[:, 0:N], rden[:, :], ALU.mult, [OTb, rdb], [Obb[qi]])
                        self.store(ya3[:, h, q0:q0 + N], Ob[qi][:, :], Obb[qi])
                        while g == NG - 1 and pend:
                            pend.pop(0)()
            self._banks, self._bank_bufs, self._bank_i = allbanks, allbufs, 0
            self.end_phase()

    def phase3(self, xoT, w_in, w_br, w_o, yrnn_s, yatt_s, hmid_s):
        nc, ph, vec = self.nc, self.ph, self.vec
        N = BLK
        with ExitStack() as st:
            Wm = self.sb(st, "Wm", [128, 8 * 2048], BF16)
            Wb = self.sb(st, "Wb", [128, 18 * D], BF16)
            Wo = self.sb(st, "Wo", [128, 8 * D], BF16)
            wmb = [Buf("wm%d" % i) for i in range(4)]
            wbb = [Buf("wbr%d" % i) for i in range(2)]
            wob = [Buf("wo%d" % i) for i in range(2)]
            w3 = w_in.rearrange("(k p) n -> p k n", p=128)
            b3 = w_br.rearrange("(k p) n -> p k n", p=128)
            o3 = w_o.rearrange("(k p) n -> p k n", p=128)

            def ld_wm(pc):
                for k in range(8):
                    self.wload(Wm[:, k * 2048 + pc * 512:k * 2048 + (pc + 1) * 512], w3[:, k, C_M + pc * 512:C_M + (pc + 1) * 512], wmb[pc])

            def ld_wb(pc):
                for k in range(18):
                    self.wload(Wb[:, k * D + pc * 512:k * D + (pc + 1) * 512], b3[:, k, pc * 512:(pc + 1) * 512], wbb[pc])

            def ld_wo(pc):
                for k in range(8):
                    self.wload(Wo[:, k * D + pc * 512:k * D + (pc + 1) * 512], o3[:, k, pc * 512:(pc + 1) * 512], wob[pc])
            ld_wm(0); ld_wm(2); ld_wb(0); ld_wm(1); ld_wm(3); ld_wb(1); ld_wo(0); ld_wo(1)
            xin = [self.sb(st, "xin3_%d" % i, [128, 8 * N], F32) for i in range(2)]; xb = [Buf("xin3_%d" % i) for i in range(2)]
            sqs = [self.sb(st, "sqs3_%d" % i, [128, N], BF16) for i in range(2)]; sqsb = [Buf("sqs3_%d" % i) for i in range(2)]
            sd = self.sb(st, "sd_3", [128, N], F32); sdb = Buf("sd_3")
            rstd = self.sb(st, "rstd3", [128, N], F32); rsb = Buf("rstd3")
            zTs = [self.sb(st, "zT3_%d" % i, [128, 8 * N], BF16) for i in range(2)]; zbs = [Buf("zT3_%d" % i) for i in range(2)]
            yr = self.sb(st, "yr", [128, NCH * N], BF16); yrb = Buf("yr")
            ya = self.sb(st, "ya", [128, 8 * N], BF16); yab = Buf("ya")
            G0 = [self.sb(st, "G0_%d" % i, [128, N], F32) for i in range(2)]; G0b = [Buf("G0_%d" % i) for i in range(2)]
            G1 = [self.sb(st, "G1_%d" % i, [128, N], F32) for i in range(2)]; G1b = [Buf("G1_%d" % i) for i in range(2)]
            mx = self.sb(st, "mx", [128, 8 * N], BF16); mxb = Buf("mx")
            xo3 = xoT.rearrange("(c p) t -> p c t", p=128)
            yr3 = yrnn_s.rearrange("(c p) t -> p c t", p=128)
            ya3 = yatt_s.rearrange("(c p) t -> p c t", p=128)
            hm3 = hmid_s.rearrange("(c p) t -> p c t", p=128)
            nb = OWN // N

            def ld(ob):
                xi = xin[ob % 2]
                for c in range(8):
                    self.load(xi[:, c * N:(c + 1) * N], xo3[:, c, ob * N:(ob + 1) * N], xb[ob % 2], first=(c == 0))
            def norm(ob):
                self.rms8p(xin[ob % 2], xb[ob % 2], N, V_GMIX, sqs, sqsb, sd, sdb, rstd, rsb, zTs[ob % 2], zbs[ob % 2])
            ld(0)
            if nb > 1:
                ld(1)
            norm(0)
            for ob in range(nb):
                xi, xbb = xin[ob % 2], xb[ob % 2]
                zT, zb = zTs[ob % 2], zbs[ob % 2]
                for c in range(NCH):
                    self.load(yr[:, c * N:(c + 1) * N], yr3[:, c, ob * N:(ob + 1) * N], yrb, first=(c == 0))
                for c in range(8):
                    self.load(ya[:, c * N:(c + 1) * N], ya3[:, c, ob * N:(ob + 1) * N], yab, first=(c == 0))
                for fc in range(8):
                    i2 = fc % 2
                    if fc == 4 and ob + 1 < nb:
                        norm(ob + 1)
                    p0, p0b = self.bank()
                    self.mm(p0[:, 0:N], [(Wm[:, k * 2048 + fc * 128:k * 2048 + (fc + 1) * 128], zT[:, k * N:(k + 1) * N]) for k in range(8)], p0b, [zb, wmb[fc // 4]])
                    p1, p1b = self.bank()
                    self.mm(p1[:, 0:N], [(Wm[:, k * 2048 + D + fc * 128:k * 2048 + D + (fc + 1) * 128], zT[:, k * N:(k + 1) * N]) for k in range(8)], p1b, [zb, wmb[2 + fc // 4]])
                    pr, prb = self.bank()
                    self.mm(pr[:, 0:N], [(Wb[:, k * D + fc * 128:k * D + (fc + 1) * 128], yr[:, k * N:(k + 1) * N]) for k in range(NCH)], prb, [yrb, wbb[fc // 4]])
                    pa, pab = self.bank()
                    self.mm(pa[:, 0:N], [(Wb[:, (NCH + k) * D + fc * 128:(NCH + k) * D + (fc + 1) * 128], ya[:, k * N:(k + 1) * N]) for k in range(8)], pab, [yab, wbb[fc // 4]])
                    self.act(G0[i2][:, :], p0[:, 0:N], AF.Sigmoid, [p0b, self.cbuf], [G0b[i2]], bias=vec[:, V_BGATE + fc:V_BGATE + fc + 1])
                    self.act(G1[i2][:, :], p1[:, 0:N], AF.Sigmoid, [p1b, self.cbuf], [G1b[i2]], bias=vec[:, V_BGATE + 8 + fc:V_BGATE + 9 + fc])
                    self.tt("vector", G0[i2][:, :], pr[:, 0:N], G0[i2][:, :], ALU.mult, [prb], [G0b[i2]])
                    self.tt("vector", G1[i2][:, :], pa[:, 0:N], G1[i2][:, :], ALU.mult, [pab], [G1b[i2]])
                    self.tt("gpsimd", mx[:, fc * N:(fc + 1) * N], G0[i2][:, :], G1[i2][:, :], ALU.add, [G0b[i2], G1b[i2]], [mxb])
                for fc in range(8):
                    ps, pb = self.bank()
                    self.mm(ps[:, 0:N], [(Wo[:, k * D + fc * 128:k * D + (fc + 1) * 128], mx[:, k * N:(k + 1) * N]) for k in range(8)], pb, [mxb, wob[fc // 4]])
                    self.tt("vector", xi[:, fc * N:(fc + 1) * N], ps[:, 0:N], xi[:, fc * N:(fc + 1) * N], ALU.add, [pb], [xbb])
                for c in range(8):
                    self.store(hm3[:, c, ob * N:(ob + 1) * N], xi[:, c * N:(c + 1) * N], xbb)
                if ob + 2 < nb:
                    ld(ob + 2)
            self.end_phase()

    def phase4(self, w_f1, w_f2, hmid_s, outT):
        nc, ph, vec = self.nc, self.ph, self.vec
        N = BLK
        with ExitStack() as st:
            W1 = self.sb(st, "W1", [128, 8 * 2 * DFF], BF16)
            W2 = self.sb(st, "W2", [128, NFF * D], BF16)
            w1b = [Buf("w1_%d" % i) for i in range(11)]
            w2b = [Buf("w2_%d" % i) for i in range(2)]
            f13 = w_f1.rearrange("(k p) n -> p k n", p=128)
            f23 = w_f2.rearrange("(k p) n -> p k n", p=128)
            for pc in (0, 5, 6, 1, 7, 2, 8, 3, 9, 4, 10):
                for k in range(8):
                    self.wload(W1[:, k * 5632 + pc * 512:k * 5632 + (pc + 1) * 512], f13[:, k, pc * 512:(pc + 1) * 512], w1b[pc])
            for pc in range(2):
                for k in range(NFF):
                    self.wload(W2[:, k * D + pc * 512:k * D + (pc + 1) * 512], f23[:, k, pc * 512:(pc + 1) * 512], w2b[pc])
            hm = [self.sb(st, "hm%d" % i, [128, 8 * N], F32) for i in range(2)]; hb = [Buf("hm%d" % i) for i in range(2)]
            sqs = [self.sb(st, "sqs4_%d" % i, [128, N], BF16) for i in range(2)]; sqsb = [Buf("sqs4_%d" % i) for i in range(2)]
            sd = self.sb(st, "sd_4", [128, N], F32); sdb = Buf("sd_4")
            rstd = self.sb(st, "rstd4", [128, N], F32); rsb = Buf("rstd4")
            zT1 = self.sb(st, "zT4", [128, 8 * N], BF16); zb1 = Buf("zT4")
            zTs = [zT1, zT1]; zbs = [zb1, zb1]
            sg = [self.sb(st, "sg%d" % i, [128, N], F32) for i in range(2)]; sgb = [Buf("sg%d" % i) for i in range(2)]
            actb = self.sb(st, "actb", [128, NFF * N], BF16); acb = Buf("actb")
            hm3 = hmid_s.rearrange("(c p) t -> p c t", p=128)
            ov = outT.rearrange("(c p) t -> p c t", p=128)
            nb = OWN // N

            def ld(ob):
                for c in range(8):
                    self.load(hm[ob % 2][:, c * N:(c + 1) * N], hm3[:, c, ob * N:(ob + 1) * N], hb[ob % 2], first=(c == 0))

            def norm(ob):
                self.rms8p(hm[ob % 2], hb[ob % 2], N, V_GFFN, sqs, sqsb, sd, sdb, rstd, rsb, zTs[ob % 2], zbs[ob % 2])
            ld(0)
            if nb > 1:
                ld(1)
            norm(0)
            for ob in range(nb):
                h_, hbb = hm[ob % 2], hb[ob % 2]
                zT, zb = zTs[ob % 2], zbs[ob % 2]
                for j in range(NFF):
                    i2 = j % 2
                    pg, pgb = self.bank()
                    self.mm(pg[:, 0:N], [(W1[:, k * 5632 + j * 128:k * 5632 + (j + 1) * 128], zT[:, k * N:(k + 1) * N]) for k in range(8)], pgb, [zb, w1b[j // 4]])
                    pu, pub = self.bank()
                    self.mm(pu[:, 0:N], [(W1[:, k * 5632 + DFF + j * 128:k * 5632 + DFF + (j + 1) * 128], zT[:, k * N:(k + 1) * N]) for k in range(8)], pub, [zb, w1b[(22 + j) // 4]])
                    self.act(sg[i2][:, :], pg[:, 0:N], AF.Silu, [pgb], [sgb[i2]])
                    self.tt("vector", actb[:, j * N:(j + 1) * N], pu[:, 0:N], sg[i2][:, :], ALU.mult, [pub, sgb[i2]], [acb])
                if ob + 1 < nb:
                    norm(ob + 1)
                for fc in range(8):
                    ps, pb = self.bank()
                    self.mm(ps[:, 0:N], [(W2[:, k * D + fc * 128:k * D + (fc + 1) * 128], actb[:, k * N:(k + 1) * N]) for k in range(NFF)], pb, [acb, w2b[fc // 4]])
                    self.tt("vector", h_[:, fc * N:(fc + 1) * N], ps[:, 0:N], h_[:, fc * N:(fc + 1) * N], ALU.add, [pb], [hbb])
                self.rms8p(h_, hbb, N, V_GFIN, sqs, sqsb, sd, sdb, rstd, rsb, h_, hbb)
                for c in range(8):
                    self.store(ov[:, c, ob * N:(ob + 1) * N], h_[:, c * N:(c + 1) * N], hbb)
                if ob + 2 < nb:
                    ld(ob + 2)
            self.end_phase()


def _cos_sin_tables():
    inv = (10000.0 ** (-np.arange(0, ROPE, 2, dtype=np.float32) / np.float32(ROPE))).astype(np.float32)
    pos = np.arange(LT, dtype=np.float32)
    ang = (pos[:, None] * inv[None, :]).astype(np.float32)
    cos = np.cos(ang).astype(np.float32).T
    sin = np.sin(ang).astype(np.float32).T
    cos2 = np.concatenate([cos, cos], 0)
    sin2 = np.concatenate([-sin, sin], 0)
    return np.ascontiguousarray(cos2), np.ascontiguousarray(sin2)


def _chunkvec(v):
    v = np.asarray(v, np.float32).reshape(-1)
    return v.reshape(-1, 128).T


def make_in_maps(inputs):
    f = lambda k: np.asarray(inputs[k], np.float32)
    x = f("x")
    cos2, sin2 = _cos_sin_tables()
    cs_k = np.ascontiguousarray(np.concatenate([cos2, sin2], 1))
    dchunk = np.zeros((128, 128), np.float32)
    kk = np.arange(128)[:, None] // 64
    qq = np.arange(128)[None, :] // 64
    dmask = (kk <= qq).astype(np.float32)
    vec_common = np.zeros((128, NV), np.float32)
    vec_common[:, V_GMIX:V_GMIX + 8] = _chunkvec(f("norm_mix_g")[0])
    vec_common[:, V_GFFN:V_GFFN + 8] = _chunkvec(f("norm_ffn_g")[0])
    vec_common[:, V_GFIN:V_GFIN + 8] = _chunkvec(f("final_norm_g"))
    vec_common[:, V_BGATE:V_BGATE + 16] = _chunkvec(f("b_gate")[0].reshape(-1))
    cw = f("conv_w")[0]
    for k in range(4):
        vec_common[:, V_CONVW + 10 * k:V_CONVW + 10 * k + 10] = _chunkvec(cw[k])
    vec_common[:, V_CONVB:V_CONVB + 10] = _chunkvec(f("conv_b")[0])
    vec_common[:, V_BA:V_BA + 10] = _chunkvec(f("b_rec_a")[0])
    vec_common[:, V_BI:V_BI + 10] = _chunkvec(f("b_rec_i")[0])
    vec_common[:, V_LAM:V_LAM + 10] = _chunkvec(f("lru_lambda")[0])
    vec_common[:, V_GQ:V_GQ + 3] = _chunkvec(f("q_norm_g")[0])
    vec_common[:, V_GKV:V_GKV + 2] = _chunkvec(f("kv_norm_g")[0])
    shared = {
        "metaT": np.ascontiguousarray(f("meta_tokens").T),
        "w_in": np.ascontiguousarray(f("w_in")[0]),
        "w_rec_a": np.ascontiguousarray(f("w_rec_a")[0]),
        "w_rec_i": np.ascontiguousarray(f("w_rec_i")[0]),
        "w_uq": np.ascontiguousarray(f("w_uq")[0]),
        "w_ukv": np.ascontiguousarray(f("w_ukv")[0]),
        "w_branch": np.ascontiguousarray(f("w_branch")[0]),
        "w_out": np.ascontiguousarray(f("w_out")[0]),
        "w_ffn_in": np.ascontiguousarray(f("w_ffn_in")[0]),
        "w_ffn_out": np.ascontiguousarray(f("w_ffn_out")[0]),
        "cs_k": cs_k,
    }
    in_maps = []
    for core in range(8):
        b, c = core // 2, core % 2
        xb = x[b]
        own = xb.reshape(S // 128, 128, D)[c::2].reshape(OWN, D)
        vecs = vec_common.copy()
        vecs[:, V_SEL] = 1.0 if c == 0 else 0.0
        vecs[:, V_SEL + 1] = 0.0 if c == 0 else 1.0
        if c == 0:
            mk = np.concatenate([dmask, np.zeros_like(dmask)], 1)
        else:
            mk = np.concatenate([np.ones_like(dmask), dmask], 1)
        tiles = np.arange(c, S // 128, 2)
        posi = (NMETA + tiles[:, None] * 128 + np.arange(128)[None, :]).reshape(-1)
        cs_q = np.ascontiguousarray(np.concatenate([cos2[:, posi], sin2[:, posi]], 1))
        m = dict(shared)
        m["xT"] = np.ascontiguousarray(xb.T)
        m["xoT"] = np.ascontiguousarray(own.T)
        m["vecs"] = vecs
        m["masks"] = np.ascontiguousarray(mk)
        m["cs_q"] = cs_q
        in_maps.append(m)
    return in_maps


def assemble(results):
    out = np.empty((NB, S, D), np.float32)
    for core in range(8):
        b, c = core // 2, core % 2
        oT = np.asarray(results[core]["outT"], np.float32)
        o = oT.T.reshape(OWN // 128, 128, D)
        out[b].reshape(S // 128, 128, D)[c::2] = o
    return out


_NC_CACHE = {}


def kernel(**inputs):
    in_maps = make_in_maps(inputs)
    if "nc" not in _NC_CACHE:
        _NC_CACHE["nc"] = Builder().build()
    nc = _NC_CACHE["nc"]
    res = run_bass_kernel_spmd(nc, in_maps, core_ids=list(range(8)))
    return assemble(res.results)
```

```python
import math
from contextlib import ExitStack

import numpy as np
import concourse.bass as bass
import concourse.mybir as mybir
from concourse.bass_utils import run_bass_kernel_spmd

F32 = mybir.dt.float32
BF16 = mybir.dt.bfloat16
AF = mybir.ActivationFunctionType
ALU = mybir.AluOpType

ENGS = ("tensor", "vector", "scalar", "gpsimd", "sync")

D = 1024
S = 8192
NB = 4
NMETA = 16
LT = NMETA + S
DRNN = 1280
NCH = 10
QR = 384
KVR = 256
ROPE = 64
NH = 8
DFF = 2816
NFF = 22
EPS = 1e-6
SCALE = 1.0 / math.sqrt(192.0)
OWN = 4096
BLK = 512
C_X, C_G, C_Q, C_KV, C_KR, C_M = 0, 1280, 2560, 2944, 3200, 3264

V_GMIX, V_GFFN, V_GFIN, V_BGATE = 0, 8, 16, 24
V_CONVW, V_CONVB, V_BA, V_BI, V_LAM = 40, 80, 90, 100, 110
V_GQ, V_GKV, V_SEL = 120, 123, 125
NV = 128


class Counter:
    def __init__(self, sem, name):
        self.sem = sem
        self.name = name
        self.count = 0


class Buf:
    __slots__ = ("name", "w", "r")

    def __init__(self, name=""):
        self.name = name
        self.w = None
        self.r = []


class Phase:
    def __init__(self, nc, ctrs):
        self.nc = nc
        self.ctrs = ctrs
        self.ops = {e: [] for e in ENGS}
        self.seen = {e: {} for e in ENGS}

    def _waits(self, eng, toks):
        seen = self.seen[eng]
        own = self.ctrs[eng] if eng == "tensor" else None
        best = {}
        for t in toks:
            if t is None:
                continue
            c, v = t
            if c is own:
                continue
            if best.get(c, 0) < v:
                best[c] = v
        out = []
        for c, v in best.items():
            if seen.get(c, 0) >= v:
                continue
            seen[c] = v
            out.append((c, v))
        return out

    def op(self, eng, fn, reads=(), writes=(), ctr=None, extra=()):
        toks = list(extra)
        for b in reads:
            toks.append(b.w)
        for b in writes:
            toks.append(b.w)
            toks.extend(b.r)
        waits = self._waits(eng, toks)
        c = ctr if ctr is not None else self.ctrs[eng]
        step = 16 if ctr is not None else 1
        c.count += step
        tok = (c, c.count)
        self.ops[eng].append((fn, waits, (c, step)))
        for b in reads:
            b.r.append(tok)
        for b in writes:
            b.w = tok
            b.r = []
        return tok

    def group(self, eng, fns, reads=(), writes=(), extra=()):
        n = len(fns)
        if n == 1:
            return self.op(eng, fns[0], reads, writes, extra=extra)
        toks = list(extra)
        for b in reads:
            toks.append(b.w)
        for b in writes:
            toks.append(b.w)
            toks.extend(b.r)
        self.ops[eng].append((fns[0], self._waits(eng, toks), None))
        for fn in fns[1:-1]:
            self.ops[eng].append((fn, [], None))
        c = self.ctrs[eng]
        c.count += 1
        tok = (c, c.count)
        self.ops[eng].append((fns[-1], [], (c, 1)))
        for b in reads:
            b.r.append(tok)
        for b in writes:
            b.w = tok
            b.r = []
        return tok

    def dma(self, eng, out, in_, ctr, reads=(), writes=(), extra=()):
        return self.op(eng, lambda e: e.dma_start(out=out, in_=in_), reads, writes, ctr=ctr, extra=extra)

    def emit(self, final_waits=()):
        nc = self.nc
        ops = self.ops
        fw = list(final_waits)
        with nc.Block() as block:
            def run(e, name):
                for fn, waits, inc in ops[name]:
                    for c, v in waits:
                        e.wait_ge(c.sem, v)
                    ins = fn(e)
                    if inc is not None:
                        ins.then_inc(inc[0].sem, inc[1])
                if name == "sync":
                    for c, v in fw:
                        e.wait_ge(c.sem, v)

            @block.tensor
            def _(e):
                run(e, "tensor")

            @block.vector
            def _(e):
                run(e, "vector")

            @block.scalar
            def _(e):
                run(e, "scalar")

            @block.gpsimd
            def _(e):
                run(e, "gpsimd")

            @block.sync
            def _(e):
                run(e, "sync")
        self.ops = {e: [] for e in ENGS}
        self.seen = {e: {} for e in ENGS}


class Builder:
    def __init__(self, debug=False, phases=(1, 2, 3, 4), nheads=None):
        self.debug = debug
        self.nheads = nheads
        self.phases = phases
        self.nc = bass.Bass("TRN2", target_bir_lowering=False)
        self.es = ExitStack()
        self.ndma = 0

    def din(self, name, shape, dt=F32):
        return self.nc.dram_tensor(name, list(shape), dt, kind="ExternalInput").ap()

    def dscratch(self, name, shape, dt):
        if self.debug:
            return self.nc.dram_tensor(name, list(shape), dt, kind="ExternalOutput").ap()
        return self.nc.dram_tensor(name, list(shape), dt).ap()

    def dctr(self):
        self.ndma += 1
        return Counter(self.es.enter_context(self.nc.semaphore("dq%d" % self.ndma)), "dq%d" % self.ndma)

    def sb(self, st, name, shape, dt):
        return st.enter_context(self.nc.sbuf_tensor(name, list(shape), dt))

    def mm(self, out, pairs, wbuf, rbufs, flags=None):
        n = len(pairs)
        fns = []
        for i, (l, r) in enumerate(pairs):
            st = (i == 0) if flags is None else flags[0]
            sp = (i == n - 1) if flags is None else flags[1]
            fns.append(self._mmfn(out, l, r, st, sp))
        return self.ph.group("tensor", fns, reads=rbufs, writes=[wbuf])

    @staticmethod
    def _mmfn(out, l, r, st, sp):
        return lambda e: e.matmul(out, l, r, start=st, stop=sp, skip_group_check=True)

    def act(self, out, in_, func, reads, writes, **kw):
        return self.ph.op("scalar", lambda e: e.activation(out=out, in_=in_, func=func, **kw), reads, writes)

    def tt(self, eng, out, in0, in1, op, reads, writes):
        return self.ph.op(eng, lambda e: e.tensor_tensor(out=out, in0=in0, in1=in1, op=op), reads, writes)

    def ts(self, eng, out, in0, s1, op0, reads, writes, s2=None, op1=None):
        if op1 is None:
            return self.ph.op(eng, lambda e: e.tensor_scalar(out=out, in0=in0, scalar1=s1, scalar2=None, op0=op0), reads, writes)
        return self.ph.op(eng, lambda e: e.tensor_scalar(out=out, in0=in0, scalar1=s1, scalar2=s2, op0=op0, op1=op1), reads, writes)

    def stt(self, out, in0, scalar, in1, op0, op1, reads, writes):
        return self.ph.op("vector", lambda e: e.scalar_tensor_tensor(out=out, in0=in0, scalar=scalar, in1=in1, op0=op0, op1=op1), reads, writes)

    def cp(self, eng, out, in_, reads, writes):
        return self.ph.op(eng, lambda e: e.tensor_copy(out=out, in_=in_), reads, writes)

    def recip(self, out, in_, reads, writes):
        return self.ph.op("vector", lambda e: e.reciprocal(out=out, in_=in_), reads, writes)

    def memset(self, eng, ap, val, writes):
        return self.ph.op(eng, lambda e: e.memset(ap, val), (), writes)

    def load(self, out, in_, wbuf, eng="sync", first=True):
        ctr = self.dctr_for(wbuf)
        if first:
            tok = self.ph.dma(eng, out, in_, ctr, writes=[wbuf])
        else:
            tok = self.ph.dma(eng, out, in_, ctr)
            wbuf.w = tok
        return tok

    def store(self, out, in_, srcbuf, eng="sync"):
        tok = self.ph.dma(eng, out, in_, self.dctr_for(srcbuf), reads=[srcbuf])
        self.pending.append(tok)
        return tok

    def end_phase(self):
        best = {}
        for c, v in self.pending:
            if best.get(c, 0) < v:
                best[c] = v
        self.pending = []
        self.ph.emit(final_waits=[(c, v) for c, v in best.items()])

    def dctr_for(self, buf):
        c = self._bufctr.get(id(buf))
        if c is None:
            c = self.dctr()
            self._bufctr[id(buf)] = c
        return c

    def bank(self, hold=False):
        nb = len(self._banks)
        for _ in range(nb):
            i = self._bank_i
            self._bank_i = (i + 1) % nb
            if id(self._bank_bufs[i]) not in self._held:
                if hold:
                    self._held.add(id(self._bank_bufs[i]))
                return self._banks[i], self._bank_bufs[i]
        raise RuntimeError("all PSUM banks held")

    def rel(self, pb):
        self._held.discard(id(pb))

    def rms8(self, xin, xbuf, N, gcol, sqb, sqbuf, sd, sdbuf, rstd, rsbuf, zT, zbuf, out_f32=None):
        vec = self.vec
        self.act(sqb[:, 0:8 * N], xin[:, 0:8 * N], AF.Square, [xbuf], [sqbuf])
        ps, pb = self.bank()
        self.mm(ps[:, 0:N], [(self.ones[:, :], sqb[:, c * N:(c + 1) * N]) for c in range(8)], pb, [sqbuf, self.cbuf])
        self.act(sd[:, 0:N], ps[:, 0:N], AF.Sqrt, [pb], [sdbuf], scale=1.0 / D, bias=self.epsc[:, 0:1])
        self.recip(rstd[:, 0:N], sd[:, 0:N], [sdbuf], [rsbuf])
        for c in range(8):
            self.stt(zT[:, c * N:(c + 1) * N], xin[:, c * N:(c + 1) * N], vec[:, gcol + c:gcol + c + 1], rstd[:, 0:N],
                     ALU.mult, ALU.mult, [xbuf, rsbuf, self.cbuf], [zbuf])

    def build(self):
        nc = self.nc
        es = self.es
        self._bufctr = {}
        self.pending = []
        xT = self.din("xT", [D, S])
        xoT = self.din("xoT", [D, OWN])
        metaT = self.din("metaT", [D, NMETA])
        w_in = self.din("w_in", [D, 5312])
        w_ra = self.din("w_rec_a", [NCH, 128, 128])
        w_ri = self.din("w_rec_i", [NCH, 128, 128])
        w_uq = self.din("w_uq", [QR, 1536])
        w_ukv = self.din("w_ukv", [KVR, 2048])
        w_br = self.din("w_branch", [2304, D])
        w_o = self.din("w_out", [D, D])
        w_f1 = self.din("w_ffn_in", [D, 2 * DFF])
        w_f2 = self.din("w_ffn_out", [DFF, D])
        vecs = self.din("vecs", [128, NV])
        masks = self.din("masks", [128, 256])
        cs_k = self.din("cs_k", [64, 2 * LT])
        cs_q = self.din("cs_q", [64, 2 * OWN])
        outT = nc.dram_tensor("outT", [D, OWN], F32, kind="ExternalOutput").ap()
        ckv_s = self.dscratch("ckv_s", [KVR, LT], BF16)
        kr_s = self.dscratch("kr_s", [ROPE, LT], BF16)
        yrnn_s = self.dscratch("yrnn_s", [DRNN, OWN], BF16)
        yatt_s = self.dscratch("yatt_s", [D, OWN], BF16)
        hmid_s = self.dscratch("hmid_s", [D, OWN], F32)
        self.sc_bufs = {k: Buf(k) for k in ["ckv", "kr", "yrnn", "yatt", "hmid", "out"]}

        self.ctrs = {e: Counter(es.enter_context(nc.semaphore("c_" + e)), e) for e in ENGS}
        self.ph = Phase(nc, self.ctrs)
        self._banks = [es.enter_context(nc.psum_tensor("pb%d" % i, [128, 512], F32)) for i in range(8)]
        self._bank_bufs = [Buf("pb%d" % i) for i in range(8)]
        self._bank_i = 0
        self._held = set()
        self.vec = self.sb(es, "vec", [128, NV], F32)
        self.vec2 = self.sb(es, "vec2", [128, 64], F32)
        self.ones = self.sb(es, "ones", [128, 128], BF16)
        self.epsc = self.sb(es, "epsc", [128, 1], F32)
        self.cbuf = Buf("consts")
        vec, vec2 = self.vec, self.vec2
        ph = self.ph
        self.load(vec[:, :], vecs, self.cbuf)
        self.memset("gpsimd", self.ones[:, :], 1.0, [self.cbuf])
        self.memset("gpsimd", self.epsc[:, :], EPS, [self.cbuf])
        self.ts("vector", vec2[:, 0:10], vec[:, V_BA:V_BA + 10], 0.5, ALU.mult, [self.cbuf], [self.cbuf])
        self.ts("vector", vec2[:, 10:20], vec[:, V_BI:V_BI + 10], 0.5, ALU.mult, [self.cbuf], [self.cbuf])
        self.act(vec2[:, 40:50], vec[:, V_LAM:V_LAM + 10], AF.Exp, [self.cbuf], [self.cbuf], scale=-1.0)
        self.act(vec2[:, 50:60], vec2[:, 40:50], AF.Ln, [self.cbuf], [self.cbuf], bias=1.0)
        self.ts("vector", vec2[:, 20:30], vec2[:, 50:60], -4.0, ALU.mult, [self.cbuf], [self.cbuf])
        self.ts("vector", vec2[:, 30:40], vec2[:, 50:60], -8.0, ALU.mult, [self.cbuf], [self.cbuf])

        if 1 in self.phases:
            self.phase1(xT, metaT, w_in, w_ra, w_ri, cs_k, ckv_s, kr_s, yrnn_s)
        if 2 in self.phases:
            self.phase2(xoT, w_in, w_uq, w_ukv, masks, cs_q, ckv_s, kr_s, yatt_s)
        if 3 in self.phases:
            self.phase3(xoT, w_in, w_br, w_o, yrnn_s, yatt_s, hmid_s)
        if 4 in self.phases:
            self.phase4(w_f1, w_f2, hmid_s, outT)
        else:
            with ExitStack() as st:
                z = self.sb(st, "zz", [128, 512], F32)
                zb = Buf("zz")
                self.memset("vector", z[:, :], 0.0, [zb])
                ov = outT.rearrange("(c p) t -> p c t", p=128)
                for c in range(8):
                    for t in range(OWN // 512):
                        self.store(ov[:, c, t * 512:(t + 1) * 512], z[:, :], zb)
                self.end_phase()
        es.close()
        return nc

    def wload(self, dst, src, buf):
        tok = self.ph.dma("gpsimd", dst, src, self.dctr_for(buf))
        buf.w = tok
        return tok

    def phase1(self, xT, metaT, w_in, w_ra, w_ri, cs_k, ckv_s, kr_s, yrnn_s):
        nc, ph, vec, vec2 = self.nc, self.ph, self.vec, self.vec2
        G = 5
        with ExitStack() as st:
            Wx = self.sb(st, "Wx", [128, 8 * DRNN], BF16)
            Wg = self.sb(st, "Wg", [128, 8 * DRNN], BF16)
            Wkv = self.sb(st, "Wkv", [128, 8 * KVR], BF16)
            Wkr = self.sb(st, "Wkr", [128, 8 * 128], BF16)
            Wa = self.sb(st, "Wa", [128, NCH * 128], BF16)
            Wi = self.sb(st, "Wi", [128, NCH * 128], BF16)
            wkvb, wxb, wab, wgb = Buf("wkv"), Buf("wx"), Buf("wa"), Buf("wg")
            w3 = w_in.rearrange("(k p) n -> p k n", p=128)
            for k in range(8):
                self.wload(Wkv[:, k * KVR:(k + 1) * KVR], w3[:, k, C_KV:C_KV + KVR], wkvb)
                self.wload(Wkr[:, k * 128:k * 128 + 64], w3[:, k, C_KR:C_KR + 64], wkvb)
                self.wload(Wkr[:, k * 128 + 64:k * 128 + 96], w3[:, k, C_KR + 32:C_KR + 64], wkvb)
                self.wload(Wkr[:, k * 128 + 96:k * 128 + 128], w3[:, k, C_KR:C_KR + 32], wkvb)
            for k in range(8):
                self.wload(Wx[:, k * DRNN:(k + 1) * DRNN], w3[:, k, C_X:C_X + DRNN], wxb)
            for j in range(NCH):
                self.wload(Wa[:, j * 128:(j + 1) * 128], w_ra[j], wab)
                self.wload(Wi[:, j * 128:(j + 1) * 128], w_ri[j], wab)
            for k in range(8):
                self.wload(Wg[:, k * DRNN:(k + 1) * DRNN], w3[:, k, C_G:C_G + DRNN], wgb)

            N = BLK
            xin = self.sb(st, "xin", [128, 8 * N], F32); xb = Buf("xin")
            sqb = self.sb(st, "sqb", [128, 8 * N], BF16); sqbuf = Buf("sqb")
            sd = self.sb(st, "sd", [128, N], F32); sdb = Buf("sd")
            rstd = self.sb(st, "rstd", [128, N], F32); rsb = Buf("rstd")
            zT = self.sb(st, "zT", [128, 8 * N], BF16); zb = Buf("zT")
            ux = self.sb(st, "ux", [128, NCH * (N + 3)], F32); uxb = [Buf("ux%d" % j) for j in range(NCH)]
            xc = self.sb(st, "xc", [128, G * N], F32); xcb_ = [Buf("xc%d" % j) for j in range(G)]
            xcb = self.sb(st, "xcb", [128, G * N], BF16); xcbb = [Buf("xcb%d" % j) for j in range(G)]
            T1 = self.sb(st, "T1", [128, G * N], F32); T1b = [Buf("T1_%d" % j) for j in range(G)]
            T2 = self.sb(st, "T2", [128, G * N], F32); T2b = [Buf("T2_%d" % j) for j in range(G)]
            T3 = self.sb(st, "T3", [128, G * N], F32); T3b = [Buf("T3_%d" % j) for j in range(G)]
            T4 = self.sb(st, "T4", [128, G * N], F32); T4b = [Buf("T4_%d" % j) for j in range(G)]
            YT = self.sb(st, "YT", [128, G * (N // 2)], F32); YTb = [Buf("YT%d" % j) for j in range(G)]
            YU = self.sb(st, "YU", [128, G * (N // 2)], F32); YUb = [Buf("YU%d" % j) for j in range(G)]
            YO = self.sb(st, "YO", [128, G * (N // 2)], BF16); YOb = Buf("YO")
            hst = self.sb(st, "hst", [128, NCH], F32); hstb = [Buf("hst%d" % j) for j in range(NCH)]
            cs = self.sb(st, "cs", [64, 2 * N], F32); csb = Buf("cs")
            kvq = self.sb(st, "kvq", [128, 2 * N], BF16); kvqb = Buf("kvq")
            sd2 = self.sb(st, "sd2", [128, N], F32); sd2b = Buf("sd2")
            rs2 = self.sb(st, "rs2", [128, N], F32); rs2b = Buf("rs2")
            kvo = self.sb(st, "kvo", [128, 2 * N], BF16); kvob = Buf("kvo")
            kt1 = self.sb(st, "kt1", [64, N], F32); kt1b = Buf("kt1")
            kt2 = self.sb(st, "kt2", [64, N], F32); kt2b = Buf("kt2")
            kro = self.sb(st, "kro", [64, N], BF16); krob = Buf("kro")

            self.memset("gpsimd", ux[:, :], 0.0, uxb)
            self.memset("gpsimd", hst[:, :], 0.0, hstb)

            x3 = xT.rearrange("(c p) t -> p c t", p=128)
            m3 = metaT.rearrange("(c p) t -> p c t", p=128)
            ckv3 = ckv_s.rearrange("(c p) t -> p c t", p=128)
            yr3 = yrnn_s.rearrange("(c p) t -> p c t", p=128)
            nblk = S // N

            def load_block(bi):
                if bi == 0:
                    n = NMETA
                    for c in range(8):
                        self.load(xin[:, c * n:(c + 1) * n], m3[:, c, :], xb, first=(c == 0))
                    t0 = 0
                else:
                    n = N
                    for c in range(8):
                        self.load(xin[:, c * n:(c + 1) * n], x3[:, c, (bi - 1) * N:bi * N], xb, first=(c == 0))
                    t0 = NMETA + (bi - 1) * N
                return n, t0

            def load_cs(bi):
                n_ = NMETA if bi == 0 else N
                t_ = 0 if bi == 0 else NMETA + (bi - 1) * N
                self.load(cs[:, 0:n_], cs_k[:, t_:t_ + n_], csb)
                self.load(cs[:, N:N + n_], cs_k[:, LT + t_:LT + t_ + n_], csb, first=False)

            zT2 = self.sb(st, "zTb", [128, 8 * N], BF16)
            zTs = [zT, zT2]
            zbs = [zb, Buf("zTb")]

            def pre_steps(bi):
                n = NMETA if bi == 0 else N
                t0 = 0 if bi == 0 else NMETA + (bi - 1) * N
                zT_, zb_ = zTs[bi % 2], zbs[bi % 2]
                zc = [zT_[:, k * n:(k + 1) * n] for k in range(8)]
                stt_ = {}

                def s1():
                    self.act(sqb[:, 0:8 * n], xin[:, 0:8 * n], AF.Square, [xb], [sqbuf])
                    ps, pb = self.bank(hold=True)
                    self.mm(ps[:, 0:n], [(self.ones[:, :], sqb[:, c * n:(c + 1) * n]) for c in range(8)], pb, [sqbuf, self.cbuf])
                    stt_["ss"] = (ps, pb)

                def s2():
                    ps, pb = stt_["ss"]
                    self.act(sd[:, 0:n], ps[:, 0:n], AF.Sqrt, [pb], [sdb], scale=1.0 / D, bias=self.epsc[:, 0:1])
                    self.rel(pb)
                    self.recip(rstd[:, 0:n], sd[:, 0:n], [sdb], [rsb])

                def s3():
                    for c in range(8):
                        self.stt(zT_[:, c * n:(c + 1) * n], xin[:, c * n:(c + 1) * n], vec[:, V_GMIX + c:V_GMIX + c + 1], rstd[:, 0:n],
                                 ALU.mult, ALU.mult, [xb, rsb, self.cbuf], [zb_])
                    if bi < nblk:
                        load_block(bi + 1)

                def s4():
                    pk = []
                    for c2 in range(2):
                        ps, pb = self.bank(hold=True)
                        self.mm(ps[:, 0:n], [(Wkv[:, k * KVR + c2 * 128:k * KVR + (c2 + 1) * 128], zc[k]) for k in range(8)], pb, [zb_, wkvb])
                        pk.append((ps, pb))
                    stt_["pk"] = pk
                    for c2 in range(2):
                        self.act(kvq[:, c2 * n:(c2 + 1) * n], pk[c2][0][:, 0:n], AF.Square, [pk[c2][1]], [kvqb])
                    ps, pb = self.bank(hold=True)
                    self.mm(ps[:, 0:n], [(self.ones[:, :], kvq[:, c2 * n:(c2 + 1) * n]) for c2 in range(2)], pb, [kvqb, self.cbuf])
                    stt_["ss2"] = (ps, pb)

                def s5():
                    ps, pb = stt_["ss2"]
                    self.act(sd2[:, 0:n], ps[:, 0:n], AF.Sqrt, [pb], [sd2b], scale=1.0 / KVR, bias=self.epsc[:, 0:1])
                    self.rel(pb)
                    self.recip(rs2[:, 0:n], sd2[:, 0:n], [sd2b], [rs2b])

                def s6():
                    pk = stt_["pk"]
                    for c2 in range(2):
                        self.stt(kvo[:, c2 * n:(c2 + 1) * n], pk[c2][0][:, 0:n], vec[:, V_GKV + c2:V_GKV + c2 + 1], rs2[:, 0:n],
                                 ALU.mult, ALU.mult, [pk[c2][1], rs2b, self.cbuf], [kvob])
                        self.rel(pk[c2][1])
                        self.store(ckv3[:, c2, t0:t0 + n], kvo[:, c2 * n:(c2 + 1) * n], kvob)

                def s7():
                    pr = []
                    for hh in range(2):
                        ps, pb = self.bank()
                        self.mm(ps[0:64, 0:n], [(Wkr[:, k * 128 + hh * 64:k * 128 + (hh + 1) * 64], zc[k]) for k in range(8)], pb, [zb_, wkvb])
                        pr.append((ps, pb))
                    self.tt("vector", kt1[:, 0:n], pr[0][0][0:64, 0:n], cs[:, 0:n], ALU.mult, [pr[0][1], csb], [kt1b])
                    self.tt("vector", kt2[:, 0:n], pr[1][0][0:64, 0:n], cs[:, N:N + n], ALU.mult, [pr[1][1], csb], [kt2b])
                    self.tt("gpsimd", kro[:, 0:n], kt1[:, 0:n], kt2[:, 0:n], ALU.add, [kt1b, kt2b], [krob])
                    self.store(kr_s[:, t0:t0 + n], kro[:, 0:n], krob)
                    if bi < nblk:
                        load_cs(bi + 1)
                return [s1, s2, s3, s4, s5, s6, s7]

            grp_state = {}

            def gctx(bi, g0):
                n = NMETA if bi == 0 else N
                zT_, zb_ = zTs[bi % 2], zbs[bi % 2]
                zc = [zT_[:, k * n:(k + 1) * n] for k in range(8)]
                st_ = grp_state.setdefault((bi, g0), {"pu": {}, "pg": {}, "pa": {}, "pi": {}})
                return n, zb_, zc, list(range(g0, g0 + G)), st_

            def front1(bi, g0):
                n, zb_, zc, chs, st_ = gctx(bi, g0)
                pu = st_["pu"]
                W = N + 3

                def issue_ux(j):
                    ps, pb = self.bank(hold=True)
                    self.mm(ps[:, 0:n], [(Wx[:, k * DRNN + j * 128:k * DRNN + (j + 1) * 128], zc[k]) for k in range(8)], pb, [zb_, wxb])
                    pu[j] = (ps, pb)

                def cast(jj):
                    self.act(xcb[:, jj * N:jj * N + n], xc[:, jj * N:jj * N + n], AF.Identity, [xcb_[jj]], [xcbb[jj]])

                def chunk(jj, j):
                    def f():
                        if jj == 0:
                            issue_ux(chs[0])
                        if jj + 1 < G:
                            issue_ux(chs[jj + 1])
                        ps, pb = pu[j]
                        o = j * W
                        self.act(ux[:, o + 3:o + 3 + n], ps[:, 0:n], AF.Identity, [pb], [uxb[j]])
                        self.rel(pb)
                        xcj = xc[:, jj * N:jj * N + n]
                        self.act(xcj, ux[:, o:o + n], AF.Identity, [uxb[j], self.cbuf], [xcb_[jj]],
                                 scale=vec[:, V_CONVW + j:V_CONVW + j + 1], bias=vec[:, V_CONVB + j:V_CONVB + j + 1])
                        if jj >= 1:
                            cast(jj - 1)
                        for kk in range(1, 4):
                            self.stt(xcj, ux[:, o + kk:o + kk + n], vec[:, V_CONVW + kk * 10 + j:V_CONVW + kk * 10 + j + 1], xcj,
                                     ALU.mult, ALU.add, [uxb[j], self.cbuf], [xcb_[jj]])
                        self.cp("gpsimd", ux[:, o:o + 3], ux[:, o + n:o + n + 3], [], [uxb[j]])
                        if jj == G - 1:
                            cast(jj)
                    return f
                return [chunk(jj, j) for jj, j in enumerate(chs)]

            def front2(bi, g0, steps=()):
                n, zb_, zc, chs, st_ = gctx(bi, g0)
                pa, pi, pg = st_["pa"], st_["pi"], st_["pg"]
                own = bi > 0
                for f in steps:
                    f()

                def issue_gates(jj, j):
                    pa_, pab = self.bank(hold=True)
                    self.mm(pa_[:, 0:n], [(Wa[:, j * 128:(j + 1) * 128], xcb[:, jj * N:jj * N + n])], pab, [xcbb[jj], wab])
                    pi_, pib = self.bank(hold=True)
                    self.mm(pi_[:, 0:n], [(Wi[:, j * 128:(j + 1) * 128], xcb[:, jj * N:jj * N + n])], pib, [xcbb[jj], wab])
                    pa[j] = (pa_, pab)
                    pi[j] = (pi_, pib)
                issue_gates(0, chs[0])
                for jj, j in enumerate(chs):
                    if jj + 1 < G:
                        issue_gates(jj + 1, chs[jj + 1])
                    sl = slice(jj * N, jj * N + n)
                    self.act(T1[:, sl], pa[j][0][:, 0:n], AF.Tanh, [pa[j][1], self.cbuf], [T1b[jj]], scale=0.5, bias=vec2[:, j:j + 1])
                    self.act(T3[:, sl], pi[j][0][:, 0:n], AF.Tanh, [pi[j][1], self.cbuf], [T3b[jj]], scale=0.5, bias=vec2[:, 10 + j:11 + j])
                    self.rel(pa[j][1])
                    self.rel(pi[j][1])
                if own:
                    for j in chs:
                        ps, pb = self.bank(hold=True)
                        self.mm(ps[:, 0:n], [(Wg[:, k * DRNN + j * 128:k * DRNN + (j + 1) * 128], zc[k]) for k in range(8)], pb, [zb_, wgb])
                        pg[j] = (ps, pb)
                for jj, j in enumerate(chs):
                    sl = slice(jj * N, jj * N + n)
                    self.stt(T3[:, sl], T3[:, sl], 1.0, xc[:, sl], ALU.add, ALU.mult, [xcb_[jj]], [T3b[jj]])

            def back(bi, g0):
                n, zb_, zc, chs, st_ = gctx(bi, g0)
                pg = st_["pg"]
                own = bi > 0
                L = []

                def mk(fn, *a):
                    return lambda: fn(*a)

                def c_(jj, j):
                    sl = slice(jj * N, jj * N + n)
                    self.act(T2[:, sl], T1[:, sl], AF.Exp, [T1b[jj], self.cbuf], [T2b[jj]], scale=vec2[:, 30 + j:31 + j], bias=vec2[:, 30 + j:31 + j])
                    self.act(T1[:, sl], T1[:, sl], AF.Exp, [self.cbuf], [T1b[jj]], scale=vec2[:, 20 + j:21 + j], bias=vec2[:, 20 + j:21 + j])

                def d_(jj, j):
                    sl = slice(jj * N, jj * N + n)
                    self.act(T2[:, sl], T2[:, sl], AF.Sqrt, [], [T2b[jj]], scale=-0.25, bias=0.25)

                def e1_(jj, j):
                    sl = slice(jj * N, jj * N + n)
                    self.tt("gpsimd", T2[:, sl], T2[:, sl], T3[:, sl], ALU.mult, [T3b[jj]], [T2b[jj]])

                def e2_(jj, j):
                    sl = slice(jj * N, jj * N + n)
                    ph.op("vector", self._scanfn(T4[:, sl], T1[:, sl], T2[:, sl], hst[:, j:j + 1]),
                          [T1b[jj], T2b[jj], hstb[j]], [T4b[jj]])

                def e3_(jj, j):
                    self.cp("gpsimd", hst[:, j:j + 1], T4[:, jj * N + n - 1:jj * N + n], [T4b[jj]], [hstb[j]])

                def f_(jj, j):
                    sl = slice(jj * N, jj * N + n)
                    self.act(T3[:, sl], pg[j][0][:, 0:n], AF.Gelu_apprx_tanh, [pg[j][1]], [T3b[jj]])
                    self.rel(pg[j][1])

                h2 = N // 2

                def h_(jj, j):
                    sl = slice(jj * N, jj * N + n)
                    gv = T3[:, sl].rearrange("p (a e t) -> p a e t", a=2, e=2)
                    hv = T4[:, sl].rearrange("p (a e t) -> p a e t", a=2, e=2)
                    ytv = YT[:, jj * h2:(jj + 1) * h2].rearrange("p (a t) -> p a t", a=2)
                    yuv = YU[:, jj * h2:(jj + 1) * h2].rearrange("p (a t) -> p a t", a=2)
                    yov = YO[:, jj * h2:(jj + 1) * h2].rearrange("p (a t) -> p a t", a=2)
                    self.stt(ytv, gv[:, :, 0, :], vec[:, V_SEL:V_SEL + 1], hv[:, :, 0, :], ALU.mult, ALU.mult,
                             [T3b[jj], T4b[jj], self.cbuf], [YTb[jj]])
                    self.stt(yuv, gv[:, :, 1, :], vec[:, V_SEL + 1:V_SEL + 2], hv[:, :, 1, :], ALU.mult, ALU.mult,
                             [T3b[jj], T4b[jj], self.cbuf], [YUb[jj]])
                    self.tt("gpsimd", yov, ytv, yuv, ALU.add, [YTb[jj], YUb[jj]], [YOb])

                def st_(jj, j):
                    ob0 = (bi - 1) * h2
                    self.store(yr3[:, j, ob0:ob0 + h2], YO[:, jj * h2:(jj + 1) * h2], YOb)
                stages = [c_, d_, e1_, e2_, e3_] + ([f_, h_, st_] if own else [])
                for fn in stages:
                    for jj, j in enumerate(chs):
                        L.append(mk(fn, jj, j))
                return L

            def merge(Lb, La):
                nb_, na_ = len(Lb), len(La)
                if na_ == 0:
                    for f in Lb:
                        f()
                    return
                every = max(1, nb_ // (na_ + 1))
                ia = 0
                for i, f in enumerate(Lb):
                    f()
                    if ia < na_ and (i + 1) % every == 0:
                        La[ia]()
                        ia += 1
                while ia < na_:
                    La[ia]()
                    ia += 1

            groups = [(bi, g0) for bi in range(nblk + 1) for g0 in (0, G)]
            load_block(0)
            load_cs(0)
            for f in pre_steps(0):
                f()
            for f in front1(*groups[0]):
                f()
            front2(*groups[0])
            pend_pre = []
            for gi_, (bi, g0) in enumerate(groups):
                Lb = back(bi, g0)
                La = []
                late = []
                if gi_ + 1 < len(groups):
                    La = front1(*groups[gi_ + 1])
                if g0 == 0 and bi < nblk:
                    ps_ = pre_steps(bi + 1)
                    La = [La[0], ps_[0], La[1], ps_[1], La[2], ps_[2]] + La[3:]
                    late = ps_[3:]
                merge(Lb, La)
                if gi_ + 1 < len(groups):
                    front2(*groups[gi_ + 1], steps=late)
                else:
                    for f in late:
                        f()
            self.end_phase()

    @staticmethod
    def _scanfn(out, d0, d1, init):
        return lambda e: e.tensor_tensor_scan(out=out, data0=d0, data1=d1, initial=init, op0=ALU.mult, op1=ALU.add)

    def phase2(self, xoT, w_in, w_uq, w_ukv, masks, cs_q, ckv_s, kr_s, yatt_s):
        nc, ph, vec = self.nc, self.ph, self.vec
        N = BLK
        NT = 65
        with ExitStack() as st:
            ckvT = self.sb(st, "ckvT", [128, 2 * LT], BF16); ckvb = Buf("ckvT")
            krT = self.sb(st, "krT", [128, LT], BF16); krb = Buf("krT")
            qnT = self.sb(st, "qnT", [128, 3 * OWN], BF16); qnb = [Buf("qn%d" % i) for i in range(OWN // N)]
            csq = self.sb(st, "csq", [64, 2 * OWN], F32); csqb = Buf("csq")
            Wuq = self.sb(st, "Wuq", [128, 3 * 1536], BF16)
            Wus = self.sb(st, "Wus", [128, 3 * 512], BF16)
            Wukv = self.sb(st, "Wukv", [128, 2 * 2048], BF16)
            Mk = self.sb(st, "Mk", [128, 256], BF16)
            wb = Buf("w2")
            ckv3 = ckv_s.rearrange("(c p) t -> p c t", p=128)
            for c2 in range(2):
                self.load(ckvT[:, c2 * LT:(c2 + 1) * LT], ckv3[:, c2, :], ckvb, first=(c2 == 0))
            self.memset("gpsimd", krT[64:128, :], 0.0, [krb])
            self.load(krT[0:64, :], kr_s, krb)
            self.load(csq[:, :], cs_q, csqb)
            with ExitStack() as st2:
                Wq = self.sb(st2, "Wq", [128, 8 * QR], BF16)
                w3 = w_in.rearrange("(k p) n -> p k n", p=128)
                for k in range(8):
                    self.wload(Wq[:, k * QR:(k + 1) * QR], w3[:, k, C_Q:C_Q + QR], wb)
                xin = self.sb(st2, "xin2", [128, 8 * N], F32); xb = Buf("xin2")
                sqb = self.sb(st2, "sqb2", [128, 8 * N], BF16); sqbuf = Buf("sqb2")
                sd = self.sb(st2, "sd_2", [128, N], F32); sdb = Buf("sd_2")
                rstd = self.sb(st2, "rstd2", [128, N], F32); rsb = Buf("rstd2")
                zT = self.sb(st2, "zT2", [128, 8 * N], BF16); zb = Buf("zT2")
                qq = self.sb(st2, "qq", [128, 3 * N], BF16); qqb = Buf("qq")
                sd3 = self.sb(st2, "sd3", [128, N], F32); sd3b = Buf("sd3")
                rs3 = self.sb(st2, "rs3", [128, N], F32); rs3b = Buf("rs3")
                xo3 = xoT.rearrange("(c p) t -> p c t", p=128)
                for c in range(8):
                    self.load(xin[:, c * N:(c + 1) * N], xo3[:, c, 0:N], xb, first=(c == 0))
                for ob in range(OWN // N):
                    self.rms8(xin, xb, N, V_GMIX, sqb, sqbuf, sd, sdb, rstd, rsb, zT, zb)
                    if ob + 1 < OWN // N:
                        for c in range(8):
                            self.load(xin[:, c * N:(c + 1) * N], xo3[:, c, (ob + 1) * N:(ob + 2) * N], xb, first=(c == 0))
                    pq = []
                    for c3 in range(3):
                        ps, pb = self.bank()
                        self.mm(ps[:, 0:N], [(Wq[:, k * QR + c3 * 128:k * QR + (c3 + 1) * 128], zT[:, k * N:(k + 1) * N]) for k in range(8)], pb, [zb, wb])
                        pq.append((ps, pb))
                    for c3 in range(3):
                        self.act(qq[:, c3 * N:(c3 + 1) * N], pq[c3][0][:, 0:N], AF.Square, [pq[c3][1]], [qqb])
                    ps, pb = self.bank()
                    self.mm(ps[:, 0:N], [(self.ones[:, :], qq[:, c3 * N:(c3 + 1) * N]) for c3 in range(3)], pb, [qqb, self.cbuf])
                    self.act(sd3[:, :], ps[:, 0:N], AF.Sqrt, [pb], [sd3b], scale=1.0 / QR, bias=self.epsc[:, 0:1])
                    self.recip(rs3[:, :], sd3[:, :], [sd3b], [rs3b])
                    for c3 in range(3):
                        self.stt(qnT[:, c3 * OWN + ob * N:c3 * OWN + (ob + 1) * N], pq[c3][0][:, 0:N], vec[:, V_GQ + c3:V_GQ + c3 + 1], rs3[:, :],
                                 ALU.mult, ALU.mult, [pq[c3][1], rs3b, self.cbuf], [qnb[ob]])
                self.end_phase()
            uq3 = w_uq.rearrange("(k p) n -> p k n", p=128)
            for k in range(3):
                self.wload(Wuq[:, k * 1536:(k + 1) * 1536], uq3[:, k, :], wb)
                src = uq3[:, k, :].rearrange("p (h d) -> p h d", d=192)
                dst = Wus[:, k * 512:(k + 1) * 512].rearrange("p (h d) -> p h d", d=64)
                self.wload(dst[:, :, 0:32], src[:, :, 160:192], wb)
                self.wload(dst[:, :, 32:64], src[:, :, 128:160], wb)
            kv3 = w_ukv.rearrange("(k p) n -> p k n", p=128)
            for k in range(2):
                self.wload(Wukv[:, k * 2048:(k + 1) * 2048], kv3[:, k, :], wb)
            self.wload(Mk[:, :], masks, wb)
            KT = [self.sb(st, "KT%d" % i, [128, LT], BF16) for i in range(2)]; KTb = [Buf("KT%d" % i) for i in range(2)]
            Vh = [self.sb(st, "Vh%d" % i, [128, NT * 128], BF16) for i in range(2)]; Vb = [Buf("Vh%d" % i) for i in range(2)]
            QT = [self.sb(st, "QT%d" % i, [128, N], BF16) for i in range(2)]; QTb = [Buf("QT%d" % i) for i in range(2)]
            QRt = [self.sb(st, "QR%d" % i, [128, N], BF16) for i in range(2)]; QRb = [Buf("QR%d" % i) for i in range(2)]
            for i in range(2):
                self.memset("gpsimd", QRt[i][64:128, :], 0.0, [QRb[i]])
            qt1 = self.sb(st, "qt1", [64, N], F32); qt1b = Buf("qt1")
            qt2 = self.sb(st, "qt2", [64, N], F32); qt2b = Buf("qt2")
            NP = 4
            Pt = [self.sb(st, "Pt%d" % i, [128, N], BF16) for i in range(NP)]; Ptb = [Buf("Pt%d" % i) for i in range(NP)]
            rden = self.sb(st, "rden", [128, N], F32); rdb = Buf("rden")
            Ob = [self.sb(st, "Ob%d" % i, [128, N], BF16) for i in range(2)]; Obb = [Buf("Ob%d" % i) for i in range(2)]
            ya3 = yatt_s.rearrange("(c p) t -> p c t", p=128)
            allbanks, allbufs = self._banks, self._bank_bufs
            self._banks, self._bank_bufs, self._bank_i = allbanks[0:4], allbufs[0:4], 0
            OTs = [(allbanks[4], allbufs[4]), (allbanks[5], allbufs[5])]
            DNs = [(allbanks[6], allbufs[6]), (allbanks[7], allbufs[7])]
            nheads = NH if self.nheads is None else self.nheads
            NG = OWN // N

            def proj_tasks(h):
                KTh, KTbh, Vhh, Vbh = KT[h % 2], KTb[h % 2], Vh[h % 2], Vb[h % 2]
                tasks = []

                def ktask(c0, n):
                    def f():
                        ps, pb = self.bank()
                        self.mm(ps[:, 0:n], [(Wukv[:, k * 2048 + h * 256:k * 2048 + h * 256 + 128], ckvT[:, k * LT + c0:k * LT + c0 + n]) for k in range(2)],
                                pb, [ckvb, wb])
                        self.cp("vector", KTh[:, c0:c0 + n], ps[:, 0:n], [pb], [KTbh])
                    return f

                def vtask0():
                    ps, pb = self.bank()
                    self.mm(ps[0:NMETA, 0:128], [(ckvT[:, k * LT:k * LT + NMETA], Wukv[:, k * 2048 + h * 256 + 128:k * 2048 + h * 256 + 256]) for k in range(2)],
                            pb, [ckvb, wb])
                    self.cp("vector", Vhh[0:NMETA, 0:128], ps[0:NMETA, 0:128], [pb], [Vbh])

                def vtask(t4):
                    def f():
                        ps, pb = self.bank()
                        for q4 in range(4):
                            c0 = NMETA + (t4 + q4) * 128
                            self.mm(ps[:, q4 * 128:(q4 + 1) * 128],
                                    [(ckvT[:, k * LT + c0:k * LT + c0 + 128], Wukv[:, k * 2048 + h * 256 + 128:k * 2048 + h * 256 + 256]) for k in range(2)],
                                    pb, [ckvb, wb])
                        self.cp("vector", Vhh[:, (1 + t4) * 128:(5 + t4) * 128], ps[:, 0:512], [pb], [Vbh])
                    return f
                tasks.append(ktask(0, NMETA))
                tasks.append(vtask0)
                for i in range(S // N):
                    tasks.append(ktask(NMETA + i * N, N))
                    tasks.append(vtask(4 * i))
                return tasks

            def qproj(h, g, qi):
                q0 = g * N
                ps, pb = self.bank()
                self.mm(ps[:, 0:N], [(Wuq[:, k * 1536 + h * 192:k * 1536 + h * 192 + 128], qnT[:, k * OWN + q0:k * OWN + q0 + N]) for k in range(3)],
                        pb, [qnb[g], wb])
                self.cp("vector", QT[qi][:, :], ps[:, 0:N], [pb], [QTb[qi]])
                psa, pba = self.bank()
                self.mm(psa[0:64, 0:N], [(Wuq[:, k * 1536 + h * 192 + 128:k * 1536 + h * 192 + 192], qnT[:, k * OWN + q0:k * OWN + q0 + N]) for k in range(3)],
                        pba, [qnb[g], wb])
                self.tt("vector", qt1[:, :], psa[0:64, 0:N], csq[:, q0:q0 + N], ALU.mult, [pba, csqb], [qt1b])
                psb, pbb = self.bank()
                self.mm(psb[0:64, 0:N], [(Wus[:, k * 512 + h * 64:k * 512 + (h + 1) * 64], qnT[:, k * OWN + q0:k * OWN + q0 + N]) for k in range(3)],
                        pbb, [qnb[g], wb])
                self.tt("vector", qt2[:, :], psb[0:64, 0:N], csq[:, OWN + q0:OWN + q0 + N], ALU.mult, [pbb, csqb], [qt2b])
                self.tt("gpsimd", QRt[qi][0:64, :], qt1[:, :], qt2[:, :], ALU.add, [qt1b, qt2b], [QRb[qi]])

            groups = [(h, g) for h in range(nheads) for g in range(NG)]
            units = []
            for gi_, (h, g) in enumerate(groups):
                nkt = 1 + 8 * g + 8
                for kt in range(nkt):
                    units.append((gi_, h, g, kt, nkt))
            for t in proj_tasks(0):
                t()
            qproj(groups[0][0], groups[0][1], 0)
            pend = []
            pend_every = 1
            DPT = 2
            nU = len(units)
            live = {}
            for step in range(nU + DPT):
                if step < nU:
                    gi_, h, g, kt, nkt = units[step]
                    qi = gi_ % 2
                    if kt == 0 and g == 0 and h + 1 < nheads:
                        pend = proj_tasks(h + 1)
                        nun = sum(1 + 8 * g2 + 8 for g2 in range(NG))
                        pend_every = max(1, (nun - 40) // len(pend))
                    if kt == nkt // 2 and gi_ + 1 < len(groups):
                        qproj(groups[gi_ + 1][0], groups[gi_ + 1][1], (gi_ + 1) % 2)
                    if kt == 0:
                        k0, nk, qs, mask = 0, NMETA, 0, None
                    else:
                        s_ = kt - 1
                        k0, nk = NMETA + s_ * 128, 128
                        d = s_ - 8 * g
                        if d < 0:
                            qs, mask = 0, None
                        else:
                            qs, mask = (d // 2) * 128, d % 2
                    ps, pb = self.bank()
                    self.mm(ps[0:nk, qs:N], [(KT[h % 2][:, k0:k0 + nk], QT[qi][:, qs:N]), (krT[:, k0:k0 + nk], QRt[qi][:, qs:N])],
                            pb, [KTb[h % 2], krb, QTb[qi], QRb[qi]])
                    P, Pb = Pt[step % NP], Ptb[step % NP]
                    self.act(P[0:nk, qs:N], ps[0:nk, qs:N], AF.Exp, [pb], [Pb], scale=SCALE)
                    if mask is not None:
                        self.tt("gpsimd", P[:, qs:qs + 128], P[:, qs:qs + 128], Mk[:, mask * 128:(mask + 1) * 128], ALU.mult, [wb], [Pb])
                    live[step] = (nk, qs)
                    if pend and kt % pend_every == 0 and g >= 0:
                        pend.pop(0)()
                if step >= DPT:
                    u = step - DPT
                    gi_, h, g, kt, nkt = units[u]
                    qi = gi_ % 2
                    nk, qs = live.pop(u)
                    P, Pb = Pt[u % NP], Ptb[u % NP]
                    OT, OTb = OTs[qi]
                    DN, DNb = DNs[qi]
                    first = kt == 0
                    last = kt == nkt - 1
                    self.mm(OT[:, qs:N], [(Vh[h % 2][0:nk, kt * 128:(kt + 1) * 128], P[0:nk, qs:N])], OTb, [Vb[h % 2], Pb], flags=(first, last))
                    self.mm(DN[:, qs:N], [(self.ones[0:nk, :], P[0:nk, qs:N])], DNb, [Pb, self.cbuf], flags=(first, last))
                    if last:
                        q0 = g * N
                        self.recip(rden[:, :], DN[:, 0:N], [DNb], [rdb])
                        self.tt("vector", Ob[qi][:, :], OT[:, 0:N], rden[:, :], ALU.mult, [OTb, rdb], [Obb[qi]])
                        self.store(ya3[:, h, q0:q0 + N], Ob[qi][:, :], Obb[qi])
                        while g == NG - 1 and pend:
                            pend.pop(0)()
            self._banks, self._bank_bufs, self._bank_i = allbanks, allbufs, 0
            self.end_phase()

    def phase3(self, xoT, w_in, w_br, w_o, yrnn_s, yatt_s, hmid_s):
        nc, ph, vec = self.nc, self.ph, self.vec
        N = BLK
        with ExitStack() as st:
            Wm = self.sb(st, "Wm", [128, 8 * 2048], BF16)
            Wb = self.sb(st, "Wb", [128, 18 * D], BF16)
            Wo = self.sb(st, "Wo", [128, 8 * D], BF16)
            wmb_, wbb_, wob_ = Buf("wm"), Buf("wbr"), Buf("wo")
            w3 = w_in.rearrange("(k p) n -> p k n", p=128)
            for k in range(8):
                self.wload(Wm[:, k * 2048:(k + 1) * 2048], w3[:, k, C_M:C_M + 2048], wmb_)
            b3 = w_br.rearrange("(k p) n -> p k n", p=128)
            for k in range(18):
                self.wload(Wb[:, k * D:(k + 1) * D], b3[:, k, :], wbb_)
            o3 = w_o.rearrange("(k p) n -> p k n", p=128)
            for k in range(8):
                self.wload(Wo[:, k * D:(k + 1) * D], o3[:, k, :], wob_)
            xin = [self.sb(st, "xin3_%d" % i, [128, 8 * N], F32) for i in range(2)]; xb = [Buf("xin3_%d" % i) for i in range(2)]
            sqb = self.sb(st, "sqb3", [128, 8 * N], BF16); sqbuf = Buf("sqb3")
            sd = self.sb(st, "sd_3", [128, N], F32); sdb = Buf("sd_3")
            rstd = self.sb(st, "rstd3", [128, N], F32); rsb = Buf("rstd3")
            zT = self.sb(st, "zT3", [128, 8 * N], BF16); zb = Buf("zT3")
            yr = self.sb(st, "yr", [128, NCH * N], BF16); yrb = Buf("yr")
            ya = self.sb(st, "ya", [128, 8 * N], BF16); yab = Buf("ya")
            G0 = [self.sb(st, "G0_%d" % i, [128, N], F32) for i in range(2)]; G0b = [Buf("G0_%d" % i) for i in range(2)]
            G1 = [self.sb(st, "G1_%d" % i, [128, N], F32) for i in range(2)]; G1b = [Buf("G1_%d" % i) for i in range(2)]
            mx = self.sb(st, "mx", [128, 8 * N], BF16); mxb = Buf("mx")
            xo3 = xoT.rearrange("(c p) t -> p c t", p=128)
            yr3 = yrnn_s.rearrange("(c p) t -> p c t", p=128)
            ya3 = yatt_s.rearrange("(c p) t -> p c t", p=128)
            hm3 = hmid_s.rearrange("(c p) t -> p c t", p=128)
            nb = OWN // N

            def ld(ob):
                xi = xin[ob % 2]
                for c in range(8):
                    self.load(xi[:, c * N:(c + 1) * N], xo3[:, c, ob * N:(ob + 1) * N], xb[ob % 2], first=(c == 0))
            ld(0)
            for ob in range(nb):
                xi, xbb = xin[ob % 2], xb[ob % 2]
                self.rms8(xi, xbb, N, V_GMIX, sqb, sqbuf, sd, sdb, rstd, rsb, zT, zb)
                for c in range(NCH):
                    self.load(yr[:, c * N:(c + 1) * N], yr3[:, c, ob * N:(ob + 1) * N], yrb, first=(c == 0))
                for c in range(8):
                    self.load(ya[:, c * N:(c + 1) * N], ya3[:, c, ob * N:(ob + 1) * N], yab, first=(c == 0))
                if ob + 1 < nb:
                    ld(ob + 1)
                for fc in range(8):
                    i2 = fc % 2
                    p0, p0b = self.bank()
                    self.mm(p0[:, 0:N], [(Wm[:, k * 2048 + fc * 128:k * 2048 + (fc + 1) * 128], zT[:, k * N:(k + 1) * N]) for k in range(8)], p0b, [zb, wmb_])
                    p1, p1b = self.bank()
                    self.mm(p1[:, 0:N], [(Wm[:, k * 2048 + D + fc * 128:k * 2048 + D + (fc + 1) * 128], zT[:, k * N:(k + 1) * N]) for k in range(8)], p1b, [zb, wmb_])
                    pr, prb = self.bank()
                    self.mm(pr[:, 0:N], [(Wb[:, k * D + fc * 128:k * D + (fc + 1) * 128], yr[:, k * N:(k + 1) * N]) for k in range(NCH)], prb, [yrb, wbb_])
                    pa, pab = self.bank()
                    self.mm(pa[:, 0:N], [(Wb[:, (NCH + k) * D + fc * 128:(NCH + k) * D + (fc + 1) * 128], ya[:, k * N:(k + 1) * N]) for k in range(8)], pab, [yab, wbb_])
                    self.act(G0[i2][:, :], p0[:, 0:N], AF.Sigmoid, [p0b, self.cbuf], [G0b[i2]], bias=vec[:, V_BGATE + fc:V_BGATE + fc + 1])
                    self.act(G1[i2][:, :], p1[:, 0:N], AF.Sigmoid, [p1b, self.cbuf], [G1b[i2]], bias=vec[:, V_BGATE + 8 + fc:V_BGATE + 9 + fc])
                    self.tt("vector", G0[i2][:, :], pr[:, 0:N], G0[i2][:, :], ALU.mult, [prb], [G0b[i2]])
                    self.tt("vector", G1[i2][:, :], pa[:, 0:N], G1[i2][:, :], ALU.mult, [pab], [G1b[i2]])
                    self.tt("gpsimd", mx[:, fc * N:(fc + 1) * N], G0[i2][:, :], G1[i2][:, :], ALU.add, [G0b[i2], G1b[i2]], [mxb])
                for fc in range(8):
                    ps, pb = self.bank()
                    self.mm(ps[:, 0:N], [(Wo[:, k * D + fc * 128:k * D + (fc + 1) * 128], mx[:, k * N:(k + 1) * N]) for k in range(8)], pb, [mxb, wob_])
                    self.tt("vector", xi[:, fc * N:(fc + 1) * N], ps[:, 0:N], xi[:, fc * N:(fc + 1) * N], ALU.add, [pb], [xbb])
                for c in range(8):
                    self.store(hm3[:, c, ob * N:(ob + 1) * N], xi[:, c * N:(c + 1) * N], xbb)
            self.end_phase()

    def phase4(self, w_f1, w_f2, hmid_s, outT):
        nc, ph, vec = self.nc, self.ph, self.vec
        N = BLK
        with ExitStack() as st:
            W1 = self.sb(st, "W1", [128, 8 * 2 * DFF], BF16)
            W2 = self.sb(st, "W2", [128, NFF * D], BF16)
            w1b_, w2b_ = Buf("w4a"), Buf("w4b")
            f13 = w_f1.rearrange("(k p) n -> p k n", p=128)
            for k in range(8):
                for (a, b) in ((0, 2048), (2048, 4096), (4096, 5632)):
                    self.wload(W1[:, k * 5632 + a:k * 5632 + b], f13[:, k, a:b], w1b_)
            f23 = w_f2.rearrange("(k p) n -> p k n", p=128)
            for k in range(NFF):
                self.wload(W2[:, k * D:(k + 1) * D], f23[:, k, :], w2b_)
            hm = [self.sb(st, "hm%d" % i, [128, 8 * N], F32) for i in range(2)]; hb = [Buf("hm%d" % i) for i in range(2)]
            sd = self.sb(st, "sd_4", [128, N], F32); sdb = Buf("sd_4")
            rstd = self.sb(st, "rstd4", [128, N], F32); rsb = Buf("rstd4")
            zT = self.sb(st, "zT4", [128, 8 * N], BF16); zb = Buf("zT4")
            sg = [self.sb(st, "sg%d" % i, [128, N], F32) for i in range(2)]; sgb = [Buf("sg%d" % i) for i in range(2)]
            actb = self.sb(st, "actb", [128, NFF * N], BF16); acb = Buf("actb")
            sqb, sqbuf = actb, acb
            hm3 = hmid_s.rearrange("(c p) t -> p c t", p=128)
            ov = outT.rearrange("(c p) t -> p c t", p=128)
            nb = OWN // N

            def ld(ob):
                for c in range(8):
                    self.load(hm[ob % 2][:, c * N:(c + 1) * N], hm3[:, c, ob * N:(ob + 1) * N], hb[ob % 2], first=(c == 0))
            ld(0)
            for ob in range(nb):
                h_, hbb = hm[ob % 2], hb[ob % 2]
                self.rms8(h_, hbb, N, V_GFFN, sqb, sqbuf, sd, sdb, rstd, rsb, zT, zb)
                if ob + 1 < nb:
                    ld(ob + 1)
                for j in range(NFF):
                    i2 = j % 2
                    pg, pgb = self.bank()
                    self.mm(pg[:, 0:N], [(W1[:, k * 5632 + j * 128:k * 5632 + (j + 1) * 128], zT[:, k * N:(k + 1) * N]) for k in range(8)], pgb, [zb, w1b_])
                    pu, pub = self.bank()
                    self.mm(pu[:, 0:N], [(W1[:, k * 5632 + DFF + j * 128:k * 5632 + DFF + (j + 1) * 128], zT[:, k * N:(k + 1) * N]) for k in range(8)], pub, [zb, w1b_])
                    self.act(sg[i2][:, :], pg[:, 0:N], AF.Silu, [pgb], [sgb[i2]])
                    self.tt("vector", actb[:, j * N:(j + 1) * N], pu[:, 0:N], sg[i2][:, :], ALU.mult, [pub, sgb[i2]], [acb])
                for fc in range(8):
                    ps, pb = self.bank()
                    self.mm(ps[:, 0:N], [(W2[:, k * D + fc * 128:k * D + (fc + 1) * 128], actb[:, k * N:(k + 1) * N]) for k in range(NFF)], pb, [acb, w2b_])
                    self.tt("vector", h_[:, fc * N:(fc + 1) * N], ps[:, 0:N], h_[:, fc * N:(fc + 1) * N], ALU.add, [pb], [hbb])
                self.act(sqb[:, 0:8 * N], h_[:, 0:8 * N], AF.Square, [hbb], [sqbuf])
                ps, pb = self.bank()
                self.mm(ps[:, 0:N], [(self.ones[:, :], sqb[:, c * N:(c + 1) * N]) for c in range(8)], pb, [sqbuf, self.cbuf])
                self.act(sd[:, 0:N], ps[:, 0:N], AF.Sqrt, [pb], [sdb], scale=1.0 / D, bias=self.epsc[:, 0:1])
                self.recip(rstd[:, 0:N], sd[:, 0:N], [sdb], [rsb])
                for c in range(8):
                    self.stt(h_[:, c * N:(c + 1) * N], h_[:, c * N:(c + 1) * N], vec[:, V_GFIN + c:V_GFIN + c + 1], rstd[:, 0:N],
                             ALU.mult, ALU.mult, [rsb, self.cbuf], [hbb])
                for c in range(8):
                    self.store(ov[:, c, ob * N:(ob + 1) * N], h_[:, c * N:(c + 1) * N], hbb)
            self.end_phase()


def _cos_sin_tables():
    inv = (10000.0 ** (-np.arange(0, ROPE, 2, dtype=np.float32) / np.float32(ROPE))).astype(np.float32)
    pos = np.arange(LT, dtype=np.float32)
    ang = (pos[:, None] * inv[None, :]).astype(np.float32)
    cos = np.cos(ang).astype(np.float32).T
    sin = np.sin(ang).astype(np.float32).T
    cos2 = np.concatenate([cos, cos], 0)
    sin2 = np.concatenate([-sin, sin], 0)
    return np.ascontiguousarray(cos2), np.ascontiguousarray(sin2)


def _chunkvec(v):
    v = np.asarray(v, np.float32).reshape(-1)
    return v.reshape(-1, 128).T


def make_in_maps(inputs):
    f = lambda k: np.asarray(inputs[k], np.float32)
    x = f("x")
    cos2, sin2 = _cos_sin_tables()
    cs_k = np.ascontiguousarray(np.concatenate([cos2, sin2], 1))
    dchunk = np.zeros((128, 128), np.float32)
    kk = np.arange(128)[:, None] // 64
    qq = np.arange(128)[None, :] // 64
    dmask = (kk <= qq).astype(np.float32)
    vec_common = np.zeros((128, NV), np.float32)
    vec_common[:, V_GMIX:V_GMIX + 8] = _chunkvec(f("norm_mix_g")[0])
    vec_common[:, V_GFFN:V_GFFN + 8] = _chunkvec(f("norm_ffn_g")[0])
    vec_common[:, V_GFIN:V_GFIN + 8] = _chunkvec(f("final_norm_g"))
    vec_common[:, V_BGATE:V_BGATE + 16] = _chunkvec(f("b_gate")[0].reshape(-1))
    cw = f("conv_w")[0]
    for k in range(4):
        vec_common[:, V_CONVW + 10 * k:V_CONVW + 10 * k + 10] = _chunkvec(cw[k])
    vec_common[:, V_CONVB:V_CONVB + 10] = _chunkvec(f("conv_b")[0])
    vec_common[:, V_BA:V_BA + 10] = _chunkvec(f("b_rec_a")[0])
    vec_common[:, V_BI:V_BI + 10] = _chunkvec(f("b_rec_i")[0])
    vec_common[:, V_LAM:V_LAM + 10] = _chunkvec(f("lru_lambda")[0])
    vec_common[:, V_GQ:V_GQ + 3] = _chunkvec(f("q_norm_g")[0])
    vec_common[:, V_GKV:V_GKV + 2] = _chunkvec(f("kv_norm_g")[0])
    shared = {
        "metaT": np.ascontiguousarray(f("meta_tokens").T),
        "w_in": np.ascontiguousarray(f("w_in")[0]),
        "w_rec_a": np.ascontiguousarray(f("w_rec_a")[0]),
        "w_rec_i": np.ascontiguousarray(f("w_rec_i")[0]),
        "w_uq": np.ascontiguousarray(f("w_uq")[0]),
        "w_ukv": np.ascontiguousarray(f("w_ukv")[0]),
        "w_branch": np.ascontiguousarray(f("w_branch")[0]),
        "w_out": np.ascontiguousarray(f("w_out")[0]),
        "w_ffn_in": np.ascontiguousarray(f("w_ffn_in")[0]),
        "w_ffn_out": np.ascontiguousarray(f("w_ffn_out")[0]),
        "cs_k": cs_k,
    }
    in_maps = []
    for core in range(8):
        b, c = core // 2, core % 2
        xb = x[b]
        own = xb.reshape(S // 128, 128, D)[c::2].reshape(OWN, D)
        vecs = vec_common.copy()
        vecs[:, V_SEL] = 1.0 if c == 0 else 0.0
        vecs[:, V_SEL + 1] = 0.0 if c == 0 else 1.0
        if c == 0:
            mk = np.concatenate([dmask, np.zeros_like(dmask)], 1)
        else:
            mk = np.concatenate([np.ones_like(dmask), dmask], 1)
        tiles = np.arange(c, S // 128, 2)
        posi = (NMETA + tiles[:, None] * 128 + np.arange(128)[None, :]).reshape(-1)
        cs_q = np.ascontiguousarray(np.concatenate([cos2[:, posi], sin2[:, posi]], 1))
        m = dict(shared)
        m["xT"] = np.ascontiguousarray(xb.T)
        m["xoT"] = np.ascontiguousarray(own.T)
        m["vecs"] = vecs
        m["masks"] = np.ascontiguousarray(mk)
        m["cs_q"] = cs_q
        in_maps.append(m)
    return in_maps


def assemble(results):
    out = np.empty((NB, S, D), np.float32)
    for core in range(8):
        b, c = core // 2, core % 2
        oT = np.asarray(results[core]["outT"], np.float32)
        o = oT.T.reshape(OWN // 128, 128, D)
        out[b].reshape(S // 128, 128, D)[c::2] = o
    return out


_NC_CACHE = {}


def kernel(**inputs):
    in_maps = make_in_maps(inputs)
    if "nc" not in _NC_CACHE:
        _NC_CACHE["nc"] = Builder().build()
    nc = _NC_CACHE["nc"]
    res = run_bass_kernel_spmd(nc, in_maps, core_ids=list(range(8)))
    return assemble(res.results)
```
